# Optimizing a Trainium2 kernel written in Bass

```python
import jax, jax.numpy as jnp
from jax import lax
import numpy as np

D_MODEL = 2048
BATCH = 2
SEQ = 16384
DEPTH = 2

F32 = jnp.float32
N_EVEN = (DEPTH + 1) // 2
N_ODD = DEPTH // 2
EPS = 1e-6
D_FF = 5632
FFN_RES = 0.5

HG_HEADS = 8
HG_DK = 128
HG_DV = 128
HG_WIDTH = HG_HEADS * HG_DK
HG_CHUNK = 64

SSM_INNER = 1024
SSM_HEAD_DIM = 64
SSM_HEADS = SSM_INNER // SSM_HEAD_DIM
SSM_GROUPS = 4
SSM_HPG = SSM_HEADS // SSM_GROUPS
SSM_STATE = 128
SSM_CONV = 4
SSM_CHUNK = 128
SSM_CONV_CH = SSM_INNER + 2 * SSM_GROUPS * SSM_STATE

AB_IN = 4 * HG_WIDTH + SSM_INNER + SSM_CONV_CH + SSM_HEADS
AB_OUT = HG_HEADS * HG_DV + SSM_INNER

ATT_PATTERNS = ((128, 1), (512, 4), (2048, 16))
ATT_GROUPS = 3
ATT_KV_HEADS = 16
ATT_HEAD_DIM = 128
ATT_BLOCK = 128
ATT_Q_WIDTH = ATT_GROUPS * ATT_KV_HEADS * ATT_HEAD_DIM
ATT_KV_WIDTH = ATT_KV_HEADS * ATT_HEAD_DIM
ATT_IN = ATT_Q_WIDTH + 2 * ATT_KV_WIDTH

kernel_name = "hybrid_hgrn2_ssd_dilated_attn_macaron"


def rmsnorm(x, w):
    xf = x.astype(F32)
    y = xf * lax.rsqrt(jnp.mean(xf * xf, axis=-1, keepdims=True) + EPS)
    return (y * w.astype(F32)).astype(x.dtype)


def swiglu(x, w_gate, w_up, w_down):
    return (jax.nn.silu(x @ w_gate) * (x @ w_up)) @ w_down


def hgrn2(q, f_raw, i, g, lb, norm_w):
    B_, S_ = q.shape[:2]
    nc = S_ // HG_CHUNK

    def heads(t):
        return t.reshape(B_, nc, HG_CHUNK, HG_HEADS, -1).transpose(0, 3, 1, 2, 4)

    lbf = lb.astype(F32)
    log_f = jnp.log(lbf + (1.0 - lbf) * jax.nn.sigmoid(f_raw.astype(F32)))
    k = 1.0 - jnp.exp(log_f)
    qh, kh, vh, lfh = heads(q.astype(F32)), heads(k), heads(i.astype(F32)), heads(log_f)
    b = jnp.cumsum(lfh, axis=3)
    b_mid = b[:, :, :, HG_CHUNK // 2:HG_CHUNK // 2 + 1]
    b_last = b[:, :, :, -1:]
    scores = jnp.einsum('bhcld,bhcsd->bhcls', qh * jnp.exp(b - b_mid), kh * jnp.exp(b_mid - b))
    causal = jnp.tril(jnp.ones((HG_CHUNK, HG_CHUNK), bool))
    o_intra = jnp.einsum('bhcls,bhcsv->bhclv', jnp.where(causal, scores, 0.0), vh)
    q_inter = qh * jnp.exp(b)
    k_state = kh * jnp.exp(b_last - b)
    chunk_decay = jnp.exp(b_last[:, :, :, 0, :])

    def step(S, xs):
        qc, kc, vc, dc = xs
        o = jnp.einsum('bhld,bhdv->bhlv', qc, S)
        S = dc[..., None] * S + jnp.einsum('bhsd,bhsv->bhdv', kc, vc)
        return S, o

    S0 = jnp.zeros((B_, HG_HEADS, HG_DK, HG_DV), F32)
    mv = lambda t: jnp.moveaxis(t, 2, 0)
    _, o_inter = lax.scan(step, S0, (mv(q_inter), mv(k_state), mv(vh), mv(chunk_decay)))
    o = o_intra + jnp.moveaxis(o_inter, 0, 2)
    o = o.transpose(0, 2, 3, 1, 4).reshape(B_, S_, HG_HEADS, HG_DV)
    o = o * lax.rsqrt(jnp.mean(o * o, axis=-1, keepdims=True) + EPS)
    o = o.reshape(B_, S_, HG_WIDTH) * norm_w.astype(F32) * jax.nn.silu(g.astype(F32))
    return o.astype(q.dtype)


def causal_depthwise_conv(x, w, bias):
    out = lax.conv_general_dilated(x, w[:, None, :], window_strides=(1,),
                                   padding=[(SSM_CONV - 1, 0)],
                                   dimension_numbers=('NWC', 'WIO', 'NWC'),
                                   feature_group_count=x.shape[-1])
    return out + bias


def mamba2_ssd(z, xbc, dt_raw, conv_w, conv_b, dt_bias, A_log, D, norm_w):
    B_, S_ = z.shape[:2]
    nc = S_ // SSM_CHUNK
    C_, G, R, P, N = SSM_CHUNK, SSM_GROUPS, SSM_HPG, SSM_HEAD_DIM, SSM_STATE
    xbc = jax.nn.silu(causal_depthwise_conv(xbc, conv_w, conv_b))
    xs, Bm, Cm = jnp.split(xbc, [SSM_INNER, SSM_INNER + G * N], axis=-1)
    x = xs.astype(F32).reshape(B_, nc, C_, G, R, P)
    Bm = Bm.astype(F32).reshape(B_, nc, C_, G, N)
    Cm = Cm.astype(F32).reshape(B_, nc, C_, G, N)
    dt = jax.nn.softplus(dt_raw.astype(F32) + dt_bias.astype(F32)).reshape(B_, nc, C_, G, R)
    A = -jnp.exp(A_log.astype(F32)).reshape(G, R)
    a_cs = jnp.cumsum(dt * A, axis=2)
    xdt = x * dt[..., None]
    seg = a_cs[:, :, :, None] - a_cs[:, :, None, :]
    causal = jnp.tril(jnp.ones((C_, C_), bool))[:, :, None, None]
    L = jnp.exp(jnp.where(causal, seg, -jnp.inf))
    CB = jnp.einsum('bclgn,bcsgn->bclsg', Cm, Bm)
    y_diag = jnp.einsum('bclsg,bclsgr,bcsgrp->bclgrp', CB, L, xdt)
    decay_states = jnp.exp(a_cs[:, :, -1:] - a_cs)
    states = jnp.einsum('bcsgn,bcsgr,bcsgrp->bcgrpn', Bm, decay_states, xdt)
    chunk_decay = jnp.exp(a_cs[:, :, -1])

    def step(h, xs_):
        st, dc = xs_
        return h * dc[..., None, None] + st, h

    h0 = jnp.zeros((B_, G, R, P, N), F32)
    _, h_prev = lax.scan(step, h0, (jnp.moveaxis(states, 1, 0), jnp.moveaxis(chunk_decay, 1, 0)))
    h_prev = jnp.moveaxis(h_prev, 0, 1)
    y_off = jnp.einsum('bclgn,bcgrpn,bclgr->bclgrp', Cm, h_prev, jnp.exp(a_cs))
    y = y_diag + y_off + x * D.astype(F32).reshape(G, R)[..., None]
    y = y.reshape(B_, S_, SSM_INNER) * jax.nn.silu(z.astype(F32))
    yg = y.reshape(B_, S_, G, -1)
    yg = yg * lax.rsqrt(jnp.mean(yg * yg, axis=-1, keepdims=True) + EPS)
    return (yg.reshape(B_, S_, SSM_INNER) * norm_w.astype(F32)).astype(z.dtype)


def mixer_ab(h, w_in, w_out, lb, hg_norm_w, conv_w, conv_b, dt_bias, A_log, D, ssm_norm_w):
    proj = h @ w_in
    idx = [int(v) for v in np.cumsum([HG_WIDTH] * 4 + [SSM_INNER, SSM_CONV_CH])]
    hq, hf, hi, hg, z, xbc, dt = jnp.split(proj, idx, axis=-1)
    o_a = hgrn2(hq, hf, hi, hg, lb, hg_norm_w)
    o_b = mamba2_ssd(z, xbc, dt, conv_w, conv_b, dt_bias, A_log, D, ssm_norm_w)
    return jnp.concatenate([o_a, o_b], axis=-1) @ w_out


def dilated_window_attention(q, k, v, dilation, span):
    B_, S_, H_, dh = q.shape
    M = S_ // dilation
    nb = -(-M // ATT_BLOCK)
    Mp = nb * ATT_BLOCK

    def strided(t):
        t = t.reshape(B_, M, dilation, H_, dh).transpose(0, 2, 3, 1, 4)
        t = jnp.pad(t, ((0, 0), (0, 0), (0, 0), (0, Mp - M), (0, 0)))
        return t.reshape(B_, dilation, H_, nb, ATT_BLOCK, dh)

    def with_prev(t):
        prev = jnp.pad(t, ((0, 0), (0, 0), (0, 0), (1, 0), (0, 0), (0, 0)))[:, :, :, :-1]
        return jnp.concatenate([prev, t], axis=4)

    qb = strided(q)
    kc, vc = with_prev(strided(k)), with_prev(strided(v))
    s = jnp.einsum('bdhnqe,bdhnke->bdhnqk', qb, kc).astype(F32) * (ATT_HEAD_DIM ** -0.5)
    qi = jnp.arange(nb)[:, None, None] * ATT_BLOCK + jnp.arange(ATT_BLOCK)[None, :, None]
    ki = (jnp.arange(nb)[:, None, None] - 1) * ATT_BLOCK + jnp.arange(2 * ATT_BLOCK)[None, None, :]
    dist = qi - ki
    allowed = (dist >= 0) & (dist <= span) & (ki >= 0)
    s = jnp.where(allowed, s, -jnp.inf)
    m = jnp.max(s, axis=-1, keepdims=True)
    p = jnp.exp(s - m)
    l = jnp.sum(p, axis=-1, keepdims=True)
    o = jnp.einsum('bdhnqk,bdhnke->bdhnqe', p, vc.astype(F32)) / l
    lse = (m + jnp.log(l))[..., 0]
    o = o.reshape(B_, dilation, H_, Mp, dh)[:, :, :, :M].transpose(0, 3, 1, 2, 4).reshape(B_, S_, H_, dh)
    lse = lse.reshape(B_, dilation, H_, Mp)[..., :M].transpose(0, 3, 1, 2).reshape(B_, S_, H_)
    return o, lse


def mixer_c(h, w_in, w_out):
    B_, S_, _ = h.shape
    proj = h @ w_in
    q, k, v = jnp.split(proj, [ATT_Q_WIDTH, ATT_Q_WIDTH + ATT_KV_WIDTH], axis=-1)
    q = q.reshape(B_, S_, ATT_GROUPS, ATT_KV_HEADS, ATT_HEAD_DIM)
    k = k.reshape(B_, S_, ATT_KV_HEADS, ATT_HEAD_DIM)
    v = v.reshape(B_, S_, ATT_KV_HEADS, ATT_HEAD_DIM)
    outs, lses = [], []
    for grp, (window, dilation) in enumerate(ATT_PATTERNS):
        o_g, lse_g = dilated_window_attention(q[:, :, grp], k, v, dilation, window // dilation)
        outs.append(o_g)
        lses.append(lse_g)
    wts = jax.nn.softmax(jnp.stack(lses, axis=0), axis=0)
    o = jnp.sum(wts[..., None] * jnp.stack(outs, axis=0), axis=0)
    return o.reshape(B_, S_, ATT_KV_WIDTH).astype(h.dtype) @ w_out


def setup_inputs(seed: int = 0) -> dict:
    key = jax.random.key(seed)
    ks = jax.random.split(key, 20)

    def nrm(k, shape, fan_in):
        return jax.random.normal(k, shape, F32) * (fan_in ** -0.5)

    def gain(k, shape):
        return 1.0 + 0.02 * jax.random.normal(k, shape, F32)

    dt0 = jnp.exp(jax.random.uniform(ks[12], (N_EVEN, SSM_HEADS), F32, np.log(1e-3), np.log(1e-1)))
    return {
        "x": jax.random.normal(ks[0], (BATCH, SEQ, D_MODEL), F32),
        "norm_pre": gain(ks[1], (DEPTH, 3, D_MODEL)),
        "norm_post": gain(ks[2], (DEPTH, 3, D_MODEL)),
        "ffn_w_gate": nrm(ks[3], (DEPTH, 2, D_MODEL, D_FF), D_MODEL),
        "ffn_w_up": nrm(ks[4], (DEPTH, 2, D_MODEL, D_FF), D_MODEL),
        "ffn_w_down": nrm(ks[5], (DEPTH, 2, D_FF, D_MODEL), D_FF),
        "ab_w_in": nrm(ks[6], (N_EVEN, D_MODEL, AB_IN), D_MODEL),
        "ab_w_out": nrm(ks[7], (N_EVEN, AB_OUT, D_MODEL), AB_OUT),
        "hgrn_lb": 0.1 * jax.random.normal(ks[8], (N_EVEN + 1, HG_WIDTH), F32),
        "hgrn_norm_w": gain(ks[9], (N_EVEN, HG_WIDTH)),
        "ssm_conv_w": nrm(ks[10], (N_EVEN, SSM_CONV, SSM_CONV_CH), SSM_CONV),
        "ssm_conv_b": 0.02 * jax.random.normal(ks[11], (N_EVEN, SSM_CONV_CH), F32),
        "ssm_dt_bias": dt0 + jnp.log(-jnp.expm1(-dt0)),
        "ssm_A_log": jnp.log(jax.random.uniform(ks[13], (N_EVEN, SSM_HEADS), F32, 1.0, 16.0)),
        "ssm_D": 1.0 + 0.1 * jax.random.normal(ks[14], (N_EVEN, SSM_HEADS), F32),
        "ssm_norm_w": gain(ks[15], (N_EVEN, SSM_INNER)),
        "att_w_in": nrm(ks[16], (N_ODD, D_MODEL, ATT_IN), D_MODEL),
        "att_w_out": nrm(ks[17], (N_ODD, ATT_KV_WIDTH, D_MODEL), ATT_KV_WIDTH),
    }


def reference(x, norm_pre, norm_post, ffn_w_gate, ffn_w_up, ffn_w_down, ab_w_in, ab_w_out,
              hgrn_lb, hgrn_norm_w, ssm_conv_w, ssm_conv_b, ssm_dt_bias, ssm_A_log, ssm_D,
              ssm_norm_w, att_w_in, att_w_out):
    lower_bounds = jnp.cumsum(jax.nn.softmax(hgrn_lb.astype(F32), axis=0), axis=0)
    h = x
    for layer in range(DEPTH):
        y = swiglu(rmsnorm(h, norm_pre[layer, 0]), ffn_w_gate[layer, 0], ffn_w_up[layer, 0], ffn_w_down[layer, 0])
        h = h + FFN_RES * rmsnorm(y, norm_post[layer, 0])
        u = rmsnorm(h, norm_pre[layer, 1])
        if layer % 2 == 0:
            e = layer // 2
            y = mixer_ab(u, ab_w_in[e], ab_w_out[e], lower_bounds[e], hgrn_norm_w[e],
                         ssm_conv_w[e], ssm_conv_b[e], ssm_dt_bias[e], ssm_A_log[e], ssm_D[e], ssm_norm_w[e])
        else:
            o_idx = layer // 2
            y = mixer_c(u, att_w_in[o_idx], att_w_out[o_idx])
        h = h + rmsnorm(y, norm_post[layer, 1])
        y = swiglu(rmsnorm(h, norm_pre[layer, 2]), ffn_w_gate[layer, 1], ffn_w_up[layer, 1], ffn_w_down[layer, 1])
        h = h + FFN_RES * rmsnorm(y, norm_post[layer, 2])
    return h
```

```python
import numpy as np
import ml_dtypes
import concourse.bass as bass
import concourse.mybir as mybir
from concourse.bass_utils import run_bass_kernel_spmd

F32 = mybir.dt.float32
BF16 = mybir.dt.bfloat16
AF = mybir.ActivationFunctionType
ALU = mybir.AluOpType
NPBF = ml_dtypes.bfloat16

SAME_SYNC = True


class Tk:
    __slots__ = ("name", "h", "w", "rd", "dsem", "dcnt", "nowaw")

    def __init__(self, name, h):
        self.name = name
        self.h = h
        self.w = None
        self.rd = {}
        self.dsem = None
        self.dcnt = 0
        self.nowaw = False

    def __getitem__(self, k):
        return self.h[k]


class KB:
    def __init__(self, nc, emit, need=None):
        self.nc = nc
        self.emit = emit
        if nc is not None:
            self.E = {"pe": nc.tensor, "act": nc.scalar, "dve": nc.vector, "pool": nc.gpsimd, "sp": nc.sync}
        else:
            self.E = {"pe": None, "act": None, "dve": None, "pool": None, "sp": None}
        self.idx = {e: 0 for e in self.E}
        self.need = need if need is not None else {e: set() for e in self.E}
        self.rank = None
        if emit:
            self.rank = {}
            for e in self.E:
                srt = sorted(self.need[e])
                self.rank[e] = {ix: i + 1 for i, ix in enumerate(srt)}
        self.waited = {e: {} for e in self.E}
        self.sems = {}
        self.tiles = {}
        self.ntile = 0
        self.nsem = 0
        self.ninst = 0

    def _sem(self, name):
        if name not in self.sems:
            self.nsem += 1
            self.sems[name] = self.nc.alloc_semaphore(name) if self.emit else name
        return self.sems[name]

    def sb(self, name, shape, dt):
        h = self.nc.alloc_sbuf_tensor(name, list(shape), dt) if self.emit else None
        t = Tk(name, h)
        self.tiles[name] = t
        return t

    def ps(self, name, shape, dt=F32):
        h = self.nc.alloc_psum_tensor(name, list(shape), dt) if self.emit else None
        t = Tk(name, h)
        self.tiles[name] = t
        return t

    def dram(self, name, shape, dt, kind="Internal"):
        h = self.nc.dram_tensor(name, list(shape), dt, kind=kind) if self.emit else None
        t = Tk(name, h)
        self.tiles[name] = t
        return t

    def _key(self, dep):
        return (dep[0], dep[1] if dep[0] == 'c' else dep[1].name)

    def _wait(self, e, dep):
        if dep is None:
            return
        key = self._key(dep)
        val = dep[2]
        if dep[0] == 'c' and dep[1] == e:
            if e in ("pe", "sp") or not SAME_SYNC:
                return
        if self.waited[e].get(key, 0) >= val:
            return
        self.waited[e][key] = val
        if dep[0] == 'c':
            if not self.emit:
                self.need[dep[1]].add(val)
            else:
                self.E[e].wait_ge(self._sem("c_" + dep[1]), self.rank[dep[1]][val])
        else:
            if self.emit:
                self.E[e].wait_ge(dep[1].dsem, 16 * val)

    def _deps(self, e, reads, writes):
        for t in reads:
            self._wait(e, t.w)
        for t in writes:
            if t.nowaw:
                continue
            self._wait(e, t.w)
            for d in t.rd.values():
                self._wait(e, d)

    def op(self, e, fn, reads=(), writes=()):
        self._deps(e, reads, writes)
        self.idx[e] += 1
        ix = self.idx[e]
        self.ninst += 1
        if self.emit:
            ins = fn(self.E[e])
            if ix in self.rank[e]:
                ins.then_inc(self._sem("c_" + e), 1)
        dep = ('c', e, ix)
        for t in reads:
            t.rd[('c', e)] = dep
        for t in writes:
            t.w = dep
            t.rd = {}

    def dma(self, q, dst, dst_ap, src, src_ap, **kw):
        self._deps(q, [src], [dst])
        self.idx[q] += 1
        self.ninst += 1
        if dst.dsem is None:
            dst.dsem = self._sem("d_" + dst.name)
        dst.dcnt += 1
        if self.emit:
            self.E[q].dma_start(out=dst_ap(), in_=src_ap(), **kw).then_inc(dst.dsem, 16)
        dep = ('d', dst, dst.dcnt)
        src.rd[('d', dst.name)] = dep
        dst.w = dep
        dst.rd = {}

    def finish(self, outs, e="sp"):
        for t in outs:
            self._wait(e, t.w)


def build(fn, *args, **kw):
    kb1 = KB(None, False)
    fn(kb1, *args, **kw)
    nc = bass.Bass("TRN2", target_bir_lowering=False)
    kb2 = KB(nc, True, need=kb1.need)
    fn(kb2, *args, **kw)
    return nc, kb2


EPS = 1e-6


class Ring:
    def __init__(self, kb, name, shape, dt, R, src, src_ap_of, nblocks, q="sp", tiles=None):
        self.kb = kb
        self.tiles = tiles if tiles is not None else [kb.sb(f"{name}{i}", shape, dt) for i in range(R)]
        self.R = R
        self.src = src
        self.src_ap_of = src_ap_of
        self.n = nblocks
        self.issued = 0
        self.q = q

    def prefetch(self, upto):
        while self.issued < min(upto + 1, self.n):
            i = self.issued
            t = self.tiles[i % self.R]
            self.kb.dma(self.q, t, (lambda t=t: t.h.ap()), self.src, (lambda i=i: self.src_ap_of(i)))
            self.issued += 1

    def get(self, i):
        self.prefetch(i + self.R - 1)
        return self.tiles[i % self.R]


def tok_builder(kb, NTOK, nffn=1, prologue=False, T=512, DM=2048, DF=5632):
    KC = DM // 128
    FC = DF // 128
    OC = DM // 128
    NT = NTOK // T
    hT = kb.dram("hT", [DM, NTOK], F32, kind="ExternalInput")
    out = kb.dram("out", [DM, NTOK], F32, kind="ExternalOutput")
    out.nowaw = True
    W = []
    for j in range(nffn):
        W.append(dict(
            wg=kb.dram(f"wg{j}", [FC, 128, KC, 128], BF16, kind="ExternalInput"),
            wu=kb.dram(f"wu{j}", [FC, 128, KC, 128], BF16, kind="ExternalInput"),
            wd=kb.dram(f"wd{j}", [OC, 128, FC, 128], BF16, kind="ExternalInput"),
            npre=kb.dram(f"npre{j}", [128, KC], F32, kind="ExternalInput"),
            npost=kb.dram(f"npost{j}", [128, OC], F32, kind="ExternalInput")))
    if prologue:
        catT = kb.dram("catT", [DM, NTOK], BF16, kind="ExternalInput")
        wo = kb.dram("wo", [OC, 128, KC, 128], BF16, kind="ExternalInput")
        npo = kb.dram("npo", [128, OC], F32, kind="ExternalInput")
    nstage = nffn + (1 if prologue else 0)
    mids = []
    for i in range(nstage - 1):
        m = kb.dram(f"hmid{i}", [DM, NTOK], F32)
        m.nowaw = True
        mids.append(m)

    ones = kb.sb("ones", [128, 128], BF16)
    kb.op("pool", lambda e: e.memset(ones.h.ap(), 1.0), writes=[ones])
    wpre = kb.sb("wpre", [128, KC], F32)
    wpost = kb.sb("wpost", [128, OC], F32)
    hin = kb.sb("hin", [128, KC, T], F32)
    xT = [kb.sb(f"xT{k}", [128, T], BF16) for k in range(KC)]
    sq = [kb.sb(f"sq{i}", [128, T], BF16) for i in range(2)]
    rstd = kb.sb("rstd", [128, T], F32)
    rstd2 = kb.sb("rstd2", [128, T], F32)
    act = [kb.sb(f"act{f}", [128, T], BF16) for f in range(FC)]
    sg = [kb.sb(f"sg{i}", [128, T], F32) for i in range(2)]
    yb = [kb.sb(f"y{o}", [128, T], F32) for o in range(OC)]
    res = [kb.sb(f"res{i}", [128, T], F32) for i in range(2)]
    pgu = [(kb.ps(f"pg{i}", [128, T]), kb.ps(f"pu{i}", [128, T])) for i in range(2)]
    pd = [kb.ps(f"pd{i}", [128, T]) for i in range(2)]
    pst = [kb.ps(f"pst{i}", [128, T]) for i in range(2)]
    rgt = [kb.sb(f"rg{i}", [128, KC, 128], BF16) for i in range(3)]
    rut = [kb.sb(f"ru{i}", [128, KC, 128], BF16) for i in range(3)]
    rdt = [kb.sb(f"rd{i}", [128, FC, 128], BF16) for i in range(2)]

    def sumsq(src_of, n, pbank, rs):
        for k in range(n):
            st, sap = src_of(k)
            q = sq[k % 2]
            kb.op("act", lambda e, q=q, sap=sap: e.activation(q.h.ap(), sap(), AF.Square), reads=[st], writes=[q])
            kb.op("pe", lambda e, q=q, k=k: e.matmul(pbank.h.ap(), ones.h.ap(), q.h.ap(), start=(k == 0), stop=(k == n - 1)),
                  reads=[ones, q], writes=[pbank])
        kb.op("act", lambda e: e.activation(rs.h.ap(), pbank.h.ap(), AF.Sqrt, bias=float(DM * EPS)), reads=[pbank], writes=[rs])
        kb.op("dve", lambda e: e.reciprocal(rs.h.ap(), rs.h.ap()), reads=[rs], writes=[rs])

    def down_post(t, src, dst, ring, nk, srcs, resw):
        for o in range(OC):
            w = ring.get(t * OC + o)
            p = pd[o % 2]
            for f in range(nk):
                kb.op("pe", lambda e, f=f, w=w, p=p: e.matmul(p.h.ap(), w.h[:, f, :], srcs[f].h.ap(), start=(f == 0), stop=(f == nk - 1)),
                      reads=[w, srcs[f]], writes=[p])
            kb.op("act", lambda e, o=o, p=p: e.activation(yb[o].h.ap(), p.h.ap(), AF.Copy), reads=[p], writes=[yb[o]])
        sumsq(lambda k: (yb[k], lambda k=k: yb[k].h.ap()), OC, pst[1], rstd2)
        for o in range(OC):
            r = res[o % 2]
            kb.dma("sp", r, lambda r=r: r.h.ap(), src, lambda o=o: src.h[o * 128:(o + 1) * 128, t * T:(t + 1) * T])
            kb.op("dve", lambda e, o=o: e.scalar_tensor_tensor(yb[o].h.ap(), yb[o].h.ap(), wpost.h[:, o:o + 1], rstd2.h.ap(),
                                                                ALU.mult, ALU.mult),
                  reads=[yb[o], wpost, rstd2], writes=[yb[o]])
            kb.op("pool", lambda e, o=o, r=r: e.tensor_tensor(r.h.ap(), yb[o].h.ap(), r.h.ap(), ALU.add),
                  reads=[yb[o], r], writes=[r])
            kb.dma("sp", dst, lambda o=o: dst.h[o * 128:(o + 1) * 128, t * T:(t + 1) * T], r, lambda r=r: r.h.ap())

    stage = 0
    cur = hT
    s = float(np.sqrt(DM))
    if prologue:
        dst = mids[0] if nstage > 1 else out
        kb.dma("sp", wpost, lambda: wpost.h.ap(), npo, lambda: npo.h.ap())
        kb.op("dve", lambda e: e.tensor_scalar_mul(wpost.h.ap(), wpost.h.ap(), s), reads=[wpost], writes=[wpost])
        ro = Ring(kb, "ro", None, None, 3, wo, lambda i: wo.h[i % OC], NT * OC, q="sp", tiles=rgt)
        for t in range(NT):
            for k in range(KC):
                kb.dma("sp", xT[k], lambda k=k: xT[k].h.ap(), catT, lambda k=k: catT.h[k * 128:(k + 1) * 128, t * T:(t + 1) * T])
            down_post(t, cur, dst, ro, KC, xT, None)
        cur = dst
        stage = 1

    for j in range(nffn):
        Wj = W[j]
        dst = out if stage == nstage - 1 else mids[stage]
        kb.dma("sp", wpre, lambda: wpre.h.ap(), Wj["npre"], lambda: Wj["npre"].h.ap())
        kb.dma("sp", wpost, lambda: wpost.h.ap(), Wj["npost"], lambda: Wj["npost"].h.ap())
        kb.op("dve", lambda e: e.tensor_scalar_mul(wpre.h.ap(), wpre.h.ap(), s), reads=[wpre], writes=[wpre])
        kb.op("dve", lambda e: e.tensor_scalar_mul(wpost.h.ap(), wpost.h.ap(), 0.5 * s), reads=[wpost], writes=[wpost])
        rg = Ring(kb, "rg", None, None, 3, Wj["wg"], lambda i, Wj=Wj: Wj["wg"].h[i % FC], NT * FC, q="sp", tiles=rgt)
        ru = Ring(kb, "ru", None, None, 3, Wj["wu"], lambda i, Wj=Wj: Wj["wu"].h[i % FC], NT * FC, q="sp", tiles=rut)
        rd = Ring(kb, "rd", None, None, 2, Wj["wd"], lambda i, Wj=Wj: Wj["wd"].h[i % OC], NT * OC, q="sp", tiles=rdt)

        def prenorm(t, cur=cur):
            kb.dma("sp", hin, lambda: hin.h.ap(), cur,
                   lambda: cur.h[:, t * T:(t + 1) * T].rearrange("(kc p) n -> p kc n", p=128))
            sumsq(lambda k: (hin, lambda k=k: hin.h[:, k, :]), KC, pst[0], rstd)
            for k in range(KC):
                kb.op("dve", lambda e, k=k: e.scalar_tensor_tensor(xT[k].h.ap(), hin.h[:, k, :], wpre.h[:, k:k + 1], rstd.h.ap(),
                                                                    ALU.mult, ALU.mult),
                      reads=[hin, wpre, rstd], writes=[xT[k]])

        def gateup(t):
            for f in range(FC):
                i = t * FC + f
                g = rg.get(i)
                u = ru.get(i)
                pg, pu = pgu[f % 2]
                for k in range(KC):
                    kb.op("pe", lambda e, k=k, g=g, pg=pg: e.matmul(pg.h.ap(), g.h[:, k, :], xT[k].h.ap(), start=(k == 0), stop=(k == KC - 1)),
                          reads=[g, xT[k]], writes=[pg])
                for k in range(KC):
                    kb.op("pe", lambda e, k=k, u=u, pu=pu: e.matmul(pu.h.ap(), u.h[:, k, :], xT[k].h.ap(), start=(k == 0), stop=(k == KC - 1)),
                          reads=[u, xT[k]], writes=[pu])
                s_ = sg[f % 2]
                kb.op("act", lambda e, s_=s_, pg=pg: e.activation(s_.h.ap(), pg.h.ap(), AF.Silu), reads=[pg], writes=[s_])
                kb.op("dve", lambda e, s_=s_, pu=pu, f=f: e.tensor_tensor(act[f].h.ap(), s_.h.ap(), pu.h.ap(), ALU.mult),
                      reads=[s_, pu], writes=[act[f]])

        prenorm(0)
        for t in range(NT):
            gateup(t)
            if t + 1 < NT:
                prenorm(t + 1)
            down_post(t, cur, dst, rd, FC, act, None)
        cur = dst
        stage += 1
    kb.finish([out])


def attn_builder(kb, S, NH=4, T=512, DM=2048, SB=2048):
    KC = DM // 128
    NSB = S // SB
    hT = kb.dram("hT", [DM, S], F32, kind="ExternalInput")
    npre = kb.dram("npre", [128, KC], F32, kind="ExternalInput")
    w = kb.dram("w", [NH, 128, KC, 640], BF16, kind="ExternalInput")
    cst = kb.dram("cst", [128, 4, 128], BF16, kind="ExternalInput")
    catT = kb.dram("catT", [NH * 128, S], BF16, kind="ExternalOutput")
    catT.nowaw = True

    cs = kb.sb("cs", [128, 4, 128], BF16)
    kb.dma("sp", cs, lambda: cs.h.ap(), cst, lambda: cst.h.ap())
    wpre = kb.sb("wpre", [128, KC], F32)
    kb.dma("sp", wpre, lambda: wpre.h.ap(), npre, lambda: npre.h.ap())
    kb.op("dve", lambda e: e.tensor_scalar_mul(wpre.h.ap(), wpre.h.ap(), float(np.sqrt(DM))), reads=[wpre], writes=[wpre])
    hin = kb.sb("hin", [128, KC, T], F32)
    xT = [kb.sb(f"xT{k}", [128, T], BF16) for k in range(KC)]
    sq = [kb.sb(f"sq{i}", [128, T], BF16) for i in range(2)]
    rstd = kb.sb("rstd", [128, T], F32)
    wsb = kb.sb("wsb", [128, KC, 640], BF16)
    Q = [kb.sb(f"Q{g}", [128, SB], BF16) for g in range(3)]
    Kb = kb.sb("Kb", [128, 2 * SB], BF16)
    VT = kb.sb("VT", [128, SB], BF16)
    Vs = [kb.sb(f"Vs{g}", [128, 32, 128], BF16) for g in range(3)]
    acc = kb.sb("acc", [128, 2, SB], F32)
    pt = [kb.sb(f"pt{i}", [128, 2, 128], BF16) for i in range(2)]
    ob = kb.sb("ob", [128, SB], BF16)
    pstat = kb.ps("pstat", [128, T])
    pproj = [kb.ps(f"pproj{i}", [128, T]) for i in range(2)]
    psc = [kb.ps(f"psc{i}", [128, 2, 128]) for i in range(2)]
    pol = [kb.ps(f"pol{i}", [128, 2, 128]) for i in range(2)]
    ptr = kb.ps("ptr", [128, 4, 128], BF16)
    DIL = (1, 4, 16)
    SCALE = float(128 ** -0.5)

    def prenorm(t0):
        kb.dma("sp", hin, lambda: hin.h.ap(), hT, lambda: hT.h[:, t0:t0 + T].rearrange("(kc p) n -> p kc n", p=128))
        for k in range(KC):
            q = sq[k % 2]
            kb.op("act", lambda e, q=q, k=k: e.activation(q.h.ap(), hin.h[:, k, :], AF.Square), reads=[hin], writes=[q])
            kb.op("pe", lambda e, q=q, k=k: e.matmul(pstat.h.ap(), cs.h[:, 3, :], q.h.ap(), start=(k == 0), stop=(k == KC - 1)),
                  reads=[cs, q], writes=[pstat])
        kb.op("act", lambda e: e.activation(rstd.h.ap(), pstat.h.ap(), AF.Sqrt, bias=float(DM * EPS)), reads=[pstat], writes=[rstd])
        kb.op("dve", lambda e: e.reciprocal(rstd.h.ap(), rstd.h.ap()), reads=[rstd], writes=[rstd])
        for k in range(KC):
            kb.op("dve", lambda e, k=k: e.scalar_tensor_tensor(xT[k].h.ap(), hin.h[:, k, :], wpre.h[:, k:k + 1], rstd.h.ap(), ALU.mult, ALU.mult),
                  reads=[hin, wpre, rstd], writes=[xT[k]])

    for hh in range(NH):
        kb.dma("sp", wsb, lambda hh=hh: wsb.h.ap(), w, lambda hh=hh: w.h[hh])
        for sbi in range(NSB):
            for tb in range(SB // T):
                t0 = sbi * SB + tb * T
                prenorm(t0)
                for cb in range(5):
                    pp = pproj[cb % 2]
                    for k in range(KC):
                        kb.op("pe", lambda e, k=k, cb=cb, pp=pp: e.matmul(pp.h.ap(), wsb.h[:, k, cb * 128:(cb + 1) * 128], xT[k].h.ap(),
                                                                          start=(k == 0), stop=(k == KC - 1)), reads=[wsb, xT[k]], writes=[pp])
                    if cb < 3:
                        dstt = Q[cb]
                        kb.op("act", lambda e, pp=pp, dstt=dstt, tb=tb: e.activation(dstt.h[:, tb * T:(tb + 1) * T], pp.h.ap(), AF.Copy, scale=SCALE),
                              reads=[pp], writes=[dstt])
                    elif cb == 3:
                        kb.op("act", lambda e, pp=pp, tb=tb: e.activation(Kb.h[:, SB + tb * T:SB + (tb + 1) * T], pp.h.ap(), AF.Copy),
                              reads=[pp], writes=[Kb])
                    else:
                        kb.op("dve", lambda e, pp=pp, tb=tb: e.tensor_copy(VT.h[:, tb * T:(tb + 1) * T], pp.h.ap()), reads=[pp], writes=[VT])
            for g, d in enumerate(DIL):
                for s4 in range(4):
                    for i in range(4):
                        st_ = s4 * 4 + i
                        blk, r = st_ // d, st_ % d
                        a = blk * 128 * d + r
                        kb.op("pe", lambda e, i=i, a=a, d=d: e.transpose(ptr.h[:, i, :], VT.h[:, a:a + 127 * d + 1:d], cs.h[:, 0, :]),
                              reads=[VT, cs], writes=[ptr])
                    kb.op("dve", lambda e, g=g, s4=s4: e.tensor_copy(Vs[g].h[:, 16 + s4 * 4:16 + s4 * 4 + 4, :], ptr.h.ap()),
                          reads=[ptr], writes=[Vs[g]])
            n = 0
            for g, d in enumerate(DIL):
                for st_ in range(16):
                    blk, r = st_ // d, st_ % d
                    a = blk * 128 * d + r
                    has_prev = not (sbi == 0 and blk == 0)
                    sc, ol, p_ = psc[n % 2], pol[n % 2], pt[n % 2]
                    n += 1
                    halves = (0, 1) if has_prev else (1,)
                    for hf in halves:
                        ka = SB + a - (128 * d if hf == 0 else 0)
                        kb.op("pe", lambda e, hf=hf, ka=ka, a=a, d=d, g=g, sc=sc: e.matmul(sc.h[:, hf, :], Kb.h[:, ka:ka + 127 * d + 1:d],
                                                                                          Q[g].h[:, a:a + 127 * d + 1:d], start=True, stop=True),
                              reads=[Kb, Q[g]], writes=[sc])
                    lo = halves[0]
                    kb.op("act", lambda e, sc=sc, p_=p_, lo=lo: e.activation(p_.h[:, lo:2, :], sc.h[:, lo:2, :], AF.Exp), reads=[sc], writes=[p_])
                    kb.op("dve", lambda e, p_=p_, lo=lo: e.tensor_tensor(p_.h[:, lo:2, :], p_.h[:, lo:2, :], cs.h[:, 1 + lo:3, :], ALU.mult),
                          reads=[p_, cs], writes=[p_])
                    for j, hf in enumerate(halves):
                        gs = 16 + st_ - (d if hf == 0 else 0)
                        kb.op("pe", lambda e, hf=hf, gs=gs, g=g, ol=ol, p_=p_, j=j: e.matmul(ol.h[:, 0, :], Vs[g].h[:, gs, :], p_.h[:, hf, :],
                                                                                            start=(j == 0), stop=(j == len(halves) - 1)),
                              reads=[Vs[g], p_], writes=[ol])
                    for j, hf in enumerate(halves):
                        kb.op("pe", lambda e, hf=hf, ol=ol, p_=p_, j=j: e.matmul(ol.h[:, 1, :], cs.h[:, 3, :], p_.h[:, hf, :],
                                                                                  start=(j == 0), stop=(j == len(halves) - 1)),
                              reads=[cs, p_], writes=[ol])
                    if g == 0:
                        kb.op("act", lambda e, ol=ol, a=a, d=d: e.activation(acc.h[:, :, a:a + 127 * d + 1:d], ol.h.ap(), AF.Copy), reads=[ol], writes=[acc])
                    else:
                        kb.op("dve", lambda e, ol=ol, a=a, d=d: e.tensor_tensor(acc.h[:, :, a:a + 127 * d + 1:d], acc.h[:, :, a:a + 127 * d + 1:d], ol.h.ap(), ALU.add),
                              reads=[ol, acc], writes=[acc])
            kb.op("dve", lambda e: e.reciprocal(acc.h[:, 1, :], acc.h[:, 1, :]), reads=[acc], writes=[acc])
            kb.op("dve", lambda e: e.tensor_tensor(ob.h.ap(), acc.h[:, 0, :], acc.h[:, 1, :], ALU.mult), reads=[acc], writes=[ob])
            kb.dma("sp", catT, lambda hh=hh, sbi=sbi: catT.h[hh * 128:(hh + 1) * 128, sbi * SB:(sbi + 1) * SB], ob, lambda: ob.h.ap())
            if sbi + 1 < NSB:
                kb.op("pool", lambda e: e.tensor_copy(Kb.h[:, 0:SB], Kb.h[:, SB:2 * SB]), reads=[Kb], writes=[Kb])
                for g in range(3):
                    kb.op("pool", lambda e, g=g: e.tensor_copy(Vs[g].h[:, 0:16, :], Vs[g].h[:, 16:32, :]), reads=[Vs[g]], writes=[Vs[g]])
    kb.finish([catT])


def attn_consts():
    c = np.zeros((128, 4, 128), np.float32)
    k = np.arange(128)[:, None]
    q = np.arange(128)[None, :]
    c[:, 0, :] = np.eye(128)
    c[:, 1, :] = (k >= q)
    c[:, 2, :] = (k <= q)
    c[:, 3, :] = 1.0
    return c.astype(NPBF)


def attn_wslab(att_w_in_bf, heads):
    out = []
    for h in heads:
        cols = [att_w_in_bf[:, g * 2048 + h * 128: g * 2048 + (h + 1) * 128] for g in range(3)]
        cols.append(att_w_in_bf[:, 6144 + h * 128: 6144 + (h + 1) * 128])
        cols.append(att_w_in_bf[:, 8192 + h * 128: 8192 + (h + 1) * 128])
        wcat = np.concatenate(cols, axis=1)
        out.append(wcat.reshape(16, 128, 640).transpose(1, 0, 2))
    return np.ascontiguousarray(np.stack(out))


AB_NCOL = 2308


def ab_builder(kb, S, T=256, DM=2048, C=128, do_hgrn=True, do_ssd=True):
    KC = DM // 128
    NB = S // T
    NCH = T // C
    hT = kb.dram("hT", [DM, S], F32, kind="ExternalInput")
    npre = kb.dram("npre", [128, KC], F32, kind="ExternalInput")
    w = kb.dram("w", [128, KC, AB_NCOL], BF16, kind="ExternalInput")
    cst = kb.dram("cst", [128, 4, 128], BF16, kind="ExternalInput")
    prm = kb.dram("prm", [128, 24], F32, kind="ExternalInput")
    cw = kb.dram("cw", [128, 6, 5], F32, kind="ExternalInput")
    dtbrow = kb.dram("dtbrow", [128, 4], F32, kind="ExternalInput")
    catT = kb.dram("catT", [512, S], BF16, kind="ExternalOutput")
    catT.nowaw = True

    cs = kb.sb("cs", [128, 4, 128], BF16)
    kb.dma("sp", cs, lambda: cs.h.ap(), cst, lambda: cst.h.ap())
    P = kb.sb("P", [128, 24], F32)
    kb.dma("sp", P, lambda: P.h.ap(), prm, lambda: prm.h.ap())
    CW = kb.sb("CW", [128, 6, 5], F32)
    kb.dma("sp", CW, lambda: CW.h.ap(), cw, lambda: cw.h.ap())
    DTB = kb.sb("DTB", [128, 4], F32)
    kb.dma("sp", DTB, lambda: DTB.h.ap(), dtbrow, lambda: dtbrow.h.ap())
    wpre = kb.sb("wpre", [128, KC], F32)
    kb.dma("sp", wpre, lambda: wpre.h.ap(), npre, lambda: npre.h.ap())
    kb.op("dve", lambda e: e.tensor_scalar_mul(wpre.h.ap(), wpre.h.ap(), float(np.sqrt(DM))), reads=[wpre], writes=[wpre])
    wsb = kb.sb("wsb", [128, KC, AB_NCOL], BF16)
    kb.dma("sp", wsb, lambda: wsb.h.ap(), w, lambda: w.h.ap())
    onesf = kb.sb("onesf", [128, C], F32)
    kb.op("pool", lambda e: e.memset(onesf.h.ap(), 1.0), writes=[onesf])
    LB = kb.sb("LB", [128, 4], F32)
    NEGA = kb.sb("NEGA", [128, 4], F32)
    for hd in range(2):
        kb.op("dve", lambda e, hd=hd: e.tensor_tensor(LB.h[:, 2 * hd:2 * hd + 1], P.h[:, 2 * hd:2 * hd + 1], P.h[:, 2 * hd + 1:2 * hd + 2], ALU.subtract),
              reads=[P], writes=[LB])
        kb.op("act", lambda e, hd=hd: e.activation(LB.h[:, 2 * hd:2 * hd + 1], LB.h[:, 2 * hd:2 * hd + 1], AF.Sigmoid), reads=[LB], writes=[LB])
        kb.op("dve", lambda e, hd=hd: e.tensor_scalar(LB.h[:, 2 * hd + 1:2 * hd + 2], LB.h[:, 2 * hd:2 * hd + 1], -1.0, 1.0, ALU.mult, ALU.add),
              reads=[LB], writes=[LB])
    kb.op("act", lambda e: e.activation(NEGA.h.ap(), P.h[:, 6:10], AF.Exp), reads=[P], writes=[NEGA])
    kb.op("dve", lambda e: e.tensor_scalar_mul(NEGA.h.ap(), NEGA.h.ap(), -1.0), reads=[NEGA], writes=[NEGA])

    hin = kb.sb("hin", [128, KC, T], F32)
    xT = [kb.sb(f"xT{k}", [128, T], BF16) for k in range(KC)]
    sq = [kb.sb(f"sq{i}", [128, T], BF16) for i in range(2)]
    rstd = kb.sb("rstd", [128, T], F32)
    pstat = kb.ps("pstat", [128, T])
    pproj = [kb.ps(f"pproj{i}", [128, T]) for i in range(2)]
    pA = kb.ps("pA", [128, 128])
    pO = kb.ps("pO", [128, 128])
    pS = kb.ps("pS", [128, 128])
    pT_ = kb.ps("pT", [128, 128], BF16)
    pN = kb.ps("pN", [128, 128])

    def prenorm(t0):
        kb.dma("sp", hin, lambda: hin.h.ap(), hT, lambda: hT.h[:, t0:t0 + T].rearrange("(kc p) n -> p kc n", p=128))
        for k in range(KC):
            q = sq[k % 2]
            kb.op("act", lambda e, q=q, k=k: e.activation(q.h.ap(), hin.h[:, k, :], AF.Square), reads=[hin], writes=[q])
            kb.op("pe", lambda e, q=q, k=k: e.matmul(pstat.h.ap(), cs.h[:, 3, :], q.h.ap(), start=(k == 0), stop=(k == KC - 1)),
                  reads=[cs, q], writes=[pstat])
        kb.op("act", lambda e: e.activation(rstd.h.ap(), pstat.h.ap(), AF.Sqrt, bias=float(DM * EPS)), reads=[pstat], writes=[rstd])
        kb.op("dve", lambda e: e.reciprocal(rstd.h.ap(), rstd.h.ap()), reads=[rstd], writes=[rstd])
        for k in range(KC):
            kb.op("dve", lambda e, k=k: e.scalar_tensor_tensor(xT[k].h.ap(), hin.h[:, k, :], wpre.h[:, k:k + 1], rstd.h.ap(), ALU.mult, ALU.mult),
                  reads=[hin, wpre, rstd], writes=[xT[k]])

    npj = [0]

    def proj_fm(col, M, evac):
        pp = pproj[npj[0] % 2]
        npj[0] += 1
        for k in range(KC):
            kb.op("pe", lambda e, k=k, pp=pp: e.matmul(pp.h[0:M, :], wsb.h[:, k, col:col + M], xT[k].h.ap(), start=(k == 0), stop=(k == KC - 1)),
                  reads=[wsb, xT[k]], writes=[pp])
        evac(pp)

    def proj_tm(col, N, c, evac):
        pp = pproj[npj[0] % 2]
        npj[0] += 1
        for k in range(KC):
            kb.op("pe", lambda e, k=k, pp=pp: e.matmul(pp.h[:, 0:N], xT[k].h[:, c * C:(c + 1) * C], wsb.h[:, k, col:col + N], start=(k == 0), stop=(k == KC - 1)),
                  reads=[wsb, xT[k]], writes=[pp])
        evac(pp)

    bt = kb.sb("g_b", [128, C], F32)
    nbm = kb.sb("g_nbm", [128, 1], F32)
    eq = kb.sb("g_eq", [128, C], F32)
    ek = kb.sb("g_ek", [128, C], F32)
    ei = kb.sb("g_ei", [128, C], F32)
    es = kb.sb("g_es", [128, C], F32)
    qs = kb.sb("g_qs", [128, C], BF16)
    ks = kb.sb("g_ks", [128, C], BF16)
    qi = kb.sb("g_qi", [128, C], BF16)
    kT = kb.sb("g_kT", [128, C], BF16)
    kst = kb.sb("g_kst", [128, 128], BF16)
    pTs = kb.sb("g_pT", [128, 128], BF16)

    acol = kb.sb("g_acol", [128, 1], F32)
    dl = kb.sb("g_dl", [128, C], F32)
    scS = kb.sb("g_scS", [128, C], F32)
    qbf = kb.sb("g_qbf", [128, C], BF16)
    kbf = kb.sb("g_kbf", [128, C], BF16)

    def gla(qt, qap, kt, kap, ldt, ldap, V, dv, Sf, Sb, scalar_decay=False):
        kb.op("dve", lambda e: e.tensor_tensor_scan(bt.h.ap(), onesf.h.ap(), ldap(), 0.0, ALU.mult, ALU.add), reads=[onesf, ldt], writes=[bt])
        kb.op("dve", lambda e: e.tensor_scalar_mul(nbm.h.ap(), bt.h[:, C // 2:C // 2 + 1], -1.0), reads=[bt], writes=[nbm])
        kb.op("act", lambda e: e.activation(ei.h.ap(), bt.h.ap(), AF.Exp), reads=[bt], writes=[ei])
        kb.op("act", lambda e: e.activation(es.h.ap(), bt.h.ap(), AF.Exp, bias=bt.h[:, C - 1:C], scale=-1.0), reads=[bt], writes=[es])
        kb.op("dve", lambda e: e.tensor_tensor(qi.h.ap(), qap(), ei.h.ap(), ALU.mult), reads=[qt, ei], writes=[qi])
        kb.op("pool", lambda e: e.tensor_tensor(kT.h.ap(), kap(), es.h.ap(), ALU.mult), reads=[kt, es], writes=[kT])
        kb.op("pe", lambda e: e.transpose(pT_.h.ap(), kT.h.ap(), cs.h[:, 0, :]), reads=[kT, cs], writes=[pT_])
        kb.op("act", lambda e: e.activation(kst.h.ap(), pT_.h.ap(), AF.Copy), reads=[pT_], writes=[kst])
        if not scalar_decay:
            kb.op("act", lambda e: e.activation(eq.h.ap(), bt.h.ap(), AF.Exp, bias=nbm.h.ap()), reads=[bt, nbm], writes=[eq])
            kb.op("act", lambda e: e.activation(ek.h.ap(), bt.h.ap(), AF.Exp, bias=bt.h[:, C // 2:C // 2 + 1], scale=-1.0), reads=[bt], writes=[ek])
            kb.op("dve", lambda e: e.tensor_tensor(qs.h.ap(), qap(), eq.h.ap(), ALU.mult), reads=[qt, eq], writes=[qs])
            kb.op("pool", lambda e: e.tensor_tensor(ks.h.ap(), kap(), ek.h.ap(), ALU.mult), reads=[kt, ek], writes=[ks])
            kb.op("pe", lambda e: e.matmul(pA.h.ap(), ks.h.ap(), qs.h.ap(), start=True, stop=True), reads=[ks, qs], writes=[pA])
            kb.op("dve", lambda e: e.tensor_tensor(pTs.h.ap(), pA.h.ap(), cs.h[:, 2, :], ALU.mult), reads=[pA, cs], writes=[pTs])
        else:
            kb.op("dve", lambda e: e.tensor_tensor(dl.h.ap(), bt.h.ap(), cs.h[:, 0, :], ALU.mult), reads=[bt, cs], writes=[dl])
            kb.op("dve", lambda e: e.reduce_sum(acol.h.ap(), dl.h.ap(), mybir.AxisListType.X), reads=[dl], writes=[acol])
            kb.op("dve", lambda e: e.tensor_scalar(dl.h.ap(), bt.h.ap(), acol.h.ap(), 0.0, ALU.subtract, ALU.min), reads=[bt, acol], writes=[dl])
            kb.op("act", lambda e: e.activation(dl.h.ap(), dl.h.ap(), AF.Exp), reads=[dl], writes=[dl])
            kb.op("pool", lambda e: e.tensor_tensor(dl.h.ap(), dl.h.ap(), cs.h[:, 2, :], ALU.mult), reads=[dl, cs], writes=[dl])
            kb.op("dve", lambda e: e.tensor_tensor(pTs.h.ap(), scS.h.ap(), dl.h.ap(), ALU.mult), reads=[scS, dl], writes=[pTs])
        kb.op("pe", lambda e: e.matmul(pO.h[0:dv, :], V.h[:, 0:dv], pTs.h.ap(), start=True, stop=False), reads=[V, pTs], writes=[pO])
        kb.op("pe", lambda e: e.matmul(pO.h[0:dv, :], Sb.h[:, 0:dv], qi.h.ap(), start=False, stop=True), reads=[Sb, qi], writes=[pO])
        kb.op("pe", lambda e: e.matmul(pS.h[:, 0:dv], kst.h.ap(), V.h[:, 0:dv], start=True, stop=True), reads=[kst, V], writes=[pS])
        kb.op("dve", lambda e: e.scalar_tensor_tensor(Sf.h[:, 0:dv], Sf.h[:, 0:dv], ei.h[:, C - 1:C], pS.h[:, 0:dv], ALU.mult, ALU.add),
              reads=[Sf, ei, pS], writes=[Sf])
        kb.op("act", lambda e: e.activation(Sb.h[:, 0:dv], Sf.h[:, 0:dv], AF.Copy), reads=[Sf], writes=[Sb])

    if do_hgrn:
        hq = kb.sb("hq", [128, T], F32)
        hk = kb.sb("hk", [128, T], F32)
        hl = kb.sb("hl", [128, T], F32)
        hg = kb.sb("hg", [128, T], F32)
        hV = kb.sb("hV", [128, 128], BF16)
        hsq = kb.sb("hsq", [128, C], BF16)
        hrs = kb.sb("hrs", [128, C], F32)
        ho = kb.sb("ho", [128, C], F32)
        hob = [kb.sb(f"hob{i}", [128, T], BF16) for i in range(2)]
        HSf = [kb.sb(f"HSf{i}", [128, 128], F32) for i in range(2)]
        HSb = [kb.sb(f"HSb{i}", [128, 128], BF16) for i in range(2)]
        for i in range(2):
            kb.op("pool", lambda e, i=i: e.memset(HSf[i].h.ap(), 0.0), writes=[HSf[i]])
            kb.op("pool", lambda e, i=i: e.memset(HSb[i].h.ap(), 0.0), writes=[HSb[i]])

    if do_ssd:
        raw = [kb.sb(f"raw{i}", [128, 3 + T], F32) for i in range(6)]
        cv = [kb.sb(f"cv{i}", [128, T], F32) for i in range(6)]
        zs = [kb.sb(f"zs{i}", [64, T], F32) for i in range(4)]
        ld = [kb.sb(f"ld{i}", [128, T], F32) for i in range(4)]
        dtt = kb.sb("dtt", [128, 4], F32)
        sV = kb.sb("sV", [128, 64], BF16)
        syz = [kb.sb(f"syz{i}", [64, C], F32) for i in range(4)]
        ssq = kb.sb("ssq", [64, C], BF16)
        srs = kb.sb("srs", [64, C], F32)
        sob = [kb.sb(f"sob{i}", [64, C], BF16) for i in range(4)]
        SSf = [kb.sb(f"SSf{i}", [128, 64], F32) for i in range(4)]
        SSb = [kb.sb(f"SSb{i}", [128, 64], BF16) for i in range(4)]
        for i in range(4):
            kb.op("pool", lambda e, i=i: e.memset(SSf[i].h.ap(), 0.0), writes=[SSf[i]])
            kb.op("pool", lambda e, i=i: e.memset(SSb[i].h.ap(), 0.0), writes=[SSb[i]])
        for i in range(6):
            kb.op("pool", lambda e, i=i: e.memset(raw[i].h[:, 0:3], 0.0), writes=[raw[i]])

    for blk in range(NB):
        t0 = blk * T
        prenorm(t0)
        if do_hgrn:
            for hd in range(2):
                base = hd * 512
                proj_fm(base, 128, lambda pp: kb.op("act", lambda e: e.activation(hq.h.ap(), pp.h.ap(), AF.Copy), reads=[pp], writes=[hq]))
                proj_fm(base + 128, 128, lambda pp: kb.op("act", lambda e: e.activation(hk.h.ap(), pp.h.ap(), AF.Sigmoid), reads=[pp], writes=[hk]))
                kb.op("dve", lambda e, hd=hd: e.tensor_scalar(hk.h.ap(), hk.h.ap(), LB.h[:, 2 * hd + 1:2 * hd + 2], LB.h[:, 2 * hd:2 * hd + 1], ALU.mult, ALU.add),
                      reads=[hk, LB], writes=[hk])
                kb.op("act", lambda e: e.activation(hl.h.ap(), hk.h.ap(), AF.Ln), reads=[hk], writes=[hl])
                kb.op("dve", lambda e: e.tensor_scalar(hk.h.ap(), hk.h.ap(), -1.0, 1.0, ALU.mult, ALU.add), reads=[hk], writes=[hk])
                proj_fm(base + 256, 128, lambda pp: kb.op("act", lambda e: e.activation(hg.h.ap(), pp.h.ap(), AF.Silu), reads=[pp], writes=[hg]))
                ob_ = hob[hd]
                for c in range(NCH):
                    cl = slice(c * C, (c + 1) * C)
                    proj_tm(base + 384, 128, c, lambda pp: kb.op("dve", lambda e: e.tensor_copy(hV.h.ap(), pp.h[:, 0:128]), reads=[pp], writes=[hV]))
                    gla(hq, lambda cl=cl: hq.h[:, cl], hk, lambda cl=cl: hk.h[:, cl], hl, lambda cl=cl: hl.h[:, cl], hV, 128, HSf[hd], HSb[hd])
                    kb.op("act", lambda e: e.activation(hsq.h.ap(), pO.h.ap(), AF.Square), reads=[pO], writes=[hsq])
                    kb.op("pe", lambda e: e.matmul(pN.h.ap(), cs.h[:, 3, :], hsq.h.ap(), start=True, stop=True), reads=[cs, hsq], writes=[pN])
                    kb.op("act", lambda e: e.activation(hrs.h.ap(), pN.h.ap(), AF.Sqrt, bias=float(EPS), scale=1.0 / 128), reads=[pN], writes=[hrs])
                    kb.op("dve", lambda e: e.reciprocal(hrs.h.ap(), hrs.h.ap()), reads=[hrs], writes=[hrs])
                    kb.op("dve", lambda e: e.tensor_tensor(ho.h.ap(), pO.h.ap(), hrs.h.ap(), ALU.mult), reads=[pO, hrs], writes=[ho])
                    kb.op("dve", lambda e, hd=hd, cl=cl, ob_=ob_: e.scalar_tensor_tensor(ob_.h[:, cl], ho.h.ap(), P.h[:, 4 + hd:5 + hd], hg.h[:, cl], ALU.mult, ALU.mult),
                          reads=[ho, P, hg], writes=[ob_])
                kb.dma("sp", catT, lambda hd=hd, t0=t0: catT.h[hd * 128:(hd + 1) * 128, t0:t0 + T], ob_, lambda ob_=ob_: ob_.h.ap())
        if do_ssd:
            for r in range(4):
                proj_fm(1024 + r * 64, 64, lambda pp, r=r: kb.op("act", lambda e: e.activation(raw[r].h[0:64, 3:3 + T], pp.h[0:64, :], AF.Copy), reads=[pp], writes=[raw[r]]))
                proj_fm(1280 + r * 64, 64, lambda pp, r=r: kb.op("act", lambda e: e.activation(zs[r].h.ap(), pp.h[0:64, :], AF.Silu), reads=[pp], writes=[zs[r]]))
            proj_fm(1536, 128, lambda pp: kb.op("act", lambda e: e.activation(raw[4].h[:, 3:3 + T], pp.h.ap(), AF.Copy), reads=[pp], writes=[raw[4]]))
            proj_fm(1664, 128, lambda pp: kb.op("act", lambda e: e.activation(raw[5].h[:, 3:3 + T], pp.h.ap(), AF.Copy), reads=[pp], writes=[raw[5]]))
            for i in range(6):
                np_ = 64 if i < 4 else 128
                kb.op("dve", lambda e, i=i, np_=np_: e.tensor_scalar(cv[i].h[0:np_, :], raw[i].h[0:np_, 0:T], CW.h[0:np_, i, 0:1], CW.h[0:np_, i, 4:5], ALU.mult, ALU.add),
                      reads=[raw[i], CW], writes=[cv[i]])
                for j in range(1, 4):
                    kb.op("dve", lambda e, i=i, j=j, np_=np_: e.scalar_tensor_tensor(cv[i].h[0:np_, :], raw[i].h[0:np_, j:j + T], CW.h[0:np_, i, j:j + 1], cv[i].h[0:np_, :], ALU.mult, ALU.add),
                          reads=[raw[i], CW, cv[i]], writes=[cv[i]])
                kb.op("act", lambda e, i=i, np_=np_: e.activation(cv[i].h[0:np_, :], cv[i].h[0:np_, :], AF.Silu), reads=[cv[i]], writes=[cv[i]])
                kb.op("pool", lambda e, i=i, np_=np_: e.tensor_copy(raw[i].h[0:np_, 0:3], raw[i].h[0:np_, T:T + 3]), reads=[raw[i]], writes=[raw[i]])
            for r in range(4):
                proj_fm(1792 + r * 128, 128, lambda pp, r=r: kb.op("act", lambda e: e.activation(ld[r].h.ap(), pp.h.ap(), AF.Exp, bias=P.h[:, 10 + r:11 + r]), reads=[pp, P], writes=[ld[r]]))
                kb.op("act", lambda e, r=r: e.activation(ld[r].h.ap(), ld[r].h.ap(), AF.Ln, bias=1.0), reads=[ld[r]], writes=[ld[r]])
                kb.op("dve", lambda e, r=r: e.tensor_scalar_mul(ld[r].h.ap(), ld[r].h.ap(), NEGA.h[:, r:r + 1]), reads=[ld[r], NEGA], writes=[ld[r]])
            for c in range(NCH):
                cl = slice(c * C, (c + 1) * C)
                proj_tm(2304, 4, c, lambda pp: kb.op("dve", lambda e: e.tensor_tensor(dtt.h.ap(), pp.h[:, 0:4], DTB.h.ap(), ALU.add), reads=[pp, DTB], writes=[dtt]))
                kb.op("act", lambda e: e.activation(dtt.h.ap(), dtt.h.ap(), AF.Exp), reads=[dtt], writes=[dtt])
                kb.op("act", lambda e: e.activation(dtt.h.ap(), dtt.h.ap(), AF.Ln, bias=1.0), reads=[dtt], writes=[dtt])
                kb.op("dve", lambda e, cl=cl: e.tensor_copy(kbf.h.ap(), cv[4].h[:, cl]), reads=[cv[4]], writes=[kbf])
                kb.op("pool", lambda e, cl=cl: e.tensor_copy(qbf.h.ap(), cv[5].h[:, cl]), reads=[cv[5]], writes=[qbf])
                kb.op("pe", lambda e: e.matmul(pA.h.ap(), kbf.h.ap(), qbf.h.ap(), start=True, stop=True), reads=[kbf, qbf], writes=[pA])
                kb.op("act", lambda e: e.activation(scS.h.ap(), pA.h.ap(), AF.Copy), reads=[pA], writes=[scS])
                for r in range(4):
                    kb.op("dve", lambda e, r=r, cl=cl: e.tensor_copy(kT.h[0:64, :], cv[r].h[0:64, cl]), reads=[cv[r]], writes=[kT])
                    kb.op("pe", lambda e: e.transpose(pT_.h[:, 0:64], kT.h[0:64, :], cs.h[0:64, 0, 0:64]), reads=[kT, cs], writes=[pT_])
                    kb.op("dve", lambda e, r=r: e.tensor_scalar_mul(sV.h.ap(), pT_.h[:, 0:64], dtt.h[:, r:r + 1]), reads=[pT_, dtt], writes=[sV])
                    gla(cv[5], lambda cl=cl: cv[5].h[:, cl], cv[4], lambda cl=cl: cv[4].h[:, cl], ld[r], lambda r=r, cl=cl: ld[r].h[:, cl], sV, 64, SSf[r], SSb[r], scalar_decay=True)
                    kb.op("dve", lambda e, r=r, cl=cl: e.scalar_tensor_tensor(syz[r].h.ap(), cv[r].h[0:64, cl], P.h[0:64, 14 + r:15 + r], pO.h[0:64, :], ALU.mult, ALU.add),
                          reads=[cv[r], P, pO], writes=[syz[r]])
                    kb.op("dve", lambda e, r=r, cl=cl: e.tensor_tensor(syz[r].h.ap(), syz[r].h.ap(), zs[r].h[:, cl], ALU.mult), reads=[syz[r], zs[r]], writes=[syz[r]])
                    kb.op("act", lambda e, r=r: e.activation(ssq.h.ap(), syz[r].h.ap(), AF.Square), reads=[syz[r]], writes=[ssq])
                    kb.op("pe", lambda e, r=r: e.matmul(pN.h[0:64, :], cs.h[0:64, 3, 0:64], ssq.h.ap(), start=(r == 0), stop=(r == 3)), reads=[cs, ssq], writes=[pN])
                kb.op("act", lambda e: e.activation(srs.h.ap(), pN.h[0:64, :], AF.Sqrt, bias=float(EPS), scale=1.0 / 256), reads=[pN], writes=[srs])
                kb.op("dve", lambda e: e.reciprocal(srs.h.ap(), srs.h.ap()), reads=[srs], writes=[srs])
                for r in range(4):
                    kb.op("dve", lambda e, r=r: e.scalar_tensor_tensor(sob[r].h.ap(), syz[r].h.ap(), P.h[0:64, 18 + r:19 + r], srs.h.ap(), ALU.mult, ALU.mult),
                          reads=[syz[r], P, srs], writes=[sob[r]])
                    kb.dma("sp", catT, lambda r=r, c=c, t0=t0: catT.h[256 + r * 64:256 + (r + 1) * 64, t0 + c * C:t0 + (c + 1) * C], sob[r], lambda r=r: sob[r].h.ap())
    kb.finish([catT])


def ab_inputs(inp_bf_w_in, z, j):
    W = inp_bf_w_in
    cols = []
    for hd in (2 * j, 2 * j + 1):
        for off in (0, 1024, 3072, 2048):
            cols.append(W[:, off + hd * 128: off + (hd + 1) * 128])
    for r in range(4):
        cols.append(W[:, 5120 + j * 256 + r * 64: 5120 + j * 256 + (r + 1) * 64])
    for r in range(4):
        cols.append(W[:, 4096 + j * 256 + r * 64: 4096 + j * 256 + (r + 1) * 64])
    cols.append(W[:, 6144 + j * 128: 6144 + (j + 1) * 128])
    cols.append(W[:, 6656 + j * 128: 6656 + (j + 1) * 128])
    for r in range(4):
        cols.append(np.repeat(W[:, 7168 + 4 * j + r: 7168 + 4 * j + r + 1], 128, axis=1))
    cols.append(W[:, 7168 + 4 * j: 7168 + 4 * j + 4])
    wc = np.concatenate(cols, axis=1)
    assert wc.shape[1] == AB_NCOL
    wt = np.ascontiguousarray(wc.reshape(16, 128, AB_NCOL).transpose(1, 0, 2))
    prm = np.zeros((128, 24), np.float32)
    for i, hd in enumerate((2 * j, 2 * j + 1)):
        prm[:, 2 * i] = z["hgrn_lb"][0, hd * 128:(hd + 1) * 128]
        prm[:, 2 * i + 1] = z["hgrn_lb"][1, hd * 128:(hd + 1) * 128]
        prm[:, 4 + i] = z["hgrn_norm_w"][0, hd * 128:(hd + 1) * 128]
    for r in range(4):
        hh = 4 * j + r
        prm[:, 6 + r] = z["ssm_A_log"][0, hh]
        prm[:, 10 + r] = z["ssm_dt_bias"][0, hh]
        prm[:, 14 + r] = z["ssm_D"][0, hh]
        prm[0:64, 18 + r] = z["ssm_norm_w"][0, j * 256 + r * 64: j * 256 + (r + 1) * 64]
    cw = np.zeros((128, 6, 5), np.float32)
    cwt, cb = z["ssm_conv_w"][0], z["ssm_conv_b"][0]
    for r in range(4):
        ch = slice(j * 256 + r * 64, j * 256 + (r + 1) * 64)
        cw[0:64, r, 0:4] = cwt[:, ch].T
        cw[0:64, r, 4] = cb[ch]
    for i, off in ((4, 1024), (5, 1536)):
        ch = slice(off + j * 128, off + (j + 1) * 128)
        cw[:, i, 0:4] = cwt[:, ch].T
        cw[:, i, 4] = cb[ch]
    dtb = np.tile(z["ssm_dt_bias"][0, 4 * j:4 * j + 4][None, :], (128, 1)).astype(np.float32)
    return {"w": wt, "prm": prm, "cw": cw, "dtbrow": dtb}


def cast_builder(kb, M, CH=4096):
    x = kb.dram("x", [128, M], F32, kind="ExternalInput")
    y = kb.dram("y", [128, M], BF16, kind="ExternalOutput")
    y.nowaw = True
    tin = [kb.sb(f"ci{i}", [128, CH], F32) for i in range(3)]
    tout = [kb.sb(f"co{i}", [128, CH], BF16) for i in range(3)]
    n = M // CH
    for i in range(n):
        a, b = tin[i % 3], tout[i % 3]
        kb.dma("sp", a, lambda a=a: a.h.ap(), x, lambda i=i: x.h[:, i * CH:(i + 1) * CH])
        if i % 2 == 0:
            kb.op("act", lambda e, a=a, b=b: e.activation(b.h.ap(), a.h.ap(), AF.Copy), reads=[a], writes=[b])
        else:
            kb.op("dve", lambda e, a=a, b=b: e.tensor_copy(b.h.ap(), a.h.ap()), reads=[a], writes=[b])
        kb.dma("sp", y, lambda i=i: y.h[:, i * CH:(i + 1) * CH], b, lambda b=b: b.h.ap())
    kb.finish([y])


def _tile_w(w):
    K, N = w.shape
    return np.ascontiguousarray(w.reshape(K // 128, 128, N // 128, 128).transpose(2, 1, 0, 3))


def _pp(v):
    return np.ascontiguousarray(np.asarray(v, np.float32).reshape(16, 128).T)


NCORES = 8
_CORES = list(range(NCORES))


def _run(nc, maps):
    return run_bass_kernel_spmd(nc, maps, core_ids=_CORES).results


def kernel(x, norm_pre, norm_post, ffn_w_gate, ffn_w_up, ffn_w_down, ab_w_in, ab_w_out,
           hgrn_lb, hgrn_norm_w, ssm_conv_w, ssm_conv_b, ssm_dt_bias, ssm_A_log, ssm_D,
           ssm_norm_w, att_w_in, att_w_out):
    f32 = lambda a: np.asarray(a, np.float32)
    x = f32(x)
    norm_pre, norm_post = f32(norm_pre), f32(norm_post)
    B, S, D = x.shape
    NTOK = B * S // NCORES
    QC = NCORES // B
    srcs = []
    for l in range(2):
        for j in range(2):
            for w in (ffn_w_gate[l, j], ffn_w_up[l, j], ffn_w_down[l, j]):
                srcs.append(_tile_w(f32(w)))
    srcs += [f32(ab_w_in[0]), _tile_w(f32(ab_w_out[0])), f32(att_w_in[0]), _tile_w(f32(att_w_out[0]))]
    shapes = [t.shape for t in srcs]
    flat = np.concatenate([t.reshape(-1) for t in srcs])
    CH = 4096
    per = NCORES * 128 * CH
    tot = -(-flat.size // per) * per
    flat = np.concatenate([flat, np.zeros(tot - flat.size, np.float32)])
    M = tot // (NCORES * 128)
    nc0, _ = build(cast_builder, M, CH)
    parts = flat.reshape(NCORES, 128, M)
    r0 = _run(nc0, [{"x": parts[c]} for c in range(NCORES)])
    fb = np.concatenate([np.asarray(r0[c]["y"]).reshape(-1) for c in range(NCORES)])
    del flat, parts, srcs
    wts, off = [], 0
    for shp in shapes:
        n = int(np.prod(shp))
        wts.append(fb[off:off + n].reshape(shp))
        off += n
    ffw = wts[:12]
    ab_in_bf, ab_out_t, att_in_bf, att_out_t = wts[12:16]
    small = {"hgrn_lb": f32(hgrn_lb), "hgrn_norm_w": f32(hgrn_norm_w), "ssm_conv_w": f32(ssm_conv_w),
             "ssm_conv_b": f32(ssm_conv_b), "ssm_dt_bias": f32(ssm_dt_bias), "ssm_A_log": f32(ssm_A_log),
             "ssm_D": f32(ssm_D), "ssm_norm_w": f32(ssm_norm_w)}

    def ffn_map(k, l, slot, idx):
        return {f"wg{idx}": ffw[3 * k], f"wu{idx}": ffw[3 * k + 1], f"wd{idx}": ffw[3 * k + 2],
                f"npre{idx}": _pp(norm_pre[l, slot]), f"npost{idx}": _pp(norm_post[l, slot])}

    def full_seq(hs):
        return [np.ascontiguousarray(np.concatenate(hs[b * QC:(b + 1) * QC], axis=1)) for b in range(B)]

    def tok_shards(cat_b):
        return [np.ascontiguousarray(cat_b[c // QC][:, (c % QC) * NTOK:((c % QC) + 1) * NTOK]) for c in range(NCORES)]

    h = x.reshape(B * S, D)
    hT = [np.ascontiguousarray(h[c * NTOK:(c + 1) * NTOK].T) for c in range(NCORES)]
    consts = attn_consts()
    nc1, _ = build(tok_builder, NTOK, nffn=1, prologue=False)
    com = ffn_map(0, 0, 0, 0)
    r = _run(nc1, [dict(com, hT=hT[c]) for c in range(NCORES)])
    hT = [np.asarray(r[c]["out"]) for c in range(NCORES)]
    nc2, _ = build(ab_builder, S)
    hb = full_seq(hT)
    maps = []
    for c in range(NCORES):
        b, j = c // QC, c % QC
        m = ab_inputs(ab_in_bf, small, j)
        m.update({"hT": hb[b], "npre": _pp(norm_pre[0, 1]), "cst": consts})
        maps.append(m)
    r = _run(nc2, maps)
    del hb, maps
    cat_b = []
    for b in range(B):
        cat = np.empty((D, S), NPBF)
        for j in range(QC):
            o = np.asarray(r[b * QC + j]["catT"])
            cat[j * 256:(j + 1) * 256] = o[0:256]
            cat[1024 + j * 256:1024 + (j + 1) * 256] = o[256:512]
        cat_b.append(cat)
    nc3, _ = build(tok_builder, NTOK, nffn=2, prologue=True)
    com = {"wo": ab_out_t, "npo": _pp(norm_post[0, 1])}
    com.update(ffn_map(1, 0, 2, 0))
    com.update(ffn_map(2, 1, 0, 1))
    cs_ = tok_shards(cat_b)
    r = _run(nc3, [dict(com, hT=hT[c], catT=cs_[c]) for c in range(NCORES)])
    hT = [np.asarray(r[c]["out"]) for c in range(NCORES)]
    nc4, _ = build(attn_builder, S, NH=4)
    hb = full_seq(hT)
    maps = []
    for c in range(NCORES):
        b, j = c // QC, c % QC
        maps.append({"hT": hb[b], "npre": _pp(norm_pre[1, 1]), "cst": consts,
                     "w": attn_wslab(att_in_bf, list(range(4 * j, 4 * j + 4)))})
    r = _run(nc4, maps)
    del hb, maps
    cat_b = [np.ascontiguousarray(np.concatenate([np.asarray(r[b * QC + j]["catT"]) for j in range(QC)], axis=0)) for b in range(B)]
    nc5, _ = build(tok_builder, NTOK, nffn=1, prologue=True)
    com = {"wo": att_out_t, "npo": _pp(norm_post[1, 1])}
    com.update(ffn_map(3, 1, 2, 0))
    cs_ = tok_shards(cat_b)
    r = _run(nc5, [dict(com, hT=hT[c], catT=cs_[c]) for c in range(NCORES)])
    out = np.concatenate([np.asarray(r[c]["out"]).T for c in range(NCORES)], axis=0).reshape(B, S, D)
    return np.ascontiguousarray(out.astype(np.float32))
```

```python
import numpy as np
import ml_dtypes
import concourse.bass as bass
import concourse.mybir as mybir
from concourse.bass_utils import run_bass_kernel_spmd

F32 = mybir.dt.float32
BF16 = mybir.dt.bfloat16
AF = mybir.ActivationFunctionType
ALU = mybir.AluOpType
NPBF = ml_dtypes.bfloat16

NOSYNC_SAME = ("dve", "act")
SMALL_FREE = 32


class Tk:
    __slots__ = ("name", "h", "w", "rd", "dsem", "dcnt", "nowaw", "small")

    def __init__(self, name, h):
        self.name = name
        self.h = h
        self.w = None
        self.rd = {}
        self.dsem = None
        self.dcnt = 0
        self.nowaw = False
        self.small = False

    def __getitem__(self, k):
        return self.h[k]


class KB:
    def __init__(self, nc, emit, need=None):
        self.nc = nc
        self.emit = emit
        if nc is not None:
            self.E = {"pe": nc.tensor, "act": nc.scalar, "dve": nc.vector, "pool": nc.gpsimd, "sp": nc.sync}
        else:
            self.E = {"pe": None, "act": None, "dve": None, "pool": None, "sp": None}
        self.idx = {e: 0 for e in self.E}
        self.need = need if need is not None else {e: set() for e in self.E}
        self.rank = None
        if emit:
            self.rank = {}
            for e in self.E:
                srt = sorted(self.need[e])
                self.rank[e] = {ix: i + 1 for i, ix in enumerate(srt)}
        self.waited = {e: {} for e in self.E}
        self.sems = {}
        self.tiles = {}
        self.ntile = 0
        self.nsem = 0
        self.ninst = 0

    def _sem(self, name):
        if name not in self.sems:
            self.nsem += 1
            self.sems[name] = self.nc.alloc_semaphore(name) if self.emit else name
        return self.sems[name]

    def sb(self, name, shape, dt):
        h = self.nc.alloc_sbuf_tensor(name, list(shape), dt) if self.emit else None
        t = Tk(name, h)
        t.small = int(np.prod(shape[1:])) <= SMALL_FREE
        self.tiles[name] = t
        return t

    def ps(self, name, shape, dt=F32):
        h = self.nc.alloc_psum_tensor(name, list(shape), dt) if self.emit else None
        t = Tk(name, h)
        self.tiles[name] = t
        return t

    def dram(self, name, shape, dt, kind="Internal"):
        h = self.nc.dram_tensor(name, list(shape), dt, kind=kind) if self.emit else None
        t = Tk(name, h)
        self.tiles[name] = t
        return t

    def _key(self, dep):
        return (dep[0], dep[1] if dep[0] == 'c' else dep[1].name)

    def _wait(self, e, dep, t=None):
        if dep is None:
            return
        key = self._key(dep)
        val = dep[2]
        if dep[0] == 'c' and dep[1] == e:
            if e in ("pe", "sp"):
                return
            if e in NOSYNC_SAME and not (t is not None and t.small):
                return
        if self.waited[e].get(key, 0) >= val:
            return
        self.waited[e][key] = val
        if dep[0] == 'c':
            if not self.emit:
                self.need[dep[1]].add(val)
            else:
                self.E[e].wait_ge(self._sem("c_" + dep[1]), self.rank[dep[1]][val])
        else:
            if self.emit:
                self.E[e].wait_ge(dep[1].dsem, 16 * val)

    def _deps(self, e, reads, writes):
        for t in reads:
            self._wait(e, t.w, t)
        for t in writes:
            if t.nowaw:
                continue
            self._wait(e, t.w, t)
            for d in t.rd.values():
                self._wait(e, d, t)

    def op(self, e, fn, reads=(), writes=()):
        self._deps(e, reads, writes)
        self.idx[e] += 1
        ix = self.idx[e]
        self.ninst += 1
        if self.emit:
            ins = fn(self.E[e])
            if ix in self.rank[e]:
                ins.then_inc(self._sem("c_" + e), 1)
        dep = ('c', e, ix)
        for t in reads:
            t.rd[('c', e)] = dep
        for t in writes:
            t.w = dep
            t.rd = {}

    def dma(self, q, dst, dst_ap, src, src_ap, **kw):
        self._deps(q, [src], [dst])
        self.idx[q] += 1
        self.ninst += 1
        if dst.dsem is None:
            dst.dsem = self._sem("d_" + dst.name)
        dst.dcnt += 1
        if self.emit:
            self.E[q].dma_start(out=dst_ap(), in_=src_ap(), **kw).then_inc(dst.dsem, 16)
        dep = ('d', dst, dst.dcnt)
        src.rd[('d', dst.name)] = dep
        dst.w = dep
        dst.rd = {}

    def finish(self, outs, e="sp"):
        for t in outs:
            self._wait(e, t.w)


def build(fn, *args, **kw):
    kb1 = KB(None, False)
    fn(kb1, *args, **kw)
    nc = bass.Bass("TRN2", target_bir_lowering=False)
    kb2 = KB(nc, True, need=kb1.need)
    fn(kb2, *args, **kw)
    return nc, kb2


EPS = 1e-6


class Ring:
    def __init__(self, kb, name, shape, dt, R, src, src_ap_of, nblocks, q="sp", tiles=None):
        self.kb = kb
        self.tiles = tiles if tiles is not None else [kb.sb(f"{name}{i}", shape, dt) for i in range(R)]
        self.R = R
        self.src = src
        self.src_ap_of = src_ap_of
        self.n = nblocks
        self.issued = 0
        self.q = q

    def prefetch(self, upto):
        while self.issued < min(upto + 1, self.n):
            i = self.issued
            t = self.tiles[i % self.R]
            self.kb.dma(self.q, t, (lambda t=t: t.h.ap()), self.src, (lambda i=i: self.src_ap_of(i)))
            self.issued += 1

    def get(self, i):
        self.prefetch(i + self.R - 1)
        return self.tiles[i % self.R]


def tok_builder(kb, NTOK, nffn=1, prologue=False, T=512, DM=2048, DF=5632):
    KC = DM // 128
    FC = DF // 128
    OC = DM // 128
    NT = NTOK // T
    hT = kb.dram("hT", [DM, NTOK], F32, kind="ExternalInput")
    out = kb.dram("out", [DM, NTOK], F32, kind="ExternalOutput")
    out.nowaw = True
    W = []
    for j in range(nffn):
        W.append(dict(
            wg=kb.dram(f"wg{j}", [FC, 128, KC, 128], BF16, kind="ExternalInput"),
            wu=kb.dram(f"wu{j}", [FC, 128, KC, 128], BF16, kind="ExternalInput"),
            wd=kb.dram(f"wd{j}", [OC, 128, FC, 128], BF16, kind="ExternalInput"),
            npre=kb.dram(f"npre{j}", [128, KC], F32, kind="ExternalInput"),
            npost=kb.dram(f"npost{j}", [128, OC], F32, kind="ExternalInput")))
    if prologue:
        catT = kb.dram("catT", [DM, NTOK], BF16, kind="ExternalInput")
        wo = kb.dram("wo", [OC, 128, KC, 128], BF16, kind="ExternalInput")
        npo = kb.dram("npo", [128, OC], F32, kind="ExternalInput")
    nstage = nffn + (1 if prologue else 0)
    mids = []
    for i in range(nstage - 1):
        m = kb.dram(f"hmid{i}", [DM, NTOK], F32)
        m.nowaw = True
        mids.append(m)

    ones = kb.sb("ones", [128, 128], BF16)
    kb.op("pool", lambda e: e.memset(ones.h.ap(), 1.0), writes=[ones])
    wpre = kb.sb("wpre", [128, KC], F32)
    wpost = kb.sb("wpost", [128, OC], F32)
    hin = kb.sb("hin", [128, KC, T], F32)
    xT = [kb.sb(f"xT{k}", [128, T], BF16) for k in range(KC)]
    sq = [kb.sb(f"sq{i}", [128, T], BF16) for i in range(2)]
    rstd = kb.sb("rstd", [128, T], F32)
    rstd2 = kb.sb("rstd2", [128, T], F32)
    act = [kb.sb(f"act{f}", [128, T], BF16) for f in range(FC)]
    sg = [kb.sb(f"sg{i}", [128, T], F32) for i in range(2)]
    yb = [kb.sb(f"y{o}", [128, T], F32) for o in range(OC)]
    res = [kb.sb(f"res{i}", [128, T], F32) for i in range(2)]
    pgu = [(kb.ps(f"pg{i}", [128, T]), kb.ps(f"pu{i}", [128, T])) for i in range(2)]
    pd = [kb.ps(f"pd{i}", [128, T]) for i in range(2)]
    pst = [kb.ps(f"pst{i}", [128, T]) for i in range(2)]
    rgt = [kb.sb(f"rg{i}", [128, KC, 128], BF16) for i in range(3)]
    rut = [kb.sb(f"ru{i}", [128, KC, 128], BF16) for i in range(3)]
    rdt = [kb.sb(f"rd{i}", [128, FC, 128], BF16) for i in range(2)]

    def sumsq(src_of, n, pbank, rs):
        for k in range(n):
            st, sap = src_of(k)
            q = sq[k % 2]
            kb.op("act", lambda e, q=q, sap=sap: e.activation(q.h.ap(), sap(), AF.Square), reads=[st], writes=[q])
            kb.op("pe", lambda e, q=q, k=k: e.matmul(pbank.h.ap(), ones.h.ap(), q.h.ap(), start=(k == 0), stop=(k == n - 1)),
                  reads=[ones, q], writes=[pbank])
        kb.op("act", lambda e: e.activation(rs.h.ap(), pbank.h.ap(), AF.Sqrt, bias=float(DM * EPS)), reads=[pbank], writes=[rs])
        kb.op("dve", lambda e: e.reciprocal(rs.h.ap(), rs.h.ap()), reads=[rs], writes=[rs])

    def down_post(t, src, dst, ring, nk, srcs, resw):
        for o in range(OC):
            w = ring.get(t * OC + o)
            p = pd[o % 2]
            for f in range(nk):
                kb.op("pe", lambda e, f=f, w=w, p=p: e.matmul(p.h.ap(), w.h[:, f, :], srcs[f].h.ap(), start=(f == 0), stop=(f == nk - 1)),
                      reads=[w, srcs[f]], writes=[p])
            kb.op("act", lambda e, o=o, p=p: e.activation(yb[o].h.ap(), p.h.ap(), AF.Copy), reads=[p], writes=[yb[o]])
        sumsq(lambda k: (yb[k], lambda k=k: yb[k].h.ap()), OC, pst[1], rstd2)
        for o in range(OC):
            r = res[o % 2]
            kb.dma("sp", r, lambda r=r: r.h.ap(), src, lambda o=o: src.h[o * 128:(o + 1) * 128, t * T:(t + 1) * T])
            kb.op("dve", lambda e, o=o: e.scalar_tensor_tensor(yb[o].h.ap(), yb[o].h.ap(), wpost.h[:, o:o + 1], rstd2.h.ap(),
                                                                ALU.mult, ALU.mult),
                  reads=[yb[o], wpost, rstd2], writes=[yb[o]])
            kb.op("pool", lambda e, o=o, r=r: e.tensor_tensor(r.h.ap(), yb[o].h.ap(), r.h.ap(), ALU.add),
                  reads=[yb[o], r], writes=[r])
            kb.dma("sp", dst, lambda o=o: dst.h[o * 128:(o + 1) * 128, t * T:(t + 1) * T], r, lambda r=r: r.h.ap())

    stage = 0
    cur = hT
    s = float(np.sqrt(DM))
    if prologue:
        dst = mids[0] if nstage > 1 else out
        kb.dma("sp", wpost, lambda: wpost.h.ap(), npo, lambda: npo.h.ap())
        kb.op("dve", lambda e: e.tensor_scalar_mul(wpost.h.ap(), wpost.h.ap(), s), reads=[wpost], writes=[wpost])
        ro = Ring(kb, "ro", None, None, 3, wo, lambda i: wo.h[i % OC], NT * OC, q="sp", tiles=rgt)
        for t in range(NT):
            for k in range(KC):
                kb.dma("sp", xT[k], lambda k=k: xT[k].h.ap(), catT, lambda k=k: catT.h[k * 128:(k + 1) * 128, t * T:(t + 1) * T])
            down_post(t, cur, dst, ro, KC, xT, None)
        cur = dst
        stage = 1

    for j in range(nffn):
        Wj = W[j]
        dst = out if stage == nstage - 1 else mids[stage]
        kb.dma("sp", wpre, lambda: wpre.h.ap(), Wj["npre"], lambda: Wj["npre"].h.ap())
        kb.dma("sp", wpost, lambda: wpost.h.ap(), Wj["npost"], lambda: Wj["npost"].h.ap())
        kb.op("dve", lambda e: e.tensor_scalar_mul(wpre.h.ap(), wpre.h.ap(), s), reads=[wpre], writes=[wpre])
        kb.op("dve", lambda e: e.tensor_scalar_mul(wpost.h.ap(), wpost.h.ap(), 0.5 * s), reads=[wpost], writes=[wpost])
        rg = Ring(kb, "rg", None, None, 3, Wj["wg"], lambda i, Wj=Wj: Wj["wg"].h[i % FC], NT * FC, q="sp", tiles=rgt)
        ru = Ring(kb, "ru", None, None, 3, Wj["wu"], lambda i, Wj=Wj: Wj["wu"].h[i % FC], NT * FC, q="sp", tiles=rut)
        rd = Ring(kb, "rd", None, None, 2, Wj["wd"], lambda i, Wj=Wj: Wj["wd"].h[i % OC], NT * OC, q="sp", tiles=rdt)

        def prenorm(t, cur=cur):
            kb.dma("sp", hin, lambda: hin.h.ap(), cur,
                   lambda: cur.h[:, t * T:(t + 1) * T].rearrange("(kc p) n -> p kc n", p=128))
            sumsq(lambda k: (hin, lambda k=k: hin.h[:, k, :]), KC, pst[0], rstd)
            for k in range(KC):
                kb.op("dve", lambda e, k=k: e.scalar_tensor_tensor(xT[k].h.ap(), hin.h[:, k, :], wpre.h[:, k:k + 1], rstd.h.ap(),
                                                                    ALU.mult, ALU.mult),
                      reads=[hin, wpre, rstd], writes=[xT[k]])

        def gateup(t):
            for f in range(FC):
                i = t * FC + f
                g = rg.get(i)
                u = ru.get(i)
                pg, pu = pgu[f % 2]
                for k in range(KC):
                    kb.op("pe", lambda e, k=k, g=g, pg=pg: e.matmul(pg.h.ap(), g.h[:, k, :], xT[k].h.ap(), start=(k == 0), stop=(k == KC - 1)),
                          reads=[g, xT[k]], writes=[pg])
                for k in range(KC):
                    kb.op("pe", lambda e, k=k, u=u, pu=pu: e.matmul(pu.h.ap(), u.h[:, k, :], xT[k].h.ap(), start=(k == 0), stop=(k == KC - 1)),
                          reads=[u, xT[k]], writes=[pu])
                s_ = sg[f % 2]
                kb.op("act", lambda e, s_=s_, pg=pg: e.activation(s_.h.ap(), pg.h.ap(), AF.Silu), reads=[pg], writes=[s_])
                kb.op("dve", lambda e, s_=s_, pu=pu, f=f: e.tensor_tensor(act[f].h.ap(), s_.h.ap(), pu.h.ap(), ALU.mult),
                      reads=[s_, pu], writes=[act[f]])

        prenorm(0)
        for t in range(NT):
            gateup(t)
            if t + 1 < NT:
                prenorm(t + 1)
            down_post(t, cur, dst, rd, FC, act, None)
        cur = dst
        stage += 1
    kb.finish([out])


def attn_builder(kb, S, NH=4, T=512, DM=2048, SB=2048):
    KC = DM // 128
    NSB = S // SB
    hT = kb.dram("hT", [DM, S], F32, kind="ExternalInput")
    npre = kb.dram("npre", [128, KC], F32, kind="ExternalInput")
    w = kb.dram("w", [NH, 128, KC, 640], BF16, kind="ExternalInput")
    cst = kb.dram("cst", [128, 4, 128], BF16, kind="ExternalInput")
    catT = kb.dram("catT", [NH * 128, S], BF16, kind="ExternalOutput")
    catT.nowaw = True

    cs = kb.sb("cs", [128, 4, 128], BF16)
    kb.dma("sp", cs, lambda: cs.h.ap(), cst, lambda: cst.h.ap())
    wpre = kb.sb("wpre", [128, KC], F32)
    kb.dma("sp", wpre, lambda: wpre.h.ap(), npre, lambda: npre.h.ap())
    kb.op("dve", lambda e: e.tensor_scalar_mul(wpre.h.ap(), wpre.h.ap(), float(np.sqrt(DM))), reads=[wpre], writes=[wpre])
    hin = kb.sb("hin", [128, KC, T], F32)
    xT = [kb.sb(f"xT{k}", [128, T], BF16) for k in range(KC)]
    sq = [kb.sb(f"sq{i}", [128, T], BF16) for i in range(2)]
    rstd = kb.sb("rstd", [128, T], F32)
    wsb = kb.sb("wsb", [128, KC, 640], BF16)
    Q = [kb.sb(f"Q{g}", [128, SB], BF16) for g in range(3)]
    Kb = kb.sb("Kb", [128, 2 * SB], BF16)
    VT = kb.sb("VT", [128, SB], BF16)
    Vs = [kb.sb(f"Vs{g}", [128, 32, 128], BF16) for g in range(3)]
    acc = kb.sb("acc", [128, 2, SB], F32)
    pt = [kb.sb(f"pt{i}", [128, 2, 128], BF16) for i in range(2)]
    ob = kb.sb("ob", [128, SB], BF16)
    pstat = kb.ps("pstat", [128, T])
    pproj = [kb.ps(f"pproj{i}", [128, T]) for i in range(2)]
    psc = [kb.ps(f"psc{i}", [128, 2, 128]) for i in range(2)]
    pol = [kb.ps(f"pol{i}", [128, 2, 128]) for i in range(2)]
    ptr = kb.ps("ptr", [128, 4, 128], BF16)
    DIL = (1, 4, 16)
    SCALE = float(128 ** -0.5)

    def prenorm(t0):
        kb.dma("sp", hin, lambda: hin.h.ap(), hT, lambda: hT.h[:, t0:t0 + T].rearrange("(kc p) n -> p kc n", p=128))
        for k in range(KC):
            q = sq[k % 2]
            kb.op("act", lambda e, q=q, k=k: e.activation(q.h.ap(), hin.h[:, k, :], AF.Square), reads=[hin], writes=[q])
            kb.op("pe", lambda e, q=q, k=k: e.matmul(pstat.h.ap(), cs.h[:, 3, :], q.h.ap(), start=(k == 0), stop=(k == KC - 1)),
                  reads=[cs, q], writes=[pstat])
        kb.op("act", lambda e: e.activation(rstd.h.ap(), pstat.h.ap(), AF.Sqrt, bias=float(DM * EPS)), reads=[pstat], writes=[rstd])
        kb.op("dve", lambda e: e.reciprocal(rstd.h.ap(), rstd.h.ap()), reads=[rstd], writes=[rstd])
        for k in range(KC):
            kb.op("dve", lambda e, k=k: e.scalar_tensor_tensor(xT[k].h.ap(), hin.h[:, k, :], wpre.h[:, k:k + 1], rstd.h.ap(), ALU.mult, ALU.mult),
                  reads=[hin, wpre, rstd], writes=[xT[k]])

    for hh in range(NH):
        kb.dma("sp", wsb, lambda hh=hh: wsb.h.ap(), w, lambda hh=hh: w.h[hh])
        for sbi in range(NSB):
            for tb in range(SB // T):
                t0 = sbi * SB + tb * T
                prenorm(t0)
                for cb in range(5):
                    pp = pproj[cb % 2]
                    for k in range(KC):
                        kb.op("pe", lambda e, k=k, cb=cb, pp=pp: e.matmul(pp.h.ap(), wsb.h[:, k, cb * 128:(cb + 1) * 128], xT[k].h.ap(),
                                                                          start=(k == 0), stop=(k == KC - 1)), reads=[wsb, xT[k]], writes=[pp])
                    if cb < 3:
                        dstt = Q[cb]
                        kb.op("act", lambda e, pp=pp, dstt=dstt, tb=tb: e.activation(dstt.h[:, tb * T:(tb + 1) * T], pp.h.ap(), AF.Copy, scale=SCALE),
                              reads=[pp], writes=[dstt])
                    elif cb == 3:
                        kb.op("act", lambda e, pp=pp, tb=tb: e.activation(Kb.h[:, SB + tb * T:SB + (tb + 1) * T], pp.h.ap(), AF.Copy),
                              reads=[pp], writes=[Kb])
                    else:
                        kb.op("dve", lambda e, pp=pp, tb=tb: e.tensor_copy(VT.h[:, tb * T:(tb + 1) * T], pp.h.ap()), reads=[pp], writes=[VT])
            for g, d in enumerate(DIL):
                for s4 in range(4):
                    for i in range(4):
                        st_ = s4 * 4 + i
                        blk, r = st_ // d, st_ % d
                        a = blk * 128 * d + r
                        kb.op("pe", lambda e, i=i, a=a, d=d: e.transpose(ptr.h[:, i, :], VT.h[:, a:a + 127 * d + 1:d], cs.h[:, 0, :]),
                              reads=[VT, cs], writes=[ptr])
                    kb.op("dve", lambda e, g=g, s4=s4: e.tensor_copy(Vs[g].h[:, 16 + s4 * 4:16 + s4 * 4 + 4, :], ptr.h.ap()),
                          reads=[ptr], writes=[Vs[g]])
            n = 0
            for g, d in enumerate(DIL):
                for st_ in range(16):
                    blk, r = st_ // d, st_ % d
                    a = blk * 128 * d + r
                    has_prev = not (sbi == 0 and blk == 0)
                    sc, ol, p_ = psc[n % 2], pol[n % 2], pt[n % 2]
                    n += 1
                    halves = (0, 1) if has_prev else (1,)
                    for hf in halves:
                        ka = SB + a - (128 * d if hf == 0 else 0)
                        kb.op("pe", lambda e, hf=hf, ka=ka, a=a, d=d, g=g, sc=sc: e.matmul(sc.h[:, hf, :], Kb.h[:, ka:ka + 127 * d + 1:d],
                                                                                          Q[g].h[:, a:a + 127 * d + 1:d], start=True, stop=True),
                              reads=[Kb, Q[g]], writes=[sc])
                    lo = halves[0]
                    kb.op("act", lambda e, sc=sc, p_=p_, lo=lo: e.activation(p_.h[:, lo:2, :], sc.h[:, lo:2, :], AF.Exp), reads=[sc], writes=[p_])
                    kb.op("dve", lambda e, p_=p_, lo=lo: e.tensor_tensor(p_.h[:, lo:2, :], p_.h[:, lo:2, :], cs.h[:, 1 + lo:3, :], ALU.mult),
                          reads=[p_, cs], writes=[p_])
                    for j, hf in enumerate(halves):
                        gs = 16 + st_ - (d if hf == 0 else 0)
                        kb.op("pe", lambda e, hf=hf, gs=gs, g=g, ol=ol, p_=p_, j=j: e.matmul(ol.h[:, 0, :], Vs[g].h[:, gs, :], p_.h[:, hf, :],
                                                                                            start=(j == 0), stop=(j == len(halves) - 1)),
                              reads=[Vs[g], p_], writes=[ol])
                    for j, hf in enumerate(halves):
                        kb.op("pe", lambda e, hf=hf, ol=ol, p_=p_, j=j: e.matmul(ol.h[:, 1, :], cs.h[:, 3, :], p_.h[:, hf, :],
                                                                                  start=(j == 0), stop=(j == len(halves) - 1)),
                              reads=[cs, p_], writes=[ol])
                    if g == 0:
                        kb.op("act", lambda e, ol=ol, a=a, d=d: e.activation(acc.h[:, :, a:a + 127 * d + 1:d], ol.h.ap(), AF.Copy), reads=[ol], writes=[acc])
                    else:
                        kb.op("dve", lambda e, ol=ol, a=a, d=d: e.tensor_tensor(acc.h[:, :, a:a + 127 * d + 1:d], acc.h[:, :, a:a + 127 * d + 1:d], ol.h.ap(), ALU.add),
                              reads=[ol, acc], writes=[acc])
            kb.op("dve", lambda e: e.reciprocal(acc.h[:, 1, :], acc.h[:, 1, :]), reads=[acc], writes=[acc])
            kb.op("dve", lambda e: e.tensor_tensor(ob.h.ap(), acc.h[:, 0, :], acc.h[:, 1, :], ALU.mult), reads=[acc], writes=[ob])
            kb.dma("sp", catT, lambda hh=hh, sbi=sbi: catT.h[hh * 128:(hh + 1) * 128, sbi * SB:(sbi + 1) * SB], ob, lambda: ob.h.ap())
            if sbi + 1 < NSB:
                kb.op("pool", lambda e: e.tensor_copy(Kb.h[:, 0:SB], Kb.h[:, SB:2 * SB]), reads=[Kb], writes=[Kb])
                for g in range(3):
                    kb.op("pool", lambda e, g=g: e.tensor_copy(Vs[g].h[:, 0:16, :], Vs[g].h[:, 16:32, :]), reads=[Vs[g]], writes=[Vs[g]])
    kb.finish([catT])


def attn_consts():
    c = np.zeros((128, 4, 128), np.float32)
    k = np.arange(128)[:, None]
    q = np.arange(128)[None, :]
    c[:, 0, :] = np.eye(128)
    c[:, 1, :] = (k >= q)
    c[:, 2, :] = (k <= q)
    c[:, 3, :] = 1.0
    return c.astype(NPBF)


def attn_wslab(att_w_in_bf, heads):
    out = []
    for h in heads:
        cols = [att_w_in_bf[:, g * 2048 + h * 128: g * 2048 + (h + 1) * 128] for g in range(3)]
        cols.append(att_w_in_bf[:, 6144 + h * 128: 6144 + (h + 1) * 128])
        cols.append(att_w_in_bf[:, 8192 + h * 128: 8192 + (h + 1) * 128])
        wcat = np.concatenate(cols, axis=1)
        out.append(wcat.reshape(16, 128, 640).transpose(1, 0, 2))
    return np.ascontiguousarray(np.stack(out))


AB_NCOL = 2308


def ab_builder(kb, S, T=256, DM=2048, C=128, do_hgrn=True, do_ssd=True):
    KC = DM // 128
    NB = S // T
    NCH = T // C
    hT = kb.dram("hT", [DM, S], F32, kind="ExternalInput")
    npre = kb.dram("npre", [128, KC], F32, kind="ExternalInput")
    w = kb.dram("w", [128, KC, AB_NCOL], BF16, kind="ExternalInput")
    cst = kb.dram("cst", [128, 4, 128], BF16, kind="ExternalInput")
    prm = kb.dram("prm", [128, 24], F32, kind="ExternalInput")
    cw = kb.dram("cw", [128, 6, 5], F32, kind="ExternalInput")
    dtbrow = kb.dram("dtbrow", [128, 4], F32, kind="ExternalInput")
    catT = kb.dram("catT", [512, S], BF16, kind="ExternalOutput")
    catT.nowaw = True

    cs = kb.sb("cs", [128, 4, 128], BF16)
    kb.dma("sp", cs, lambda: cs.h.ap(), cst, lambda: cst.h.ap())
    P = kb.sb("P", [128, 24], F32)
    kb.dma("sp", P, lambda: P.h.ap(), prm, lambda: prm.h.ap())
    CW = kb.sb("CW", [128, 6, 5], F32)
    kb.dma("sp", CW, lambda: CW.h.ap(), cw, lambda: cw.h.ap())
    DTB = kb.sb("DTB", [128, 4], F32)
    kb.dma("sp", DTB, lambda: DTB.h.ap(), dtbrow, lambda: dtbrow.h.ap())
    wpre = kb.sb("wpre", [128, KC], F32)
    kb.dma("sp", wpre, lambda: wpre.h.ap(), npre, lambda: npre.h.ap())
    kb.op("dve", lambda e: e.tensor_scalar_mul(wpre.h.ap(), wpre.h.ap(), float(np.sqrt(DM))), reads=[wpre], writes=[wpre])
    wsb = kb.sb("wsb", [128, KC, AB_NCOL], BF16)
    kb.dma("sp", wsb, lambda: wsb.h.ap(), w, lambda: w.h.ap())
    onesf = kb.sb("onesf", [128, C], F32)
    kb.op("pool", lambda e: e.memset(onesf.h.ap(), 1.0), writes=[onesf])
    LB = kb.sb("LB", [128, 4], F32)
    NEGA = kb.sb("NEGA", [128, 4], F32)
    for hd in range(2):
        kb.op("dve", lambda e, hd=hd: e.tensor_tensor(LB.h[:, 2 * hd:2 * hd + 1], P.h[:, 2 * hd:2 * hd + 1], P.h[:, 2 * hd + 1:2 * hd + 2], ALU.subtract),
              reads=[P], writes=[LB])
        kb.op("act", lambda e, hd=hd: e.activation(LB.h[:, 2 * hd:2 * hd + 1], LB.h[:, 2 * hd:2 * hd + 1], AF.Sigmoid), reads=[LB], writes=[LB])
        kb.op("dve", lambda e, hd=hd: e.tensor_scalar(LB.h[:, 2 * hd + 1:2 * hd + 2], LB.h[:, 2 * hd:2 * hd + 1], -1.0, 1.0, ALU.mult, ALU.add),
              reads=[LB], writes=[LB])
    kb.op("act", lambda e: e.activation(NEGA.h.ap(), P.h[:, 6:10], AF.Exp), reads=[P], writes=[NEGA])
    kb.op("dve", lambda e: e.tensor_scalar_mul(NEGA.h.ap(), NEGA.h.ap(), -1.0), reads=[NEGA], writes=[NEGA])

    hin = kb.sb("hin", [128, KC, T], F32)
    xT = [kb.sb(f"xT{k}", [128, T], BF16) for k in range(KC)]
    sq = [kb.sb(f"sq{i}", [128, T], BF16) for i in range(2)]
    rstd = kb.sb("rstd", [128, T], F32)
    pstat = kb.ps("pstat", [128, T])
    pproj = [kb.ps(f"pproj{i}", [128, T]) for i in range(2)]
    pA = kb.ps("pA", [128, 128])
    pO = kb.ps("pO", [128, 128])
    pS = kb.ps("pS", [128, 128])
    pT_ = kb.ps("pT", [128, 128], BF16)
    pN = kb.ps("pN", [128, 128])

    def prenorm(t0):
        kb.dma("sp", hin, lambda: hin.h.ap(), hT, lambda: hT.h[:, t0:t0 + T].rearrange("(kc p) n -> p kc n", p=128))
        for k in range(KC):
            q = sq[k % 2]
            kb.op("act", lambda e, q=q, k=k: e.activation(q.h.ap(), hin.h[:, k, :], AF.Square), reads=[hin], writes=[q])
            kb.op("pe", lambda e, q=q, k=k: e.matmul(pstat.h.ap(), cs.h[:, 3, :], q.h.ap(), start=(k == 0), stop=(k == KC - 1)),
                  reads=[cs, q], writes=[pstat])
        kb.op("act", lambda e: e.activation(rstd.h.ap(), pstat.h.ap(), AF.Sqrt, bias=float(DM * EPS)), reads=[pstat], writes=[rstd])
        kb.op("dve", lambda e: e.reciprocal(rstd.h.ap(), rstd.h.ap()), reads=[rstd], writes=[rstd])
        for k in range(KC):
            kb.op("dve", lambda e, k=k: e.scalar_tensor_tensor(xT[k].h.ap(), hin.h[:, k, :], wpre.h[:, k:k + 1], rstd.h.ap(), ALU.mult, ALU.mult),
                  reads=[hin, wpre, rstd], writes=[xT[k]])

    npj = [0]

    def proj_fm(col, M, evac):
        pp = pproj[npj[0] % 2]
        npj[0] += 1
        for k in range(KC):
            kb.op("pe", lambda e, k=k, pp=pp: e.matmul(pp.h[0:M, :], wsb.h[:, k, col:col + M], xT[k].h.ap(), start=(k == 0), stop=(k == KC - 1)),
                  reads=[wsb, xT[k]], writes=[pp])
        evac(pp)

    def proj_tm(col, N, c, evac):
        pp = pproj[npj[0] % 2]
        npj[0] += 1
        for k in range(KC):
            kb.op("pe", lambda e, k=k, pp=pp: e.matmul(pp.h[:, 0:N], xT[k].h[:, c * C:(c + 1) * C], wsb.h[:, k, col:col + N], start=(k == 0), stop=(k == KC - 1)),
                  reads=[wsb, xT[k]], writes=[pp])
        evac(pp)

    bt = kb.sb("g_b", [128, C], F32)
    nbm = kb.sb("g_nbm", [128, 1], F32)
    eq = kb.sb("g_eq", [128, C], F32)
    ek = kb.sb("g_ek", [128, C], F32)
    ei = kb.sb("g_ei", [128, C], F32)
    es = kb.sb("g_es", [128, C], F32)
    qs = kb.sb("g_qs", [128, C], BF16)
    ks = kb.sb("g_ks", [128, C], BF16)
    qi = kb.sb("g_qi", [128, C], BF16)
    kT = kb.sb("g_kT", [128, C], BF16)
    kst = kb.sb("g_kst", [128, 128], BF16)
    pTs = kb.sb("g_pT", [128, 128], BF16)

    acol = kb.sb("g_acol", [128, 1], F32)
    dl = kb.sb("g_dl", [128, C], F32)
    scS = kb.sb("g_scS", [128, C], F32)
    qbf = kb.sb("g_qbf", [128, C], BF16)
    kbf = kb.sb("g_kbf", [128, C], BF16)

    def gla(qt, qap, kt, kap, ldt, ldap, V, dv, Sf, Sb, scalar_decay=False):
        kb.op("dve", lambda e: e.tensor_tensor_scan(bt.h.ap(), onesf.h.ap(), ldap(), 0.0, ALU.mult, ALU.add), reads=[onesf, ldt], writes=[bt])
        kb.op("dve", lambda e: e.tensor_scalar_mul(nbm.h.ap(), bt.h[:, C // 2:C // 2 + 1], -1.0), reads=[bt], writes=[nbm])
        kb.op("act", lambda e: e.activation(ei.h.ap(), bt.h.ap(), AF.Exp), reads=[bt], writes=[ei])
        kb.op("act", lambda e: e.activation(es.h.ap(), bt.h.ap(), AF.Exp, bias=bt.h[:, C - 1:C], scale=-1.0), reads=[bt], writes=[es])
        kb.op("dve", lambda e: e.tensor_tensor(qi.h.ap(), qap(), ei.h.ap(), ALU.mult), reads=[qt, ei], writes=[qi])
        kb.op("pool", lambda e: e.tensor_tensor(kT.h.ap(), kap(), es.h.ap(), ALU.mult), reads=[kt, es], writes=[kT])
        kb.op("pe", lambda e: e.transpose(pT_.h.ap(), kT.h.ap(), cs.h[:, 0, :]), reads=[kT, cs], writes=[pT_])
        kb.op("act", lambda e: e.activation(kst.h.ap(), pT_.h.ap(), AF.Copy), reads=[pT_], writes=[kst])
        if not scalar_decay:
            kb.op("act", lambda e: e.activation(eq.h.ap(), bt.h.ap(), AF.Exp, bias=nbm.h.ap()), reads=[bt, nbm], writes=[eq])
            kb.op("act", lambda e: e.activation(ek.h.ap(), bt.h.ap(), AF.Exp, bias=bt.h[:, C // 2:C // 2 + 1], scale=-1.0), reads=[bt], writes=[ek])
            kb.op("dve", lambda e: e.tensor_tensor(qs.h.ap(), qap(), eq.h.ap(), ALU.mult), reads=[qt, eq], writes=[qs])
            kb.op("pool", lambda e: e.tensor_tensor(ks.h.ap(), kap(), ek.h.ap(), ALU.mult), reads=[kt, ek], writes=[ks])
            kb.op("pe", lambda e: e.matmul(pA.h.ap(), ks.h.ap(), qs.h.ap(), start=True, stop=True), reads=[ks, qs], writes=[pA])
            kb.op("dve", lambda e: e.tensor_tensor(pTs.h.ap(), pA.h.ap(), cs.h[:, 2, :], ALU.mult), reads=[pA, cs], writes=[pTs])
        else:
            kb.op("dve", lambda e: e.tensor_tensor(dl.h.ap(), bt.h.ap(), cs.h[:, 0, :], ALU.mult), reads=[bt, cs], writes=[dl])
            kb.op("dve", lambda e: e.reduce_sum(acol.h.ap(), dl.h.ap(), mybir.AxisListType.X), reads=[dl], writes=[acol])
            kb.op("dve", lambda e: e.tensor_scalar(dl.h.ap(), bt.h.ap(), acol.h.ap(), 0.0, ALU.subtract, ALU.min), reads=[bt, acol], writes=[dl])
            kb.op("act", lambda e: e.activation(dl.h.ap(), dl.h.ap(), AF.Exp), reads=[dl], writes=[dl])
            kb.op("pool", lambda e: e.tensor_tensor(dl.h.ap(), dl.h.ap(), cs.h[:, 2, :], ALU.mult), reads=[dl, cs], writes=[dl])
            kb.op("dve", lambda e: e.tensor_tensor(pTs.h.ap(), scS.h.ap(), dl.h.ap(), ALU.mult), reads=[scS, dl], writes=[pTs])
        kb.op("pe", lambda e: e.matmul(pO.h[0:dv, :], V.h[:, 0:dv], pTs.h.ap(), start=True, stop=False), reads=[V, pTs], writes=[pO])
        kb.op("pe", lambda e: e.matmul(pO.h[0:dv, :], Sb.h[:, 0:dv], qi.h.ap(), start=False, stop=True), reads=[Sb, qi], writes=[pO])
        kb.op("pe", lambda e: e.matmul(pS.h[:, 0:dv], kst.h.ap(), V.h[:, 0:dv], start=True, stop=True), reads=[kst, V], writes=[pS])
        kb.op("dve", lambda e: e.scalar_tensor_tensor(Sf.h[:, 0:dv], Sf.h[:, 0:dv], ei.h[:, C - 1:C], pS.h[:, 0:dv], ALU.mult, ALU.add),
              reads=[Sf, ei, pS], writes=[Sf])
        kb.op("act", lambda e: e.activation(Sb.h[:, 0:dv], Sf.h[:, 0:dv], AF.Copy), reads=[Sf], writes=[Sb])

    if do_hgrn:
        hq = kb.sb("hq", [128, T], F32)
        hk = kb.sb("hk", [128, T], F32)
        hl = kb.sb("hl", [128, T], F32)
        hg = kb.sb("hg", [128, T], F32)
        hV = kb.sb("hV", [128, 128], BF16)
        hsq = kb.sb("hsq", [128, C], BF16)
        hrs = kb.sb("hrs", [128, C], F32)
        ho = kb.sb("ho", [128, C], F32)
        hob = [kb.sb(f"hob{i}", [128, T], BF16) for i in range(2)]
        HSf = [kb.sb(f"HSf{i}", [128, 128], F32) for i in range(2)]
        HSb = [kb.sb(f"HSb{i}", [128, 128], BF16) for i in range(2)]
        for i in range(2):
            kb.op("pool", lambda e, i=i: e.memset(HSf[i].h.ap(), 0.0), writes=[HSf[i]])
            kb.op("pool", lambda e, i=i: e.memset(HSb[i].h.ap(), 0.0), writes=[HSb[i]])

    if do_ssd:
        raw = [kb.sb(f"raw{i}", [128, 3 + T], F32) for i in range(6)]
        cv = [kb.sb(f"cv{i}", [128, T], F32) for i in range(6)]
        zs = [kb.sb(f"zs{i}", [64, T], F32) for i in range(4)]
        ld = [kb.sb(f"ld{i}", [128, T], F32) for i in range(4)]
        dtt = kb.sb("dtt", [128, 4], F32)
        sV = kb.sb("sV", [128, 64], BF16)
        syz = [kb.sb(f"syz{i}", [64, C], F32) for i in range(4)]
        ssq = kb.sb("ssq", [64, C], BF16)
        srs = kb.sb("srs", [64, C], F32)
        sob = [kb.sb(f"sob{i}", [64, C], BF16) for i in range(4)]
        SSf = [kb.sb(f"SSf{i}", [128, 64], F32) for i in range(4)]
        SSb = [kb.sb(f"SSb{i}", [128, 64], BF16) for i in range(4)]
        for i in range(4):
            kb.op("pool", lambda e, i=i: e.memset(SSf[i].h.ap(), 0.0), writes=[SSf[i]])
            kb.op("pool", lambda e, i=i: e.memset(SSb[i].h.ap(), 0.0), writes=[SSb[i]])
        for i in range(6):
            kb.op("pool", lambda e, i=i: e.memset(raw[i].h[:, 0:3], 0.0), writes=[raw[i]])

    for blk in range(NB):
        t0 = blk * T
        prenorm(t0)
        if do_hgrn:
            for hd in range(2):
                base = hd * 512
                proj_fm(base, 128, lambda pp: kb.op("act", lambda e: e.activation(hq.h.ap(), pp.h.ap(), AF.Copy), reads=[pp], writes=[hq]))
                proj_fm(base + 128, 128, lambda pp: kb.op("act", lambda e: e.activation(hk.h.ap(), pp.h.ap(), AF.Sigmoid), reads=[pp], writes=[hk]))
                kb.op("dve", lambda e, hd=hd: e.tensor_scalar(hk.h.ap(), hk.h.ap(), LB.h[:, 2 * hd + 1:2 * hd + 2], LB.h[:, 2 * hd:2 * hd + 1], ALU.mult, ALU.add),
                      reads=[hk, LB], writes=[hk])
                kb.op("act", lambda e: e.activation(hl.h.ap(), hk.h.ap(), AF.Ln), reads=[hk], writes=[hl])
                kb.op("dve", lambda e: e.tensor_scalar(hk.h.ap(), hk.h.ap(), -1.0, 1.0, ALU.mult, ALU.add), reads=[hk], writes=[hk])
                proj_fm(base + 256, 128, lambda pp: kb.op("act", lambda e: e.activation(hg.h.ap(), pp.h.ap(), AF.Silu), reads=[pp], writes=[hg]))
                ob_ = hob[hd]
                for c in range(NCH):
                    cl = slice(c * C, (c + 1) * C)
                    proj_tm(base + 384, 128, c, lambda pp: kb.op("dve", lambda e: e.tensor_copy(hV.h.ap(), pp.h[:, 0:128]), reads=[pp], writes=[hV]))
                    gla(hq, lambda cl=cl: hq.h[:, cl], hk, lambda cl=cl: hk.h[:, cl], hl, lambda cl=cl: hl.h[:, cl], hV, 128, HSf[hd], HSb[hd])
                    kb.op("act", lambda e: e.activation(hsq.h.ap(), pO.h.ap(), AF.Square), reads=[pO], writes=[hsq])
                    kb.op("pe", lambda e: e.matmul(pN.h.ap(), cs.h[:, 3, :], hsq.h.ap(), start=True, stop=True), reads=[cs, hsq], writes=[pN])
                    kb.op("act", lambda e: e.activation(hrs.h.ap(), pN.h.ap(), AF.Sqrt, bias=float(EPS), scale=1.0 / 128), reads=[pN], writes=[hrs])
                    kb.op("dve", lambda e: e.reciprocal(hrs.h.ap(), hrs.h.ap()), reads=[hrs], writes=[hrs])
                    kb.op("dve", lambda e: e.tensor_tensor(ho.h.ap(), pO.h.ap(), hrs.h.ap(), ALU.mult), reads=[pO, hrs], writes=[ho])
                    kb.op("dve", lambda e, hd=hd, cl=cl, ob_=ob_: e.scalar_tensor_tensor(ob_.h[:, cl], ho.h.ap(), P.h[:, 4 + hd:5 + hd], hg.h[:, cl], ALU.mult, ALU.mult),
                          reads=[ho, P, hg], writes=[ob_])
                kb.dma("sp", catT, lambda hd=hd, t0=t0: catT.h[hd * 128:(hd + 1) * 128, t0:t0 + T], ob_, lambda ob_=ob_: ob_.h.ap())
        if do_ssd:
            for r in range(4):
                proj_fm(1024 + r * 64, 64, lambda pp, r=r: kb.op("act", lambda e: e.activation(raw[r].h[0:64, 3:3 + T], pp.h[0:64, :], AF.Copy), reads=[pp], writes=[raw[r]]))
                proj_fm(1280 + r * 64, 64, lambda pp, r=r: kb.op("act", lambda e: e.activation(zs[r].h.ap(), pp.h[0:64, :], AF.Silu), reads=[pp], writes=[zs[r]]))
            proj_fm(1536, 128, lambda pp: kb.op("act", lambda e: e.activation(raw[4].h[:, 3:3 + T], pp.h.ap(), AF.Copy), reads=[pp], writes=[raw[4]]))
            proj_fm(1664, 128, lambda pp: kb.op("act", lambda e: e.activation(raw[5].h[:, 3:3 + T], pp.h.ap(), AF.Copy), reads=[pp], writes=[raw[5]]))
            for i in range(6):
                np_ = 64 if i < 4 else 128
                kb.op("dve", lambda e, i=i, np_=np_: e.tensor_scalar(cv[i].h[0:np_, :], raw[i].h[0:np_, 0:T], CW.h[0:np_, i, 0:1], CW.h[0:np_, i, 4:5], ALU.mult, ALU.add),
                      reads=[raw[i], CW], writes=[cv[i]])
                for j in range(1, 4):
                    kb.op("dve", lambda e, i=i, j=j, np_=np_: e.scalar_tensor_tensor(cv[i].h[0:np_, :], raw[i].h[0:np_, j:j + T], CW.h[0:np_, i, j:j + 1], cv[i].h[0:np_, :], ALU.mult, ALU.add),
                          reads=[raw[i], CW, cv[i]], writes=[cv[i]])
                kb.op("act", lambda e, i=i, np_=np_: e.activation(cv[i].h[0:np_, :], cv[i].h[0:np_, :], AF.Silu), reads=[cv[i]], writes=[cv[i]])
                kb.op("pool", lambda e, i=i, np_=np_: e.tensor_copy(raw[i].h[0:np_, 0:3], raw[i].h[0:np_, T:T + 3]), reads=[raw[i]], writes=[raw[i]])
            for r in range(4):
                proj_fm(1792 + r * 128, 128, lambda pp, r=r: kb.op("act", lambda e: e.activation(ld[r].h.ap(), pp.h.ap(), AF.Exp, bias=P.h[:, 10 + r:11 + r]), reads=[pp, P], writes=[ld[r]]))
                kb.op("act", lambda e, r=r: e.activation(ld[r].h.ap(), ld[r].h.ap(), AF.Ln, bias=1.0), reads=[ld[r]], writes=[ld[r]])
                kb.op("dve", lambda e, r=r: e.tensor_scalar_mul(ld[r].h.ap(), ld[r].h.ap(), NEGA.h[:, r:r + 1]), reads=[ld[r], NEGA], writes=[ld[r]])
            for c in range(NCH):
                cl = slice(c * C, (c + 1) * C)
                proj_tm(2304, 4, c, lambda pp: kb.op("dve", lambda e: e.tensor_tensor(dtt.h.ap(), pp.h[:, 0:4], DTB.h.ap(), ALU.add), reads=[pp, DTB], writes=[dtt]))
                kb.op("act", lambda e: e.activation(dtt.h.ap(), dtt.h.ap(), AF.Exp), reads=[dtt], writes=[dtt])
                kb.op("act", lambda e: e.activation(dtt.h.ap(), dtt.h.ap(), AF.Ln, bias=1.0), reads=[dtt], writes=[dtt])
                kb.op("dve", lambda e, cl=cl: e.tensor_copy(kbf.h.ap(), cv[4].h[:, cl]), reads=[cv[4]], writes=[kbf])
                kb.op("pool", lambda e, cl=cl: e.tensor_copy(qbf.h.ap(), cv[5].h[:, cl]), reads=[cv[5]], writes=[qbf])
                kb.op("pe", lambda e: e.matmul(pA.h.ap(), kbf.h.ap(), qbf.h.ap(), start=True, stop=True), reads=[kbf, qbf], writes=[pA])
                kb.op("act", lambda e: e.activation(scS.h.ap(), pA.h.ap(), AF.Copy), reads=[pA], writes=[scS])
                for r in range(4):
                    kb.op("dve", lambda e, r=r, cl=cl: e.tensor_copy(kT.h[0:64, :], cv[r].h[0:64, cl]), reads=[cv[r]], writes=[kT])
                    kb.op("pe", lambda e: e.transpose(pT_.h[:, 0:64], kT.h[0:64, :], cs.h[0:64, 0, 0:64]), reads=[kT, cs], writes=[pT_])
                    kb.op("dve", lambda e, r=r: e.tensor_scalar_mul(sV.h.ap(), pT_.h[:, 0:64], dtt.h[:, r:r + 1]), reads=[pT_, dtt], writes=[sV])
                    gla(cv[5], lambda cl=cl: cv[5].h[:, cl], cv[4], lambda cl=cl: cv[4].h[:, cl], ld[r], lambda r=r, cl=cl: ld[r].h[:, cl], sV, 64, SSf[r], SSb[r], scalar_decay=True)
                    kb.op("dve", lambda e, r=r, cl=cl: e.scalar_tensor_tensor(syz[r].h.ap(), cv[r].h[0:64, cl], P.h[0:64, 14 + r:15 + r], pO.h[0:64, :], ALU.mult, ALU.add),
                          reads=[cv[r], P, pO], writes=[syz[r]])
                    kb.op("dve", lambda e, r=r, cl=cl: e.tensor_tensor(syz[r].h.ap(), syz[r].h.ap(), zs[r].h[:, cl], ALU.mult), reads=[syz[r], zs[r]], writes=[syz[r]])
                    kb.op("act", lambda e, r=r: e.activation(ssq.h.ap(), syz[r].h.ap(), AF.Square), reads=[syz[r]], writes=[ssq])
                    kb.op("pe", lambda e, r=r: e.matmul(pN.h[0:64, :], cs.h[0:64, 3, 0:64], ssq.h.ap(), start=(r == 0), stop=(r == 3)), reads=[cs, ssq], writes=[pN])
                kb.op("act", lambda e: e.activation(srs.h.ap(), pN.h[0:64, :], AF.Sqrt, bias=float(EPS), scale=1.0 / 256), reads=[pN], writes=[srs])
                kb.op("dve", lambda e: e.reciprocal(srs.h.ap(), srs.h.ap()), reads=[srs], writes=[srs])
                for r in range(4):
                    kb.op("dve", lambda e, r=r: e.scalar_tensor_tensor(sob[r].h.ap(), syz[r].h.ap(), P.h[0:64, 18 + r:19 + r], srs.h.ap(), ALU.mult, ALU.mult),
                          reads=[syz[r], P, srs], writes=[sob[r]])
                    kb.dma("sp", catT, lambda r=r, c=c, t0=t0: catT.h[256 + r * 64:256 + (r + 1) * 64, t0 + c * C:t0 + (c + 1) * C], sob[r], lambda r=r: sob[r].h.ap())
    kb.finish([catT])


def ab_inputs(inp_bf_w_in, z, j):
    W = inp_bf_w_in
    cols = []
    for hd in (2 * j, 2 * j + 1):
        for off in (0, 1024, 3072, 2048):
            cols.append(W[:, off + hd * 128: off + (hd + 1) * 128])
    for r in range(4):
        cols.append(W[:, 5120 + j * 256 + r * 64: 5120 + j * 256 + (r + 1) * 64])
    for r in range(4):
        cols.append(W[:, 4096 + j * 256 + r * 64: 4096 + j * 256 + (r + 1) * 64])
    cols.append(W[:, 6144 + j * 128: 6144 + (j + 1) * 128])
    cols.append(W[:, 6656 + j * 128: 6656 + (j + 1) * 128])
    for r in range(4):
        cols.append(np.repeat(W[:, 7168 + 4 * j + r: 7168 + 4 * j + r + 1], 128, axis=1))
    cols.append(W[:, 7168 + 4 * j: 7168 + 4 * j + 4])
    wc = np.concatenate(cols, axis=1)
    assert wc.shape[1] == AB_NCOL
    wt = np.ascontiguousarray(wc.reshape(16, 128, AB_NCOL).transpose(1, 0, 2))
    prm = np.zeros((128, 24), np.float32)
    for i, hd in enumerate((2 * j, 2 * j + 1)):
        prm[:, 2 * i] = z["hgrn_lb"][0, hd * 128:(hd + 1) * 128]
        prm[:, 2 * i + 1] = z["hgrn_lb"][1, hd * 128:(hd + 1) * 128]
        prm[:, 4 + i] = z["hgrn_norm_w"][0, hd * 128:(hd + 1) * 128]
    for r in range(4):
        hh = 4 * j + r
        prm[:, 6 + r] = z["ssm_A_log"][0, hh]
        prm[:, 10 + r] = z["ssm_dt_bias"][0, hh]
        prm[:, 14 + r] = z["ssm_D"][0, hh]
        prm[0:64, 18 + r] = z["ssm_norm_w"][0, j * 256 + r * 64: j * 256 + (r + 1) * 64]
    cw = np.zeros((128, 6, 5), np.float32)
    cwt, cb = z["ssm_conv_w"][0], z["ssm_conv_b"][0]
    for r in range(4):
        ch = slice(j * 256 + r * 64, j * 256 + (r + 1) * 64)
        cw[0:64, r, 0:4] = cwt[:, ch].T
        cw[0:64, r, 4] = cb[ch]
    for i, off in ((4, 1024), (5, 1536)):
        ch = slice(off + j * 128, off + (j + 1) * 128)
        cw[:, i, 0:4] = cwt[:, ch].T
        cw[:, i, 4] = cb[ch]
    dtb = np.tile(z["ssm_dt_bias"][0, 4 * j:4 * j + 4][None, :], (128, 1)).astype(np.float32)
    return {"w": wt, "prm": prm, "cw": cw, "dtbrow": dtb}


def cast_builder(kb, M, CH=4096):
    x = kb.dram("x", [128, M], F32, kind="ExternalInput")
    y = kb.dram("y", [128, M], BF16, kind="ExternalOutput")
    y.nowaw = True
    tin = [kb.sb(f"ci{i}", [128, CH], F32) for i in range(3)]
    tout = [kb.sb(f"co{i}", [128, CH], BF16) for i in range(3)]
    n = M // CH
    for i in range(n):
        a, b = tin[i % 3], tout[i % 3]
        kb.dma("sp", a, lambda a=a: a.h.ap(), x, lambda i=i: x.h[:, i * CH:(i + 1) * CH])
        if i % 2 == 0:
            kb.op("act", lambda e, a=a, b=b: e.activation(b.h.ap(), a.h.ap(), AF.Copy), reads=[a], writes=[b])
        else:
            kb.op("dve", lambda e, a=a, b=b: e.tensor_copy(b.h.ap(), a.h.ap()), reads=[a], writes=[b])
        kb.dma("sp", y, lambda i=i: y.h[:, i * CH:(i + 1) * CH], b, lambda b=b: b.h.ap())
    kb.finish([y])


def _tile_w(w):
    K, N = w.shape
    return np.ascontiguousarray(w.reshape(K // 128, 128, N // 128, 128).transpose(2, 1, 0, 3))


def _pp(v):
    return np.ascontiguousarray(np.asarray(v, np.float32).reshape(16, 128).T)


NCORES = 8
_CORES = list(range(NCORES))


def _run(nc, maps):
    return run_bass_kernel_spmd(nc, maps, core_ids=_CORES).results


def kernel(x, norm_pre, norm_post, ffn_w_gate, ffn_w_up, ffn_w_down, ab_w_in, ab_w_out,
           hgrn_lb, hgrn_norm_w, ssm_conv_w, ssm_conv_b, ssm_dt_bias, ssm_A_log, ssm_D,
           ssm_norm_w, att_w_in, att_w_out):
    f32 = lambda a: np.asarray(a, np.float32)
    x = f32(x)
    norm_pre, norm_post = f32(norm_pre), f32(norm_post)
    B, S, D = x.shape
    NTOK = B * S // NCORES
    QC = NCORES // B
    srcs = []
    for l in range(2):
        for j in range(2):
            for w in (ffn_w_gate[l, j], ffn_w_up[l, j], ffn_w_down[l, j]):
                srcs.append(_tile_w(f32(w)))
    srcs += [f32(ab_w_in[0]), _tile_w(f32(ab_w_out[0])), f32(att_w_in[0]), _tile_w(f32(att_w_out[0]))]
    shapes = [t.shape for t in srcs]
    flat = np.concatenate([t.reshape(-1) for t in srcs])
    CH = 4096
    per = NCORES * 128 * CH
    tot = -(-flat.size // per) * per
    flat = np.concatenate([flat, np.zeros(tot - flat.size, np.float32)])
    M = tot // (NCORES * 128)
    nc0, _ = build(cast_builder, M, CH)
    parts = flat.reshape(NCORES, 128, M)
    r0 = _run(nc0, [{"x": parts[c]} for c in range(NCORES)])
    fb = np.concatenate([np.asarray(r0[c]["y"]).reshape(-1) for c in range(NCORES)])
    del flat, parts, srcs
    wts, off = [], 0
    for shp in shapes:
        n = int(np.prod(shp))
        wts.append(fb[off:off + n].reshape(shp))
        off += n
    ffw = wts[:12]
    ab_in_bf, ab_out_t, att_in_bf, att_out_t = wts[12:16]
    small = {"hgrn_lb": f32(hgrn_lb), "hgrn_norm_w": f32(hgrn_norm_w), "ssm_conv_w": f32(ssm_conv_w),
             "ssm_conv_b": f32(ssm_conv_b), "ssm_dt_bias": f32(ssm_dt_bias), "ssm_A_log": f32(ssm_A_log),
             "ssm_D": f32(ssm_D), "ssm_norm_w": f32(ssm_norm_w)}

    def ffn_map(k, l, slot, idx):
        return {f"wg{idx}": ffw[3 * k], f"wu{idx}": ffw[3 * k + 1], f"wd{idx}": ffw[3 * k + 2],
                f"npre{idx}": _pp(norm_pre[l, slot]), f"npost{idx}": _pp(norm_post[l, slot])}

    def full_seq(hs):
        return [np.ascontiguousarray(np.concatenate(hs[b * QC:(b + 1) * QC], axis=1)) for b in range(B)]

    def tok_shards(cat_b):
        return [np.ascontiguousarray(cat_b[c // QC][:, (c % QC) * NTOK:((c % QC) + 1) * NTOK]) for c in range(NCORES)]

    h = x.reshape(B * S, D)
    hT = [np.ascontiguousarray(h[c * NTOK:(c + 1) * NTOK].T) for c in range(NCORES)]
    consts = attn_consts()
    nc1, _ = build(tok_builder, NTOK, nffn=1, prologue=False)
    com = ffn_map(0, 0, 0, 0)
    r = _run(nc1, [dict(com, hT=hT[c]) for c in range(NCORES)])
    hT = [np.asarray(r[c]["out"]) for c in range(NCORES)]
    nc2, _ = build(ab_builder, S)
    hb = full_seq(hT)
    maps = []
    for c in range(NCORES):
        b, j = c // QC, c % QC
        m = ab_inputs(ab_in_bf, small, j)
        m.update({"hT": hb[b], "npre": _pp(norm_pre[0, 1]), "cst": consts})
        maps.append(m)
    r = _run(nc2, maps)
    del hb, maps
    cat_b = []
    for b in range(B):
        cat = np.empty((D, S), NPBF)
        for j in range(QC):
            o = np.asarray(r[b * QC + j]["catT"])
            cat[j * 256:(j + 1) * 256] = o[0:256]
            cat[1024 + j * 256:1024 + (j + 1) * 256] = o[256:512]
        cat_b.append(cat)
    nc3, _ = build(tok_builder, NTOK, nffn=2, prologue=True)
    com = {"wo": ab_out_t, "npo": _pp(norm_post[0, 1])}
    com.update(ffn_map(1, 0, 2, 0))
    com.update(ffn_map(2, 1, 0, 1))
    cs_ = tok_shards(cat_b)
    r = _run(nc3, [dict(com, hT=hT[c], catT=cs_[c]) for c in range(NCORES)])
    hT = [np.asarray(r[c]["out"]) for c in range(NCORES)]
    nc4, _ = build(attn_builder, S, NH=4)
    hb = full_seq(hT)
    maps = []
    for c in range(NCORES):
        b, j = c // QC, c % QC
        maps.append({"hT": hb[b], "npre": _pp(norm_pre[1, 1]), "cst": consts,
                     "w": attn_wslab(att_in_bf, list(range(4 * j, 4 * j + 4)))})
    r = _run(nc4, maps)
    del hb, maps
    cat_b = [np.ascontiguousarray(np.concatenate([np.asarray(r[b * QC + j]["catT"]) for j in range(QC)], axis=0)) for b in range(B)]
    nc5, _ = build(tok_builder, NTOK, nffn=1, prologue=True)
    com = {"wo": att_out_t, "npo": _pp(norm_post[1, 1])}
    com.update(ffn_map(3, 1, 2, 0))
    cs_ = tok_shards(cat_b)
    r = _run(nc5, [dict(com, hT=hT[c], catT=cs_[c]) for c in range(NCORES)])
    out = np.concatenate([np.asarray(r[c]["out"]).T for c in range(NCORES)], axis=0).reshape(B, S, D)
    return np.ascontiguousarray(out.astype(np.float32))
```

```python
import numpy as np
import ml_dtypes
import concourse.bass as bass
import concourse.mybir as mybir
from concourse.bass_utils import run_bass_kernel_spmd

F32 = mybir.dt.float32
BF16 = mybir.dt.bfloat16
AF = mybir.ActivationFunctionType
ALU = mybir.AluOpType
NPBF = ml_dtypes.bfloat16

NOSYNC_SAME = ("dve", "act")
SMALL_FREE = 32


class Tk:
    __slots__ = ("name", "h", "w", "rd", "dsem", "dcnt", "nowaw", "small")

    def __init__(self, name, h):
        self.name = name
        self.h = h
        self.w = None
        self.rd = {}
        self.dsem = None
        self.dcnt = 0
        self.nowaw = False
        self.small = False

    def __getitem__(self, k):
        return self.h[k]


class KB:
    def __init__(self, nc, emit, need=None):
        self.nc = nc
        self.emit = emit
        if nc is not None:
            self.E = {"pe": nc.tensor, "act": nc.scalar, "dve": nc.vector, "pool": nc.gpsimd, "sp": nc.sync}
        else:
            self.E = {"pe": None, "act": None, "dve": None, "pool": None, "sp": None}
        self.idx = {e: 0 for e in self.E}
        self.need = need if need is not None else {e: set() for e in self.E}
        self.rank = None
        if emit:
            self.rank = {}
            for e in self.E:
                srt = sorted(self.need[e])
                self.rank[e] = {ix: i + 1 for i, ix in enumerate(srt)}
        self.waited = {e: {} for e in self.E}
        self.sems = {}
        self.tiles = {}
        self.ntile = 0
        self.nsem = 0
        self.ninst = 0

    def _sem(self, name):
        if name not in self.sems:
            self.nsem += 1
            self.sems[name] = self.nc.alloc_semaphore(name) if self.emit else name
        return self.sems[name]

    def sb(self, name, shape, dt):
        h = self.nc.alloc_sbuf_tensor(name, list(shape), dt) if self.emit else None
        t = Tk(name, h)
        t.small = int(np.prod(shape[1:])) <= SMALL_FREE
        self.tiles[name] = t
        return t

    def ps(self, name, shape, dt=F32):
        h = self.nc.alloc_psum_tensor(name, list(shape), dt) if self.emit else None
        t = Tk(name, h)
        self.tiles[name] = t
        return t

    def dram(self, name, shape, dt, kind="Internal"):
        h = self.nc.dram_tensor(name, list(shape), dt, kind=kind) if self.emit else None
        t = Tk(name, h)
        self.tiles[name] = t
        return t

    def _key(self, dep):
        return (dep[0], dep[1] if dep[0] == 'c' else dep[1].name)

    def _wait(self, e, dep, t=None):
        if dep is None:
            return
        key = self._key(dep)
        val = dep[2]
        if dep[0] == 'c' and dep[1] == e:
            if e in ("pe", "sp"):
                return
            if e in NOSYNC_SAME and not (t is not None and t.small):
                return
        if self.waited[e].get(key, 0) >= val:
            return
        self.waited[e][key] = val
        if dep[0] == 'c':
            if not self.emit:
                self.need[dep[1]].add(val)
            else:
                self.E[e].wait_ge(self._sem("c_" + dep[1]), self.rank[dep[1]][val])
        else:
            if self.emit:
                self.E[e].wait_ge(dep[1].dsem, 16 * val)

    def _deps(self, e, reads, writes):
        for t in reads:
            self._wait(e, t.w, t)
        for t in writes:
            if t.nowaw:
                continue
            self._wait(e, t.w, t)
            for d in t.rd.values():
                self._wait(e, d, t)

    def op(self, e, fn, reads=(), writes=()):
        self._deps(e, reads, writes)
        self.idx[e] += 1
        ix = self.idx[e]
        self.ninst += 1
        if self.emit:
            ins = fn(self.E[e])
            if ix in self.rank[e]:
                ins.then_inc(self._sem("c_" + e), 1)
        dep = ('c', e, ix)
        for t in reads:
            t.rd[('c', e)] = dep
        for t in writes:
            t.w = dep
            t.rd = {}

    def dma(self, q, dst, dst_ap, src, src_ap, **kw):
        self._deps(q, [src], [dst])
        self.idx[q] += 1
        self.ninst += 1
        if dst.dsem is None:
            dst.dsem = self._sem("d_" + dst.name)
        dst.dcnt += 1
        if self.emit:
            self.E[q].dma_start(out=dst_ap(), in_=src_ap(), **kw).then_inc(dst.dsem, 16)
        dep = ('d', dst, dst.dcnt)
        src.rd[('d', dst.name)] = dep
        dst.w = dep
        dst.rd = {}

    def finish(self, outs, e="sp"):
        for t in outs:
            self._wait(e, t.w)


def build(fn, *args, **kw):
    kb1 = KB(None, False)
    fn(kb1, *args, **kw)
    nc = bass.Bass("TRN2", target_bir_lowering=False)
    kb2 = KB(nc, True, need=kb1.need)
    fn(kb2, *args, **kw)
    return nc, kb2


EPS = 1e-6


class Ring:
    def __init__(self, kb, name, shape, dt, R, src, src_ap_of, nblocks, q="sp", tiles=None):
        self.kb = kb
        self.tiles = tiles if tiles is not None else [kb.sb(f"{name}{i}", shape, dt) for i in range(R)]
        self.R = R
        self.src = src
        self.src_ap_of = src_ap_of
        self.n = nblocks
        self.issued = 0
        self.q = q

    def prefetch(self, upto):
        while self.issued < min(upto + 1, self.n):
            i = self.issued
            t = self.tiles[i % self.R]
            self.kb.dma(self.q, t, (lambda t=t: t.h.ap()), self.src, (lambda i=i: self.src_ap_of(i)))
            self.issued += 1

    def get(self, i):
        self.prefetch(i + self.R - 1)
        return self.tiles[i % self.R]


def tok_builder(kb, NTOK, nffn=1, prologue=False, T=512, DM=2048, DF=5632):
    KC = DM // 128
    FC = DF // 128
    OC = DM // 128
    NT = NTOK // T
    hT = kb.dram("hT", [DM, NTOK], F32, kind="ExternalInput")
    out = kb.dram("out", [DM, NTOK], F32, kind="ExternalOutput")
    out.nowaw = True
    W = []
    for j in range(nffn):
        W.append(dict(
            wg=kb.dram(f"wg{j}", [FC, 128, KC, 128], BF16, kind="ExternalInput"),
            wu=kb.dram(f"wu{j}", [FC, 128, KC, 128], BF16, kind="ExternalInput"),
            wd=kb.dram(f"wd{j}", [OC, 128, FC, 128], BF16, kind="ExternalInput"),
            npre=kb.dram(f"npre{j}", [128, KC], F32, kind="ExternalInput"),
            npost=kb.dram(f"npost{j}", [128, OC], F32, kind="ExternalInput")))
    if prologue:
        catT = kb.dram("catT", [DM, NTOK], BF16, kind="ExternalInput")
        wo = kb.dram("wo", [OC, 128, KC, 128], BF16, kind="ExternalInput")
        npo = kb.dram("npo", [128, OC], F32, kind="ExternalInput")
    nstage = nffn + (1 if prologue else 0)
    mids = []
    for i in range(nstage - 1):
        m = kb.dram(f"hmid{i}", [DM, NTOK], F32)
        m.nowaw = True
        mids.append(m)

    ones = kb.sb("ones", [128, 128], BF16)
    kb.op("pool", lambda e: e.memset(ones.h.ap(), 1.0), writes=[ones])
    wpre = kb.sb("wpre", [128, KC], F32)
    wpost = kb.sb("wpost", [128, OC], F32)
    hin = kb.sb("hin", [128, KC, T], F32)
    xT = [kb.sb(f"xT{k}", [128, T], BF16) for k in range(KC)]
    sq = [kb.sb(f"sq{i}", [128, T], BF16) for i in range(2)]
    rstd = kb.sb("rstd", [128, T], F32)
    rstd2 = kb.sb("rstd2", [128, T], F32)
    act = [kb.sb(f"act{f}", [128, T], BF16) for f in range(FC)]
    sg = [kb.sb(f"sg{i}", [128, T], F32) for i in range(2)]
    yb = [kb.sb(f"y{o}", [128, T], F32) for o in range(OC)]
    res = [kb.sb(f"res{i}", [128, T], F32) for i in range(2)]
    pgu = [(kb.ps(f"pg{i}", [128, T]), kb.ps(f"pu{i}", [128, T])) for i in range(2)]
    pd = [kb.ps(f"pd{i}", [128, T]) for i in range(2)]
    pst = [kb.ps(f"pst{i}", [128, T]) for i in range(2)]
    rgt = [kb.sb(f"rg{i}", [128, KC, 128], BF16) for i in range(3)]
    rut = [kb.sb(f"ru{i}", [128, KC, 128], BF16) for i in range(3)]
    rdt = [kb.sb(f"rd{i}", [128, FC, 128], BF16) for i in range(2)]

    def sumsq(src_of, n, pbank, rs):
        for k in range(n):
            st, sap = src_of(k)
            q = sq[k % 2]
            kb.op("act", lambda e, q=q, sap=sap: e.activation(q.h.ap(), sap(), AF.Square), reads=[st], writes=[q])
            kb.op("pe", lambda e, q=q, k=k: e.matmul(pbank.h.ap(), ones.h.ap(), q.h.ap(), start=(k == 0), stop=(k == n - 1)),
                  reads=[ones, q], writes=[pbank])
        kb.op("act", lambda e: e.activation(rs.h.ap(), pbank.h.ap(), AF.Sqrt, bias=float(DM * EPS)), reads=[pbank], writes=[rs])
        kb.op("dve", lambda e: e.reciprocal(rs.h.ap(), rs.h.ap()), reads=[rs], writes=[rs])

    def down_post(t, src, dst, ring, nk, srcs, resw):
        for o in range(OC):
            w = ring.get(t * OC + o)
            p = pd[o % 2]
            for f in range(nk):
                kb.op("pe", lambda e, f=f, w=w, p=p: e.matmul(p.h.ap(), w.h[:, f, :], srcs[f].h.ap(), start=(f == 0), stop=(f == nk - 1)),
                      reads=[w, srcs[f]], writes=[p])
            kb.op("act", lambda e, o=o, p=p: e.activation(yb[o].h.ap(), p.h.ap(), AF.Copy), reads=[p], writes=[yb[o]])
        sumsq(lambda k: (yb[k], lambda k=k: yb[k].h.ap()), OC, pst[1], rstd2)
        for o in range(OC):
            r = res[o % 2]
            kb.dma("sp", r, lambda r=r: r.h.ap(), src, lambda o=o: src.h[o * 128:(o + 1) * 128, t * T:(t + 1) * T])
            kb.op("dve", lambda e, o=o: e.scalar_tensor_tensor(yb[o].h.ap(), yb[o].h.ap(), wpost.h[:, o:o + 1], rstd2.h.ap(),
                                                                ALU.mult, ALU.mult),
                  reads=[yb[o], wpost, rstd2], writes=[yb[o]])
            kb.op("pool", lambda e, o=o, r=r: e.tensor_tensor(r.h.ap(), yb[o].h.ap(), r.h.ap(), ALU.add),
                  reads=[yb[o], r], writes=[r])
            kb.dma("sp", dst, lambda o=o: dst.h[o * 128:(o + 1) * 128, t * T:(t + 1) * T], r, lambda r=r: r.h.ap())

    stage = 0
    cur = hT
    s = float(np.sqrt(DM))
    if prologue:
        dst = mids[0] if nstage > 1 else out
        kb.dma("sp", wpost, lambda: wpost.h.ap(), npo, lambda: npo.h.ap())
        kb.op("dve", lambda e: e.tensor_scalar_mul(wpost.h.ap(), wpost.h.ap(), s), reads=[wpost], writes=[wpost])
        ro = Ring(kb, "ro", None, None, 3, wo, lambda i: wo.h[i % OC], NT * OC, q="sp", tiles=rgt)
        for t in range(NT):
            for k in range(KC):
                kb.dma("sp", xT[k], lambda k=k: xT[k].h.ap(), catT, lambda k=k: catT.h[k * 128:(k + 1) * 128, t * T:(t + 1) * T])
            down_post(t, cur, dst, ro, KC, xT, None)
        cur = dst
        stage = 1

    for j in range(nffn):
        Wj = W[j]
        dst = out if stage == nstage - 1 else mids[stage]
        kb.dma("sp", wpre, lambda: wpre.h.ap(), Wj["npre"], lambda: Wj["npre"].h.ap())
        kb.dma("sp", wpost, lambda: wpost.h.ap(), Wj["npost"], lambda: Wj["npost"].h.ap())
        kb.op("dve", lambda e: e.tensor_scalar_mul(wpre.h.ap(), wpre.h.ap(), s), reads=[wpre], writes=[wpre])
        kb.op("dve", lambda e: e.tensor_scalar_mul(wpost.h.ap(), wpost.h.ap(), 0.5 * s), reads=[wpost], writes=[wpost])
        rg = Ring(kb, "rg", None, None, 3, Wj["wg"], lambda i, Wj=Wj: Wj["wg"].h[i % FC], NT * FC, q="sp", tiles=rgt)
        ru = Ring(kb, "ru", None, None, 3, Wj["wu"], lambda i, Wj=Wj: Wj["wu"].h[i % FC], NT * FC, q="sp", tiles=rut)
        rd = Ring(kb, "rd", None, None, 2, Wj["wd"], lambda i, Wj=Wj: Wj["wd"].h[i % OC], NT * OC, q="sp", tiles=rdt)

        def prenorm(t, cur=cur):
            kb.dma("sp", hin, lambda: hin.h.ap(), cur,
                   lambda: cur.h[:, t * T:(t + 1) * T].rearrange("(kc p) n -> p kc n", p=128))
            sumsq(lambda k: (hin, lambda k=k: hin.h[:, k, :]), KC, pst[0], rstd)
            for k in range(KC):
                kb.op("dve", lambda e, k=k: e.scalar_tensor_tensor(xT[k].h.ap(), hin.h[:, k, :], wpre.h[:, k:k + 1], rstd.h.ap(),
                                                                    ALU.mult, ALU.mult),
                      reads=[hin, wpre, rstd], writes=[xT[k]])

        def gateup(t):
            for f in range(FC):
                i = t * FC + f
                g = rg.get(i)
                u = ru.get(i)
                pg, pu = pgu[f % 2]
                for k in range(KC):
                    kb.op("pe", lambda e, k=k, g=g, pg=pg: e.matmul(pg.h.ap(), g.h[:, k, :], xT[k].h.ap(), start=(k == 0), stop=(k == KC - 1)),
                          reads=[g, xT[k]], writes=[pg])
                for k in range(KC):
                    kb.op("pe", lambda e, k=k, u=u, pu=pu: e.matmul(pu.h.ap(), u.h[:, k, :], xT[k].h.ap(), start=(k == 0), stop=(k == KC - 1)),
                          reads=[u, xT[k]], writes=[pu])
                s_ = sg[f % 2]
                kb.op("act", lambda e, s_=s_, pg=pg: e.activation(s_.h.ap(), pg.h.ap(), AF.Silu), reads=[pg], writes=[s_])
                kb.op("dve", lambda e, s_=s_, pu=pu, f=f: e.tensor_tensor(act[f].h.ap(), s_.h.ap(), pu.h.ap(), ALU.mult),
                      reads=[s_, pu], writes=[act[f]])

        prenorm(0)
        for t in range(NT):
            gateup(t)
            if t + 1 < NT:
                prenorm(t + 1)
            down_post(t, cur, dst, rd, FC, act, None)
        cur = dst
        stage += 1
    kb.finish([out])


def attn_builder(kb, S, NH=4, T=512, DM=2048, SB=2048):
    KC = DM // 128
    NSB = S // SB
    hT = kb.dram("hT", [DM, S], F32, kind="ExternalInput")
    npre = kb.dram("npre", [128, KC], F32, kind="ExternalInput")
    w = kb.dram("w", [NH, 128, KC, 640], BF16, kind="ExternalInput")
    cst = kb.dram("cst", [128, 4, 128], BF16, kind="ExternalInput")
    catT = kb.dram("catT", [NH * 128, S], BF16, kind="ExternalOutput")
    catT.nowaw = True

    cs = kb.sb("cs", [128, 4, 128], BF16)
    kb.dma("sp", cs, lambda: cs.h.ap(), cst, lambda: cst.h.ap())
    wpre = kb.sb("wpre", [128, KC], F32)
    kb.dma("sp", wpre, lambda: wpre.h.ap(), npre, lambda: npre.h.ap())
    kb.op("dve", lambda e: e.tensor_scalar_mul(wpre.h.ap(), wpre.h.ap(), float(np.sqrt(DM))), reads=[wpre], writes=[wpre])
    hinB = [kb.sb(f"hin{i}", [128, KC, T], F32) for i in range(2)]
    xTB = [[kb.sb(f"xT{i}_{k}", [128, T], BF16) for k in range(KC)] for i in range(2)]
    sq = [kb.sb(f"sq{i}", [128, T], BF16) for i in range(2)]
    rstdB = [kb.sb(f"rstd{i}", [128, T], F32) for i in range(2)]
    wsb = kb.sb("wsb", [128, KC, 640], BF16)
    Q = [kb.sb(f"Q{g}", [128, SB], BF16) for g in range(3)]
    Kb = kb.sb("Kb", [128, 2 * SB], BF16)
    VT = kb.sb("VT", [128, SB], BF16)
    Vs = [kb.sb(f"Vs{g}", [128, 32, 128], BF16) for g in range(3)]
    acc = kb.sb("acc", [128, 2, SB], F32)
    pt = [kb.sb(f"pt{i}", [128, 2, 128], BF16) for i in range(2)]
    ob = kb.sb("ob", [128, SB], BF16)
    pstat = kb.ps("pstat", [128, T])
    pproj = [kb.ps(f"pproj{i}", [128, T]) for i in range(2)]
    psc = [kb.ps(f"psc{i}", [128, 2, 128]) for i in range(2)]
    pol = [kb.ps(f"pol{i}", [128, 2, 128]) for i in range(2)]
    ptr = kb.ps("ptr", [128, 4, 128], BF16)
    DIL = (1, 4, 16)
    SCALE = float(128 ** -0.5)

    def prenorm(t0, par):
        hin, xT, rstd = hinB[par], xTB[par], rstdB[par]
        kb.dma("sp", hin, lambda: hin.h.ap(), hT, lambda: hT.h[:, t0:t0 + T].rearrange("(kc p) n -> p kc n", p=128))
        for k in range(KC):
            q = sq[k % 2]
            kb.op("act", lambda e, q=q, k=k: e.activation(q.h.ap(), hin.h[:, k, :], AF.Square), reads=[hin], writes=[q])
            kb.op("pe", lambda e, q=q, k=k: e.matmul(pstat.h.ap(), cs.h[:, 3, :], q.h.ap(), start=(k == 0), stop=(k == KC - 1)),
                  reads=[cs, q], writes=[pstat])
        kb.op("act", lambda e: e.activation(rstd.h.ap(), pstat.h.ap(), AF.Sqrt, bias=float(DM * EPS)), reads=[pstat], writes=[rstd])
        kb.op("dve", lambda e: e.reciprocal(rstd.h.ap(), rstd.h.ap()), reads=[rstd], writes=[rstd])
        for k in range(KC):
            kb.op("dve", lambda e, k=k: e.scalar_tensor_tensor(xT[k].h.ap(), hin.h[:, k, :], wpre.h[:, k:k + 1], rstd.h.ap(), ALU.mult, ALU.mult),
                  reads=[hin, wpre, rstd], writes=[xT[k]])

    NTB = SB // T
    blocks = [(hh, sbi, tb) for hh in range(NH) for sbi in range(NSB) for tb in range(NTB)]
    bidx = {b: i for i, b in enumerate(blocks)}
    prenorm(0, 0)
    for hh in range(NH):
        kb.dma("sp", wsb, lambda hh=hh: wsb.h.ap(), w, lambda hh=hh: w.h[hh])
        for sbi in range(NSB):
            for tb in range(NTB):
                bi = bidx[(hh, sbi, tb)]
                xT = xTB[bi % 2]
                if bi + 1 < len(blocks):
                    _, nsb, ntb = blocks[bi + 1]
                    prenorm(nsb * SB + ntb * T, (bi + 1) % 2)
                for cb in range(5):
                    pp = pproj[cb % 2]
                    for k in range(KC):
                        kb.op("pe", lambda e, k=k, cb=cb, pp=pp: e.matmul(pp.h.ap(), wsb.h[:, k, cb * 128:(cb + 1) * 128], xT[k].h.ap(),
                                                                          start=(k == 0), stop=(k == KC - 1)), reads=[wsb, xT[k]], writes=[pp])
                    if cb < 3:
                        dstt = Q[cb]
                        kb.op("act", lambda e, pp=pp, dstt=dstt, tb=tb: e.activation(dstt.h[:, tb * T:(tb + 1) * T], pp.h.ap(), AF.Copy, scale=SCALE),
                              reads=[pp], writes=[dstt])
                    elif cb == 3:
                        kb.op("act", lambda e, pp=pp, tb=tb: e.activation(Kb.h[:, SB + tb * T:SB + (tb + 1) * T], pp.h.ap(), AF.Copy),
                              reads=[pp], writes=[Kb])
                    else:
                        kb.op("dve", lambda e, pp=pp, tb=tb: e.tensor_copy(VT.h[:, tb * T:(tb + 1) * T], pp.h.ap()), reads=[pp], writes=[VT])
            for g, d in enumerate(DIL):
                for s4 in range(4):
                    for i in range(4):
                        st_ = s4 * 4 + i
                        blk, r = st_ // d, st_ % d
                        a = blk * 128 * d + r
                        kb.op("pe", lambda e, i=i, a=a, d=d: e.transpose(ptr.h[:, i, :], VT.h[:, a:a + 127 * d + 1:d], cs.h[:, 0, :]),
                              reads=[VT, cs], writes=[ptr])
                    kb.op("dve", lambda e, g=g, s4=s4: e.tensor_copy(Vs[g].h[:, 16 + s4 * 4:16 + s4 * 4 + 4, :], ptr.h.ap()),
                          reads=[ptr], writes=[Vs[g]])
            n = 0
            for g, d in enumerate(DIL):
                for st_ in range(16):
                    blk, r = st_ // d, st_ % d
                    a = blk * 128 * d + r
                    has_prev = not (sbi == 0 and blk == 0)
                    sc, ol, p_ = psc[n % 2], pol[n % 2], pt[n % 2]
                    n += 1
                    halves = (0, 1) if has_prev else (1,)
                    for hf in halves:
                        ka = SB + a - (128 * d if hf == 0 else 0)
                        kb.op("pe", lambda e, hf=hf, ka=ka, a=a, d=d, g=g, sc=sc: e.matmul(sc.h[:, hf, :], Kb.h[:, ka:ka + 127 * d + 1:d],
                                                                                          Q[g].h[:, a:a + 127 * d + 1:d], start=True, stop=True),
                              reads=[Kb, Q[g]], writes=[sc])
                    lo = halves[0]
                    kb.op("act", lambda e, sc=sc, p_=p_, lo=lo: e.activation(p_.h[:, lo:2, :], sc.h[:, lo:2, :], AF.Exp), reads=[sc], writes=[p_])
                    kb.op("dve", lambda e, p_=p_, lo=lo: e.tensor_tensor(p_.h[:, lo:2, :], p_.h[:, lo:2, :], cs.h[:, 1 + lo:3, :], ALU.mult),
                          reads=[p_, cs], writes=[p_])
                    for j, hf in enumerate(halves):
                        gs = 16 + st_ - (d if hf == 0 else 0)
                        kb.op("pe", lambda e, hf=hf, gs=gs, g=g, ol=ol, p_=p_, j=j: e.matmul(ol.h[:, 0, :], Vs[g].h[:, gs, :], p_.h[:, hf, :],
                                                                                            start=(j == 0), stop=(j == len(halves) - 1)),
                              reads=[Vs[g], p_], writes=[ol])
                    for j, hf in enumerate(halves):
                        kb.op("pe", lambda e, hf=hf, ol=ol, p_=p_, j=j: e.matmul(ol.h[:, 1, :], cs.h[:, 3, :], p_.h[:, hf, :],
                                                                                  start=(j == 0), stop=(j == len(halves) - 1)),
                              reads=[cs, p_], writes=[ol])
                    if g == 0:
                        kb.op("act", lambda e, ol=ol, a=a, d=d: e.activation(acc.h[:, :, a:a + 127 * d + 1:d], ol.h.ap(), AF.Copy), reads=[ol], writes=[acc])
                    else:
                        kb.op("dve", lambda e, ol=ol, a=a, d=d: e.tensor_tensor(acc.h[:, :, a:a + 127 * d + 1:d], acc.h[:, :, a:a + 127 * d + 1:d], ol.h.ap(), ALU.add),
                              reads=[ol, acc], writes=[acc])
            kb.op("dve", lambda e: e.reciprocal(acc.h[:, 1, :], acc.h[:, 1, :]), reads=[acc], writes=[acc])
            kb.op("dve", lambda e: e.tensor_tensor(ob.h.ap(), acc.h[:, 0, :], acc.h[:, 1, :], ALU.mult), reads=[acc], writes=[ob])
            kb.dma("sp", catT, lambda hh=hh, sbi=sbi: catT.h[hh * 128:(hh + 1) * 128, sbi * SB:(sbi + 1) * SB], ob, lambda: ob.h.ap())
            if sbi + 1 < NSB:
                kb.op("pool", lambda e: e.tensor_copy(Kb.h[:, 0:SB], Kb.h[:, SB:2 * SB]), reads=[Kb], writes=[Kb])
                for g in range(3):
                    kb.op("pool", lambda e, g=g: e.tensor_copy(Vs[g].h[:, 0:16, :], Vs[g].h[:, 16:32, :]), reads=[Vs[g]], writes=[Vs[g]])
    kb.finish([catT])


def attn_consts():
    c = np.zeros((128, 4, 128), np.float32)
    k = np.arange(128)[:, None]
    q = np.arange(128)[None, :]
    c[:, 0, :] = np.eye(128)
    c[:, 1, :] = (k >= q)
    c[:, 2, :] = (k <= q)
    c[:, 3, :] = 1.0
    return c.astype(NPBF)


def attn_wslab(att_w_in_bf, heads):
    out = []
    for h in heads:
        cols = [att_w_in_bf[:, g * 2048 + h * 128: g * 2048 + (h + 1) * 128] for g in range(3)]
        cols.append(att_w_in_bf[:, 6144 + h * 128: 6144 + (h + 1) * 128])
        cols.append(att_w_in_bf[:, 8192 + h * 128: 8192 + (h + 1) * 128])
        wcat = np.concatenate(cols, axis=1)
        out.append(wcat.reshape(16, 128, 640).transpose(1, 0, 2))
    return np.ascontiguousarray(np.stack(out))


AB_NCOL = 2308


def ab_builder(kb, S, T=256, DM=2048, C=128, do_hgrn=True, do_ssd=True):
    KC = DM // 128
    NB = S // T
    NCH = T // C
    hT = kb.dram("hT", [DM, S], F32, kind="ExternalInput")
    npre = kb.dram("npre", [128, KC], F32, kind="ExternalInput")
    w = kb.dram("w", [128, KC, AB_NCOL], BF16, kind="ExternalInput")
    cst = kb.dram("cst", [128, 4, 128], BF16, kind="ExternalInput")
    prm = kb.dram("prm", [128, 24], F32, kind="ExternalInput")
    cw = kb.dram("cw", [128, 6, 5], F32, kind="ExternalInput")
    dtbrow = kb.dram("dtbrow", [128, 4], F32, kind="ExternalInput")
    catT = kb.dram("catT", [512, S], BF16, kind="ExternalOutput")
    catT.nowaw = True

    cs = kb.sb("cs", [128, 4, 128], BF16)
    kb.dma("sp", cs, lambda: cs.h.ap(), cst, lambda: cst.h.ap())
    P = kb.sb("P", [128, 24], F32)
    kb.dma("sp", P, lambda: P.h.ap(), prm, lambda: prm.h.ap())
    CW = kb.sb("CW", [128, 6, 5], F32)
    kb.dma("sp", CW, lambda: CW.h.ap(), cw, lambda: cw.h.ap())
    DTB = kb.sb("DTB", [128, 4], F32)
    kb.dma("sp", DTB, lambda: DTB.h.ap(), dtbrow, lambda: dtbrow.h.ap())
    wpre = kb.sb("wpre", [128, KC], F32)
    kb.dma("sp", wpre, lambda: wpre.h.ap(), npre, lambda: npre.h.ap())
    kb.op("dve", lambda e: e.tensor_scalar_mul(wpre.h.ap(), wpre.h.ap(), float(np.sqrt(DM))), reads=[wpre], writes=[wpre])
    wsb = kb.sb("wsb", [128, KC, AB_NCOL], BF16)
    kb.dma("sp", wsb, lambda: wsb.h.ap(), w, lambda: w.h.ap())
    onesf = kb.sb("onesf", [128, C], F32)
    kb.op("pool", lambda e: e.memset(onesf.h.ap(), 1.0), writes=[onesf])
    LB = kb.sb("LB", [128, 4], F32)
    NEGA = kb.sb("NEGA", [128, 4], F32)
    for hd in range(2):
        kb.op("dve", lambda e, hd=hd: e.tensor_tensor(LB.h[:, 2 * hd:2 * hd + 1], P.h[:, 2 * hd:2 * hd + 1], P.h[:, 2 * hd + 1:2 * hd + 2], ALU.subtract),
              reads=[P], writes=[LB])
        kb.op("act", lambda e, hd=hd: e.activation(LB.h[:, 2 * hd:2 * hd + 1], LB.h[:, 2 * hd:2 * hd + 1], AF.Sigmoid), reads=[LB], writes=[LB])
        kb.op("dve", lambda e, hd=hd: e.tensor_scalar(LB.h[:, 2 * hd + 1:2 * hd + 2], LB.h[:, 2 * hd:2 * hd + 1], -1.0, 1.0, ALU.mult, ALU.add),
              reads=[LB], writes=[LB])
    kb.op("act", lambda e: e.activation(NEGA.h.ap(), P.h[:, 6:10], AF.Exp), reads=[P], writes=[NEGA])
    kb.op("dve", lambda e: e.tensor_scalar_mul(NEGA.h.ap(), NEGA.h.ap(), -1.0), reads=[NEGA], writes=[NEGA])

    hin = kb.sb("hin", [128, KC, T], F32)
    xT = [kb.sb(f"xT{k}", [128, T], BF16) for k in range(KC)]
    sq = [kb.sb(f"sq{i}", [128, T], BF16) for i in range(2)]
    rstd = kb.sb("rstd", [128, T], F32)
    pstat = kb.ps("pstat", [128, T])
    pproj = [kb.ps(f"pproj{i}", [128, T]) for i in range(2)]
    pA = kb.ps("pA", [128, 128])
    pO = kb.ps("pO", [128, 128])
    pS = kb.ps("pS", [128, 128])
    pT_ = kb.ps("pT", [128, 128], BF16)
    pN = kb.ps("pN", [128, 128])

    def prenorm(t0):
        kb.dma("sp", hin, lambda: hin.h.ap(), hT, lambda: hT.h[:, t0:t0 + T].rearrange("(kc p) n -> p kc n", p=128))
        for k in range(KC):
            q = sq[k % 2]
            kb.op("act", lambda e, q=q, k=k: e.activation(q.h.ap(), hin.h[:, k, :], AF.Square), reads=[hin], writes=[q])
            kb.op("pe", lambda e, q=q, k=k: e.matmul(pstat.h.ap(), cs.h[:, 3, :], q.h.ap(), start=(k == 0), stop=(k == KC - 1)),
                  reads=[cs, q], writes=[pstat])
        kb.op("act", lambda e: e.activation(rstd.h.ap(), pstat.h.ap(), AF.Sqrt, bias=float(DM * EPS)), reads=[pstat], writes=[rstd])
        kb.op("dve", lambda e: e.reciprocal(rstd.h.ap(), rstd.h.ap()), reads=[rstd], writes=[rstd])
        for k in range(KC):
            kb.op("dve", lambda e, k=k: e.scalar_tensor_tensor(xT[k].h.ap(), hin.h[:, k, :], wpre.h[:, k:k + 1], rstd.h.ap(), ALU.mult, ALU.mult),
                  reads=[hin, wpre, rstd], writes=[xT[k]])

    npj = [0]

    def proj_fm(col, M, evac):
        pp = pproj[npj[0] % 2]
        npj[0] += 1
        for k in range(KC):
            kb.op("pe", lambda e, k=k, pp=pp: e.matmul(pp.h[0:M, :], wsb.h[:, k, col:col + M], xT[k].h.ap(), start=(k == 0), stop=(k == KC - 1)),
                  reads=[wsb, xT[k]], writes=[pp])
        evac(pp)

    def proj_tm(col, N, c, evac):
        pp = pproj[npj[0] % 2]
        npj[0] += 1
        for k in range(KC):
            kb.op("pe", lambda e, k=k, pp=pp: e.matmul(pp.h[:, 0:N], xT[k].h[:, c * C:(c + 1) * C], wsb.h[:, k, col:col + N], start=(k == 0), stop=(k == KC - 1)),
                  reads=[wsb, xT[k]], writes=[pp])
        evac(pp)

    bt = kb.sb("g_b", [128, C], F32)
    nbm = kb.sb("g_nbm", [128, 1], F32)
    eq = kb.sb("g_eq", [128, C], F32)
    ek = kb.sb("g_ek", [128, C], F32)
    ei = kb.sb("g_ei", [128, C], F32)
    es = kb.sb("g_es", [128, C], F32)
    qs = kb.sb("g_qs", [128, C], BF16)
    ks = kb.sb("g_ks", [128, C], BF16)
    qi = kb.sb("g_qi", [128, C], BF16)
    kT = kb.sb("g_kT", [128, C], BF16)
    kst = kb.sb("g_kst", [128, 128], BF16)
    pTs = kb.sb("g_pT", [128, 128], BF16)

    acol = kb.sb("g_acol", [128, 1], F32)
    dl = kb.sb("g_dl", [128, C], F32)
    scS = kb.sb("g_scS", [128, C], F32)
    qbf = kb.sb("g_qbf", [128, C], BF16)
    kbf = kb.sb("g_kbf", [128, C], BF16)

    def gla(qt, qap, kt, kap, ldt, ldap, V, dv, Sf, Sb, scalar_decay=False):
        kb.op("dve", lambda e: e.tensor_tensor_scan(bt.h.ap(), onesf.h.ap(), ldap(), 0.0, ALU.mult, ALU.add), reads=[onesf, ldt], writes=[bt])
        kb.op("dve", lambda e: e.tensor_scalar_mul(nbm.h.ap(), bt.h[:, C // 2:C // 2 + 1], -1.0), reads=[bt], writes=[nbm])
        kb.op("act", lambda e: e.activation(ei.h.ap(), bt.h.ap(), AF.Exp), reads=[bt], writes=[ei])
        kb.op("act", lambda e: e.activation(es.h.ap(), bt.h.ap(), AF.Exp, bias=bt.h[:, C - 1:C], scale=-1.0), reads=[bt], writes=[es])
        kb.op("dve", lambda e: e.tensor_tensor(qi.h.ap(), qap(), ei.h.ap(), ALU.mult), reads=[qt, ei], writes=[qi])
        kb.op("pool", lambda e: e.tensor_tensor(kT.h.ap(), kap(), es.h.ap(), ALU.mult), reads=[kt, es], writes=[kT])
        kb.op("pe", lambda e: e.transpose(pT_.h.ap(), kT.h.ap(), cs.h[:, 0, :]), reads=[kT, cs], writes=[pT_])
        kb.op("act", lambda e: e.activation(kst.h.ap(), pT_.h.ap(), AF.Copy), reads=[pT_], writes=[kst])
        if not scalar_decay:
            kb.op("act", lambda e: e.activation(eq.h.ap(), bt.h.ap(), AF.Exp, bias=nbm.h.ap()), reads=[bt, nbm], writes=[eq])
            kb.op("act", lambda e: e.activation(ek.h.ap(), bt.h.ap(), AF.Exp, bias=bt.h[:, C // 2:C // 2 + 1], scale=-1.0), reads=[bt], writes=[ek])
            kb.op("dve", lambda e: e.tensor_tensor(qs.h.ap(), qap(), eq.h.ap(), ALU.mult), reads=[qt, eq], writes=[qs])
            kb.op("pool", lambda e: e.tensor_tensor(ks.h.ap(), kap(), ek.h.ap(), ALU.mult), reads=[kt, ek], writes=[ks])
            kb.op("pe", lambda e: e.matmul(pA.h.ap(), ks.h.ap(), qs.h.ap(), start=True, stop=True), reads=[ks, qs], writes=[pA])
            kb.op("dve", lambda e: e.tensor_tensor(pTs.h.ap(), pA.h.ap(), cs.h[:, 2, :], ALU.mult), reads=[pA, cs], writes=[pTs])
        else:
            kb.op("dve", lambda e: e.tensor_tensor(dl.h.ap(), bt.h.ap(), cs.h[:, 0, :], ALU.mult), reads=[bt, cs], writes=[dl])
            kb.op("dve", lambda e: e.reduce_sum(acol.h.ap(), dl.h.ap(), mybir.AxisListType.X), reads=[dl], writes=[acol])
            kb.op("dve", lambda e: e.tensor_scalar(dl.h.ap(), bt.h.ap(), acol.h.ap(), 0.0, ALU.subtract, ALU.min), reads=[bt, acol], writes=[dl])
            kb.op("act", lambda e: e.activation(dl.h.ap(), dl.h.ap(), AF.Exp), reads=[dl], writes=[dl])
            kb.op("pool", lambda e: e.tensor_tensor(dl.h.ap(), dl.h.ap(), cs.h[:, 2, :], ALU.mult), reads=[dl, cs], writes=[dl])
            kb.op("dve", lambda e: e.tensor_tensor(pTs.h.ap(), scS.h.ap(), dl.h.ap(), ALU.mult), reads=[scS, dl], writes=[pTs])
        kb.op("pe", lambda e: e.matmul(pO.h[0:dv, :], V.h[:, 0:dv], pTs.h.ap(), start=True, stop=False), reads=[V, pTs], writes=[pO])
        kb.op("pe", lambda e: e.matmul(pO.h[0:dv, :], Sb.h[:, 0:dv], qi.h.ap(), start=False, stop=True), reads=[Sb, qi], writes=[pO])
        kb.op("pe", lambda e: e.matmul(pS.h[:, 0:dv], kst.h.ap(), V.h[:, 0:dv], start=True, stop=True), reads=[kst, V], writes=[pS])
        kb.op("dve", lambda e: e.scalar_tensor_tensor(Sf.h[:, 0:dv], Sf.h[:, 0:dv], ei.h[:, C - 1:C], pS.h[:, 0:dv], ALU.mult, ALU.add),
              reads=[Sf, ei, pS], writes=[Sf])
        kb.op("act", lambda e: e.activation(Sb.h[:, 0:dv], Sf.h[:, 0:dv], AF.Copy), reads=[Sf], writes=[Sb])

    if do_hgrn:
        hq = kb.sb("hq", [128, T], F32)
        hk = kb.sb("hk", [128, T], F32)
        hl = kb.sb("hl", [128, T], F32)
        hg = kb.sb("hg", [128, T], F32)
        hV = kb.sb("hV", [128, 128], BF16)
        hsq = kb.sb("hsq", [128, C], BF16)
        hrs = kb.sb("hrs", [128, C], F32)
        ho = kb.sb("ho", [128, C], F32)
        hob = [kb.sb(f"hob{i}", [128, T], BF16) for i in range(2)]
        HSf = [kb.sb(f"HSf{i}", [128, 128], F32) for i in range(2)]
        HSb = [kb.sb(f"HSb{i}", [128, 128], BF16) for i in range(2)]
        for i in range(2):
            kb.op("pool", lambda e, i=i: e.memset(HSf[i].h.ap(), 0.0), writes=[HSf[i]])
            kb.op("pool", lambda e, i=i: e.memset(HSb[i].h.ap(), 0.0), writes=[HSb[i]])

    if do_ssd:
        raw = [kb.sb(f"raw{i}", [128, 3 + T], F32) for i in range(6)]
        cv = [kb.sb(f"cv{i}", [128, T], F32) for i in range(6)]
        zs = [kb.sb(f"zs{i}", [64, T], F32) for i in range(4)]
        ld = [kb.sb(f"ld{i}", [128, T], F32) for i in range(4)]
        dtt = kb.sb("dtt", [128, 4], F32)
        sV = kb.sb("sV", [128, 64], BF16)
        syz = [kb.sb(f"syz{i}", [64, C], F32) for i in range(4)]
        ssq = kb.sb("ssq", [64, C], BF16)
        srs = kb.sb("srs", [64, C], F32)
        sob = [kb.sb(f"sob{i}", [64, C], BF16) for i in range(4)]
        SSf = [kb.sb(f"SSf{i}", [128, 64], F32) for i in range(4)]
        SSb = [kb.sb(f"SSb{i}", [128, 64], BF16) for i in range(4)]
        for i in range(4):
            kb.op("pool", lambda e, i=i: e.memset(SSf[i].h.ap(), 0.0), writes=[SSf[i]])
            kb.op("pool", lambda e, i=i: e.memset(SSb[i].h.ap(), 0.0), writes=[SSb[i]])
        for i in range(6):
            kb.op("pool", lambda e, i=i: e.memset(raw[i].h[:, 0:3], 0.0), writes=[raw[i]])

    for blk in range(NB):
        t0 = blk * T
        prenorm(t0)
        if do_hgrn:
            for hd in range(2):
                base = hd * 512
                proj_fm(base, 128, lambda pp: kb.op("act", lambda e: e.activation(hq.h.ap(), pp.h.ap(), AF.Copy), reads=[pp], writes=[hq]))
                proj_fm(base + 128, 128, lambda pp: kb.op("act", lambda e: e.activation(hk.h.ap(), pp.h.ap(), AF.Sigmoid), reads=[pp], writes=[hk]))
                kb.op("dve", lambda e, hd=hd: e.tensor_scalar(hk.h.ap(), hk.h.ap(), LB.h[:, 2 * hd + 1:2 * hd + 2], LB.h[:, 2 * hd:2 * hd + 1], ALU.mult, ALU.add),
                      reads=[hk, LB], writes=[hk])
                kb.op("act", lambda e: e.activation(hl.h.ap(), hk.h.ap(), AF.Ln), reads=[hk], writes=[hl])
                kb.op("dve", lambda e: e.tensor_scalar(hk.h.ap(), hk.h.ap(), -1.0, 1.0, ALU.mult, ALU.add), reads=[hk], writes=[hk])
                proj_fm(base + 256, 128, lambda pp: kb.op("act", lambda e: e.activation(hg.h.ap(), pp.h.ap(), AF.Silu), reads=[pp], writes=[hg]))
                ob_ = hob[hd]
                for c in range(NCH):
                    cl = slice(c * C, (c + 1) * C)
                    proj_tm(base + 384, 128, c, lambda pp: kb.op("dve", lambda e: e.tensor_copy(hV.h.ap(), pp.h[:, 0:128]), reads=[pp], writes=[hV]))
                    gla(hq, lambda cl=cl: hq.h[:, cl], hk, lambda cl=cl: hk.h[:, cl], hl, lambda cl=cl: hl.h[:, cl], hV, 128, HSf[hd], HSb[hd])
                    kb.op("act", lambda e: e.activation(hsq.h.ap(), pO.h.ap(), AF.Square), reads=[pO], writes=[hsq])
                    kb.op("pe", lambda e: e.matmul(pN.h.ap(), cs.h[:, 3, :], hsq.h.ap(), start=True, stop=True), reads=[cs, hsq], writes=[pN])
                    kb.op("act", lambda e: e.activation(hrs.h.ap(), pN.h.ap(), AF.Sqrt, bias=float(EPS), scale=1.0 / 128), reads=[pN], writes=[hrs])
                    kb.op("dve", lambda e: e.reciprocal(hrs.h.ap(), hrs.h.ap()), reads=[hrs], writes=[hrs])
                    kb.op("dve", lambda e: e.tensor_tensor(ho.h.ap(), pO.h.ap(), hrs.h.ap(), ALU.mult), reads=[pO, hrs], writes=[ho])
                    kb.op("dve", lambda e, hd=hd, cl=cl, ob_=ob_: e.scalar_tensor_tensor(ob_.h[:, cl], ho.h.ap(), P.h[:, 4 + hd:5 + hd], hg.h[:, cl], ALU.mult, ALU.mult),
                          reads=[ho, P, hg], writes=[ob_])
                kb.dma("sp", catT, lambda hd=hd, t0=t0: catT.h[hd * 128:(hd + 1) * 128, t0:t0 + T], ob_, lambda ob_=ob_: ob_.h.ap())
        if do_ssd:
            for r in range(4):
                proj_fm(1024 + r * 64, 64, lambda pp, r=r: kb.op("act", lambda e: e.activation(raw[r].h[0:64, 3:3 + T], pp.h[0:64, :], AF.Copy), reads=[pp], writes=[raw[r]]))
                proj_fm(1280 + r * 64, 64, lambda pp, r=r: kb.op("act", lambda e: e.activation(zs[r].h.ap(), pp.h[0:64, :], AF.Silu), reads=[pp], writes=[zs[r]]))
            proj_fm(1536, 128, lambda pp: kb.op("act", lambda e: e.activation(raw[4].h[:, 3:3 + T], pp.h.ap(), AF.Copy), reads=[pp], writes=[raw[4]]))
            proj_fm(1664, 128, lambda pp: kb.op("act", lambda e: e.activation(raw[5].h[:, 3:3 + T], pp.h.ap(), AF.Copy), reads=[pp], writes=[raw[5]]))
            for i in range(6):
                np_ = 64 if i < 4 else 128
                kb.op("dve", lambda e, i=i, np_=np_: e.tensor_scalar(cv[i].h[0:np_, :], raw[i].h[0:np_, 0:T], CW.h[0:np_, i, 0:1], CW.h[0:np_, i, 4:5], ALU.mult, ALU.add),
                      reads=[raw[i], CW], writes=[cv[i]])
                for j in range(1, 4):
                    kb.op("dve", lambda e, i=i, j=j, np_=np_: e.scalar_tensor_tensor(cv[i].h[0:np_, :], raw[i].h[0:np_, j:j + T], CW.h[0:np_, i, j:j + 1], cv[i].h[0:np_, :], ALU.mult, ALU.add),
                          reads=[raw[i], CW, cv[i]], writes=[cv[i]])
                kb.op("act", lambda e, i=i, np_=np_: e.activation(cv[i].h[0:np_, :], cv[i].h[0:np_, :], AF.Silu), reads=[cv[i]], writes=[cv[i]])
                kb.op("pool", lambda e, i=i, np_=np_: e.tensor_copy(raw[i].h[0:np_, 0:3], raw[i].h[0:np_, T:T + 3]), reads=[raw[i]], writes=[raw[i]])
            for r in range(4):
                proj_fm(1792 + r * 128, 128, lambda pp, r=r: kb.op("act", lambda e: e.activation(ld[r].h.ap(), pp.h.ap(), AF.Exp, bias=P.h[:, 10 + r:11 + r]), reads=[pp, P], writes=[ld[r]]))
                kb.op("act", lambda e, r=r: e.activation(ld[r].h.ap(), ld[r].h.ap(), AF.Ln, bias=1.0), reads=[ld[r]], writes=[ld[r]])
                kb.op("dve", lambda e, r=r: e.tensor_scalar_mul(ld[r].h.ap(), ld[r].h.ap(), NEGA.h[:, r:r + 1]), reads=[ld[r], NEGA], writes=[ld[r]])
            for c in range(NCH):
                cl = slice(c * C, (c + 1) * C)
                proj_tm(2304, 4, c, lambda pp: kb.op("dve", lambda e: e.tensor_tensor(dtt.h.ap(), pp.h[:, 0:4], DTB.h.ap(), ALU.add), reads=[pp, DTB], writes=[dtt]))
                kb.op("act", lambda e: e.activation(dtt.h.ap(), dtt.h.ap(), AF.Exp), reads=[dtt], writes=[dtt])
                kb.op("act", lambda e: e.activation(dtt.h.ap(), dtt.h.ap(), AF.Ln, bias=1.0), reads=[dtt], writes=[dtt])
                kb.op("dve", lambda e, cl=cl: e.tensor_copy(kbf.h.ap(), cv[4].h[:, cl]), reads=[cv[4]], writes=[kbf])
                kb.op("pool", lambda e, cl=cl: e.tensor_copy(qbf.h.ap(), cv[5].h[:, cl]), reads=[cv[5]], writes=[qbf])
                kb.op("pe", lambda e: e.matmul(pA.h.ap(), kbf.h.ap(), qbf.h.ap(), start=True, stop=True), reads=[kbf, qbf], writes=[pA])
                kb.op("act", lambda e: e.activation(scS.h.ap(), pA.h.ap(), AF.Copy), reads=[pA], writes=[scS])
                for r in range(4):
                    kb.op("dve", lambda e, r=r, cl=cl: e.tensor_copy(kT.h[0:64, :], cv[r].h[0:64, cl]), reads=[cv[r]], writes=[kT])
                    kb.op("pe", lambda e: e.transpose(pT_.h[:, 0:64], kT.h[0:64, :], cs.h[0:64, 0, 0:64]), reads=[kT, cs], writes=[pT_])
                    kb.op("dve", lambda e, r=r: e.tensor_scalar_mul(sV.h.ap(), pT_.h[:, 0:64], dtt.h[:, r:r + 1]), reads=[pT_, dtt], writes=[sV])
                    gla(cv[5], lambda cl=cl: cv[5].h[:, cl], cv[4], lambda cl=cl: cv[4].h[:, cl], ld[r], lambda r=r, cl=cl: ld[r].h[:, cl], sV, 64, SSf[r], SSb[r], scalar_decay=True)
                    kb.op("dve", lambda e, r=r, cl=cl: e.scalar_tensor_tensor(syz[r].h.ap(), cv[r].h[0:64, cl], P.h[0:64, 14 + r:15 + r], pO.h[0:64, :], ALU.mult, ALU.add),
                          reads=[cv[r], P, pO], writes=[syz[r]])
                    kb.op("dve", lambda e, r=r, cl=cl: e.tensor_tensor(syz[r].h.ap(), syz[r].h.ap(), zs[r].h[:, cl], ALU.mult), reads=[syz[r], zs[r]], writes=[syz[r]])
                    kb.op("act", lambda e, r=r: e.activation(ssq.h.ap(), syz[r].h.ap(), AF.Square), reads=[syz[r]], writes=[ssq])
                    kb.op("pe", lambda e, r=r: e.matmul(pN.h[0:64, :], cs.h[0:64, 3, 0:64], ssq.h.ap(), start=(r == 0), stop=(r == 3)), reads=[cs, ssq], writes=[pN])
                kb.op("act", lambda e: e.activation(srs.h.ap(), pN.h[0:64, :], AF.Sqrt, bias=float(EPS), scale=1.0 / 256), reads=[pN], writes=[srs])
                kb.op("dve", lambda e: e.reciprocal(srs.h.ap(), srs.h.ap()), reads=[srs], writes=[srs])
                for r in range(4):
                    kb.op("dve", lambda e, r=r: e.scalar_tensor_tensor(sob[r].h.ap(), syz[r].h.ap(), P.h[0:64, 18 + r:19 + r], srs.h.ap(), ALU.mult, ALU.mult),
                          reads=[syz[r], P, srs], writes=[sob[r]])
                    kb.dma("sp", catT, lambda r=r, c=c, t0=t0: catT.h[256 + r * 64:256 + (r + 1) * 64, t0 + c * C:t0 + (c + 1) * C], sob[r], lambda r=r: sob[r].h.ap())
    kb.finish([catT])


def ab_inputs(inp_bf_w_in, z, j):
    W = inp_bf_w_in
    cols = []
    for hd in (2 * j, 2 * j + 1):
        for off in (0, 1024, 3072, 2048):
            cols.append(W[:, off + hd * 128: off + (hd + 1) * 128])
    for r in range(4):
        cols.append(W[:, 5120 + j * 256 + r * 64: 5120 + j * 256 + (r + 1) * 64])
    for r in range(4):
        cols.append(W[:, 4096 + j * 256 + r * 64: 4096 + j * 256 + (r + 1) * 64])
    cols.append(W[:, 6144 + j * 128: 6144 + (j + 1) * 128])
    cols.append(W[:, 6656 + j * 128: 6656 + (j + 1) * 128])
    for r in range(4):
        cols.append(np.repeat(W[:, 7168 + 4 * j + r: 7168 + 4 * j + r + 1], 128, axis=1))
    cols.append(W[:, 7168 + 4 * j: 7168 + 4 * j + 4])
    wc = np.concatenate(cols, axis=1)
    assert wc.shape[1] == AB_NCOL
    wt = np.ascontiguousarray(wc.reshape(16, 128, AB_NCOL).transpose(1, 0, 2))
    prm = np.zeros((128, 24), np.float32)
    for i, hd in enumerate((2 * j, 2 * j + 1)):
        prm[:, 2 * i] = z["hgrn_lb"][0, hd * 128:(hd + 1) * 128]
        prm[:, 2 * i + 1] = z["hgrn_lb"][1, hd * 128:(hd + 1) * 128]
        prm[:, 4 + i] = z["hgrn_norm_w"][0, hd * 128:(hd + 1) * 128]
    for r in range(4):
        hh = 4 * j + r
        prm[:, 6 + r] = z["ssm_A_log"][0, hh]
        prm[:, 10 + r] = z["ssm_dt_bias"][0, hh]
        prm[:, 14 + r] = z["ssm_D"][0, hh]
        prm[0:64, 18 + r] = z["ssm_norm_w"][0, j * 256 + r * 64: j * 256 + (r + 1) * 64]
    cw = np.zeros((128, 6, 5), np.float32)
    cwt, cb = z["ssm_conv_w"][0], z["ssm_conv_b"][0]
    for r in range(4):
        ch = slice(j * 256 + r * 64, j * 256 + (r + 1) * 64)
        cw[0:64, r, 0:4] = cwt[:, ch].T
        cw[0:64, r, 4] = cb[ch]
    for i, off in ((4, 1024), (5, 1536)):
        ch = slice(off + j * 128, off + (j + 1) * 128)
        cw[:, i, 0:4] = cwt[:, ch].T
        cw[:, i, 4] = cb[ch]
    dtb = np.tile(z["ssm_dt_bias"][0, 4 * j:4 * j + 4][None, :], (128, 1)).astype(np.float32)
    return {"w": wt, "prm": prm, "cw": cw, "dtbrow": dtb}


def cast_builder(kb, M, CH=4096):
    x = kb.dram("x", [128, M], F32, kind="ExternalInput")
    y = kb.dram("y", [128, M], BF16, kind="ExternalOutput")
    y.nowaw = True
    tin = [kb.sb(f"ci{i}", [128, CH], F32) for i in range(3)]
    tout = [kb.sb(f"co{i}", [128, CH], BF16) for i in range(3)]
    n = M // CH
    for i in range(n):
        a, b = tin[i % 3], tout[i % 3]
        kb.dma("sp", a, lambda a=a: a.h.ap(), x, lambda i=i: x.h[:, i * CH:(i + 1) * CH])
        if i % 2 == 0:
            kb.op("act", lambda e, a=a, b=b: e.activation(b.h.ap(), a.h.ap(), AF.Copy), reads=[a], writes=[b])
        else:
            kb.op("dve", lambda e, a=a, b=b: e.tensor_copy(b.h.ap(), a.h.ap()), reads=[a], writes=[b])
        kb.dma("sp", y, lambda i=i: y.h[:, i * CH:(i + 1) * CH], b, lambda b=b: b.h.ap())
    kb.finish([y])


def _tile_w(w):
    K, N = w.shape
    return np.ascontiguousarray(w.reshape(K // 128, 128, N // 128, 128).transpose(2, 1, 0, 3))


def _pp(v):
    return np.ascontiguousarray(np.asarray(v, np.float32).reshape(16, 128).T)


NCORES = 8
_CORES = list(range(NCORES))


def _run(nc, maps):
    return run_bass_kernel_spmd(nc, maps, core_ids=_CORES).results


def kernel(x, norm_pre, norm_post, ffn_w_gate, ffn_w_up, ffn_w_down, ab_w_in, ab_w_out,
           hgrn_lb, hgrn_norm_w, ssm_conv_w, ssm_conv_b, ssm_dt_bias, ssm_A_log, ssm_D,
           ssm_norm_w, att_w_in, att_w_out):
    f32 = lambda a: np.asarray(a, np.float32)
    x = f32(x)
    norm_pre, norm_post = f32(norm_pre), f32(norm_post)
    B, S, D = x.shape
    NTOK = B * S // NCORES
    QC = NCORES // B
    srcs = []
    for l in range(2):
        for j in range(2):
            for w in (ffn_w_gate[l, j], ffn_w_up[l, j], ffn_w_down[l, j]):
                srcs.append(_tile_w(f32(w)))
    srcs += [f32(ab_w_in[0]), _tile_w(f32(ab_w_out[0])), f32(att_w_in[0]), _tile_w(f32(att_w_out[0]))]
    shapes = [t.shape for t in srcs]
    flat = np.concatenate([t.reshape(-1) for t in srcs])
    CH = 4096
    per = NCORES * 128 * CH
    tot = -(-flat.size // per) * per
    flat = np.concatenate([flat, np.zeros(tot - flat.size, np.float32)])
    M = tot // (NCORES * 128)
    nc0, _ = build(cast_builder, M, CH)
    parts = flat.reshape(NCORES, 128, M)
    r0 = _run(nc0, [{"x": parts[c]} for c in range(NCORES)])
    fb = np.concatenate([np.asarray(r0[c]["y"]).reshape(-1) for c in range(NCORES)])
    del flat, parts, srcs
    wts, off = [], 0
    for shp in shapes:
        n = int(np.prod(shp))
        wts.append(fb[off:off + n].reshape(shp))
        off += n
    ffw = wts[:12]
    ab_in_bf, ab_out_t, att_in_bf, att_out_t = wts[12:16]
    small = {"hgrn_lb": f32(hgrn_lb), "hgrn_norm_w": f32(hgrn_norm_w), "ssm_conv_w": f32(ssm_conv_w),
             "ssm_conv_b": f32(ssm_conv_b), "ssm_dt_bias": f32(ssm_dt_bias), "ssm_A_log": f32(ssm_A_log),
             "ssm_D": f32(ssm_D), "ssm_norm_w": f32(ssm_norm_w)}

    def ffn_map(k, l, slot, idx):
        return {f"wg{idx}": ffw[3 * k], f"wu{idx}": ffw[3 * k + 1], f"wd{idx}": ffw[3 * k + 2],
                f"npre{idx}": _pp(norm_pre[l, slot]), f"npost{idx}": _pp(norm_post[l, slot])}

    def full_seq(hs):
        return [np.ascontiguousarray(np.concatenate(hs[b * QC:(b + 1) * QC], axis=1)) for b in range(B)]

    def tok_shards(cat_b):
        return [np.ascontiguousarray(cat_b[c // QC][:, (c % QC) * NTOK:((c % QC) + 1) * NTOK]) for c in range(NCORES)]

    h = x.reshape(B * S, D)
    hT = [np.ascontiguousarray(h[c * NTOK:(c + 1) * NTOK].T) for c in range(NCORES)]
    consts = attn_consts()
    nc1, _ = build(tok_builder, NTOK, nffn=1, prologue=False)
    com = ffn_map(0, 0, 0, 0)
    r = _run(nc1, [dict(com, hT=hT[c]) for c in range(NCORES)])
    hT = [np.asarray(r[c]["out"]) for c in range(NCORES)]
    nc2, _ = build(ab_builder, S)
    hb = full_seq(hT)
    maps = []
    for c in range(NCORES):
        b, j = c // QC, c % QC
        m = ab_inputs(ab_in_bf, small, j)
        m.update({"hT": hb[b], "npre": _pp(norm_pre[0, 1]), "cst": consts})
        maps.append(m)
    r = _run(nc2, maps)
    del hb, maps
    cat_b = []
    for b in range(B):
        cat = np.empty((D, S), NPBF)
        for j in range(QC):
            o = np.asarray(r[b * QC + j]["catT"])
            cat[j * 256:(j + 1) * 256] = o[0:256]
            cat[1024 + j * 256:1024 + (j + 1) * 256] = o[256:512]
        cat_b.append(cat)
    nc3, _ = build(tok_builder, NTOK, nffn=2, prologue=True)
    com = {"wo": ab_out_t, "npo": _pp(norm_post[0, 1])}
    com.update(ffn_map(1, 0, 2, 0))
    com.update(ffn_map(2, 1, 0, 1))
    cs_ = tok_shards(cat_b)
    r = _run(nc3, [dict(com, hT=hT[c], catT=cs_[c]) for c in range(NCORES)])
    hT = [np.asarray(r[c]["out"]) for c in range(NCORES)]
    nc4, _ = build(attn_builder, S, NH=4)
    hb = full_seq(hT)
    maps = []
    for c in range(NCORES):
        b, j = c // QC, c % QC
        maps.append({"hT": hb[b], "npre": _pp(norm_pre[1, 1]), "cst": consts,
                     "w": attn_wslab(att_in_bf, list(range(4 * j, 4 * j + 4)))})
    r = _run(nc4, maps)
    del hb, maps
    cat_b = [np.ascontiguousarray(np.concatenate([np.asarray(r[b * QC + j]["catT"]) for j in range(QC)], axis=0)) for b in range(B)]
    nc5, _ = build(tok_builder, NTOK, nffn=1, prologue=True)
    com = {"wo": att_out_t, "npo": _pp(norm_post[1, 1])}
    com.update(ffn_map(3, 1, 2, 0))
    cs_ = tok_shards(cat_b)
    r = _run(nc5, [dict(com, hT=hT[c], catT=cs_[c]) for c in range(NCORES)])
    out = np.concatenate([np.asarray(r[c]["out"]).T for c in range(NCORES)], axis=0).reshape(B, S, D)
    return np.ascontiguousarray(out.astype(np.float32))
```

```python
import numpy as np
import ml_dtypes
import concourse.bass as bass
import concourse.mybir as mybir
from concourse.bass_utils import run_bass_kernel_spmd

F32 = mybir.dt.float32
BF16 = mybir.dt.bfloat16
AF = mybir.ActivationFunctionType
ALU = mybir.AluOpType
NPBF = ml_dtypes.bfloat16

NOSYNC_SAME = ("dve", "act")
SMALL_FREE = 32


class Tk:
    __slots__ = ("name", "h", "w", "rd", "dsem", "dcnt", "nowaw", "small")

    def __init__(self, name, h):
        self.name = name
        self.h = h
        self.w = None
        self.rd = {}
        self.dsem = None
        self.dcnt = 0
        self.nowaw = False
        self.small = False

    def __getitem__(self, k):
        return self.h[k]


class KB:
    def __init__(self, nc, emit, need=None):
        self.nc = nc
        self.emit = emit
        if nc is not None:
            self.E = {"pe": nc.tensor, "act": nc.scalar, "dve": nc.vector, "pool": nc.gpsimd, "sp": nc.sync}
        else:
            self.E = {"pe": None, "act": None, "dve": None, "pool": None, "sp": None}
        self.idx = {e: 0 for e in self.E}
        self.need = need if need is not None else {e: set() for e in self.E}
        self.rank = None
        if emit:
            self.rank = {}
            for e in self.E:
                srt = sorted(self.need[e])
                self.rank[e] = {ix: i + 1 for i, ix in enumerate(srt)}
        self.waited = {e: {} for e in self.E}
        self.sems = {}
        self.tiles = {}
        self.ntile = 0
        self.nsem = 0
        self.ninst = 0

    def _sem(self, name):
        if name not in self.sems:
            self.nsem += 1
            self.sems[name] = self.nc.alloc_semaphore(name) if self.emit else name
        return self.sems[name]

    def sb(self, name, shape, dt):
        h = self.nc.alloc_sbuf_tensor(name, list(shape), dt) if self.emit else None
        t = Tk(name, h)
        t.small = int(np.prod(shape[1:])) <= SMALL_FREE
        self.tiles[name] = t
        return t

    def ps(self, name, shape, dt=F32):
        h = self.nc.alloc_psum_tensor(name, list(shape), dt) if self.emit else None
        t = Tk(name, h)
        self.tiles[name] = t
        return t

    def dram(self, name, shape, dt, kind="Internal"):
        h = self.nc.dram_tensor(name, list(shape), dt, kind=kind) if self.emit else None
        t = Tk(name, h)
        self.tiles[name] = t
        return t

    def _key(self, dep):
        return (dep[0], dep[1] if dep[0] == 'c' else dep[1].name)

    def _wait(self, e, dep, t=None):
        if dep is None:
            return
        key = self._key(dep)
        val = dep[2]
        if dep[0] == 'c' and dep[1] == e:
            if e in ("pe", "sp"):
                return
            if e in NOSYNC_SAME and not (t is not None and t.small):
                return
        if self.waited[e].get(key, 0) >= val:
            return
        self.waited[e][key] = val
        if dep[0] == 'c':
            if not self.emit:
                self.need[dep[1]].add(val)
            else:
                self.E[e].wait_ge(self._sem("c_" + dep[1]), self.rank[dep[1]][val])
        else:
            if self.emit:
                self.E[e].wait_ge(dep[1].dsem, 16 * val)

    def _deps(self, e, reads, writes):
        for t in reads:
            self._wait(e, t.w, t)
        for t in writes:
            if t.nowaw:
                continue
            self._wait(e, t.w, t)
            for d in t.rd.values():
                self._wait(e, d, t)

    def op(self, e, fn, reads=(), writes=()):
        self._deps(e, reads, writes)
        self.idx[e] += 1
        ix = self.idx[e]
        self.ninst += 1
        if self.emit:
            ins = fn(self.E[e])
            if ix in self.rank[e]:
                ins.then_inc(self._sem("c_" + e), 1)
        dep = ('c', e, ix)
        for t in reads:
            t.rd[('c', e)] = dep
        for t in writes:
            t.w = dep
            t.rd = {}

    def dma(self, q, dst, dst_ap, src, src_ap, **kw):
        self._deps(q, [src], [dst])
        self.idx[q] += 1
        self.ninst += 1
        if dst.dsem is None:
            dst.dsem = self._sem("d_" + dst.name)
        dst.dcnt += 1
        if self.emit:
            self.E[q].dma_start(out=dst_ap(), in_=src_ap(), **kw).then_inc(dst.dsem, 16)
        dep = ('d', dst, dst.dcnt)
        src.rd[('d', dst.name)] = dep
        dst.w = dep
        dst.rd = {}

    def finish(self, outs, e="sp"):
        for t in outs:
            self._wait(e, t.w)


def build(fn, *args, **kw):
    kb1 = KB(None, False)
    fn(kb1, *args, **kw)
    nc = bass.Bass("TRN2", target_bir_lowering=False)
    kb2 = KB(nc, True, need=kb1.need)
    fn(kb2, *args, **kw)
    return nc, kb2


EPS = 1e-6


class Ring:
    def __init__(self, kb, name, shape, dt, R, src, src_ap_of, nblocks, q="sp", tiles=None):
        self.kb = kb
        self.tiles = tiles if tiles is not None else [kb.sb(f"{name}{i}", shape, dt) for i in range(R)]
        self.R = R
        self.src = src
        self.src_ap_of = src_ap_of
        self.n = nblocks
        self.issued = 0
        self.q = q

    def prefetch(self, upto):
        while self.issued < min(upto + 1, self.n):
            i = self.issued
            t = self.tiles[i % self.R]
            self.kb.dma(self.q, t, (lambda t=t: t.h.ap()), self.src, (lambda i=i: self.src_ap_of(i)))
            self.issued += 1

    def get(self, i):
        self.prefetch(i + self.R - 1)
        return self.tiles[i % self.R]


def tok_builder(kb, NTOK, nffn=1, prologue=False, T=512, DM=2048, DF=5632):
    KC = DM // 128
    FC = DF // 128
    OC = DM // 128
    NT = NTOK // T
    hT = kb.dram("hT", [DM, NTOK], F32, kind="ExternalInput")
    out = kb.dram("out", [DM, NTOK], F32, kind="ExternalOutput")
    out.nowaw = True
    W = []
    for j in range(nffn):
        W.append(dict(
            wg=kb.dram(f"wg{j}", [FC, 128, KC, 128], BF16, kind="ExternalInput"),
            wu=kb.dram(f"wu{j}", [FC, 128, KC, 128], BF16, kind="ExternalInput"),
            wd=kb.dram(f"wd{j}", [OC, 128, FC, 128], BF16, kind="ExternalInput"),
            npre=kb.dram(f"npre{j}", [128, KC], F32, kind="ExternalInput"),
            npost=kb.dram(f"npost{j}", [128, OC], F32, kind="ExternalInput")))
    if prologue:
        catT = kb.dram("catT", [DM, NTOK], BF16, kind="ExternalInput")
        wo = kb.dram("wo", [OC, 128, KC, 128], BF16, kind="ExternalInput")
        npo = kb.dram("npo", [128, OC], F32, kind="ExternalInput")
    nstage = nffn + (1 if prologue else 0)
    mids = []
    for i in range(nstage - 1):
        m = kb.dram(f"hmid{i}", [DM, NTOK], F32)
        m.nowaw = True
        mids.append(m)

    ones = kb.sb("ones", [128, 128], BF16)
    kb.op("pool", lambda e: e.memset(ones.h.ap(), 1.0), writes=[ones])
    wpre = kb.sb("wpre", [128, KC], F32)
    wpost = kb.sb("wpost", [128, OC], F32)
    hin = kb.sb("hin", [128, KC, T], F32)
    xT = [kb.sb(f"xT{k}", [128, T], BF16) for k in range(KC)]
    sq = [kb.sb(f"sq{i}", [128, T], BF16) for i in range(2)]
    rstd = kb.sb("rstd", [128, T], F32)
    rstd2 = kb.sb("rstd2", [128, T], F32)
    act = [kb.sb(f"act{f}", [128, T], BF16) for f in range(FC)]
    sg = [kb.sb(f"sg{i}", [128, T], F32) for i in range(2)]
    yb = [kb.sb(f"y{o}", [128, T], F32) for o in range(OC)]
    res = [kb.sb(f"res{i}", [128, T], F32) for i in range(2)]
    pgu = [(kb.ps(f"pg{i}", [128, T]), kb.ps(f"pu{i}", [128, T])) for i in range(2)]
    pd = [kb.ps(f"pd{i}", [128, T]) for i in range(2)]
    pst = [kb.ps(f"pst{i}", [128, T]) for i in range(2)]
    rgt = [kb.sb(f"rg{i}", [128, KC, 128], BF16) for i in range(3)]
    rut = [kb.sb(f"ru{i}", [128, KC, 128], BF16) for i in range(3)]
    rdt = [kb.sb(f"rd{i}", [128, FC, 128], BF16) for i in range(2)]

    def sumsq(src_of, n, pbank, rs):
        for k in range(n):
            st, sap = src_of(k)
            q = sq[k % 2]
            kb.op("act", lambda e, q=q, sap=sap: e.activation(q.h.ap(), sap(), AF.Square), reads=[st], writes=[q])
            kb.op("pe", lambda e, q=q, k=k: e.matmul(pbank.h.ap(), ones.h.ap(), q.h.ap(), start=(k == 0), stop=(k == n - 1)),
                  reads=[ones, q], writes=[pbank])
        kb.op("act", lambda e: e.activation(rs.h.ap(), pbank.h.ap(), AF.Sqrt, bias=float(DM * EPS)), reads=[pbank], writes=[rs])
        kb.op("dve", lambda e: e.reciprocal(rs.h.ap(), rs.h.ap()), reads=[rs], writes=[rs])

    def down_post(t, src, dst, ring, nk, srcs, resw):
        for o in range(OC):
            w = ring.get(t * OC + o)
            p = pd[o % 2]
            for f in range(nk):
                kb.op("pe", lambda e, f=f, w=w, p=p: e.matmul(p.h.ap(), w.h[:, f, :], srcs[f].h.ap(), start=(f == 0), stop=(f == nk - 1)),
                      reads=[w, srcs[f]], writes=[p])
            kb.op("act", lambda e, o=o, p=p: e.activation(yb[o].h.ap(), p.h.ap(), AF.Copy), reads=[p], writes=[yb[o]])
        sumsq(lambda k: (yb[k], lambda k=k: yb[k].h.ap()), OC, pst[1], rstd2)
        for o in range(OC):
            r = res[o % 2]
            kb.dma("sp", r, lambda r=r: r.h.ap(), src, lambda o=o: src.h[o * 128:(o + 1) * 128, t * T:(t + 1) * T])
            kb.op("dve", lambda e, o=o: e.scalar_tensor_tensor(yb[o].h.ap(), yb[o].h.ap(), wpost.h[:, o:o + 1], rstd2.h.ap(),
                                                                ALU.mult, ALU.mult),
                  reads=[yb[o], wpost, rstd2], writes=[yb[o]])
            kb.op("pool", lambda e, o=o, r=r: e.tensor_tensor(r.h.ap(), yb[o].h.ap(), r.h.ap(), ALU.add),
                  reads=[yb[o], r], writes=[r])
            kb.dma("sp", dst, lambda o=o: dst.h[o * 128:(o + 1) * 128, t * T:(t + 1) * T], r, lambda r=r: r.h.ap())

    stage = 0
    cur = hT
    s = float(np.sqrt(DM))
    if prologue:
        dst = mids[0] if nstage > 1 else out
        kb.dma("sp", wpost, lambda: wpost.h.ap(), npo, lambda: npo.h.ap())
        kb.op("dve", lambda e: e.tensor_scalar_mul(wpost.h.ap(), wpost.h.ap(), s), reads=[wpost], writes=[wpost])
        ro = Ring(kb, "ro", None, None, 3, wo, lambda i: wo.h[i % OC], NT * OC, q="sp", tiles=rgt)
        for t in range(NT):
            for k in range(KC):
                kb.dma("sp", xT[k], lambda k=k: xT[k].h.ap(), catT, lambda k=k: catT.h[k * 128:(k + 1) * 128, t * T:(t + 1) * T])
            down_post(t, cur, dst, ro, KC, xT, None)
        cur = dst
        stage = 1

    for j in range(nffn):
        Wj = W[j]
        dst = out if stage == nstage - 1 else mids[stage]
        kb.dma("sp", wpre, lambda: wpre.h.ap(), Wj["npre"], lambda: Wj["npre"].h.ap())
        kb.dma("sp", wpost, lambda: wpost.h.ap(), Wj["npost"], lambda: Wj["npost"].h.ap())
        kb.op("dve", lambda e: e.tensor_scalar_mul(wpre.h.ap(), wpre.h.ap(), s), reads=[wpre], writes=[wpre])
        kb.op("dve", lambda e: e.tensor_scalar_mul(wpost.h.ap(), wpost.h.ap(), 0.5 * s), reads=[wpost], writes=[wpost])
        rg = Ring(kb, "rg", None, None, 3, Wj["wg"], lambda i, Wj=Wj: Wj["wg"].h[i % FC], NT * FC, q="sp", tiles=rgt)
        ru = Ring(kb, "ru", None, None, 3, Wj["wu"], lambda i, Wj=Wj: Wj["wu"].h[i % FC], NT * FC, q="sp", tiles=rut)
        rd = Ring(kb, "rd", None, None, 2, Wj["wd"], lambda i, Wj=Wj: Wj["wd"].h[i % OC], NT * OC, q="sp", tiles=rdt)

        def prenorm(t, cur=cur):
            kb.dma("sp", hin, lambda: hin.h.ap(), cur,
                   lambda: cur.h[:, t * T:(t + 1) * T].rearrange("(kc p) n -> p kc n", p=128))
            sumsq(lambda k: (hin, lambda k=k: hin.h[:, k, :]), KC, pst[0], rstd)
            for k in range(KC):
                kb.op("dve", lambda e, k=k: e.scalar_tensor_tensor(xT[k].h.ap(), hin.h[:, k, :], wpre.h[:, k:k + 1], rstd.h.ap(),
                                                                    ALU.mult, ALU.mult),
                      reads=[hin, wpre, rstd], writes=[xT[k]])

        def gateup(t):
            for f in range(FC):
                i = t * FC + f
                g = rg.get(i)
                u = ru.get(i)
                pg, pu = pgu[f % 2]
                for k in range(KC):
                    kb.op("pe", lambda e, k=k, g=g, pg=pg: e.matmul(pg.h.ap(), g.h[:, k, :], xT[k].h.ap(), start=(k == 0), stop=(k == KC - 1)),
                          reads=[g, xT[k]], writes=[pg])
                for k in range(KC):
                    kb.op("pe", lambda e, k=k, u=u, pu=pu: e.matmul(pu.h.ap(), u.h[:, k, :], xT[k].h.ap(), start=(k == 0), stop=(k == KC - 1)),
                          reads=[u, xT[k]], writes=[pu])
                s_ = sg[f % 2]
                kb.op("act", lambda e, s_=s_, pg=pg: e.activation(s_.h.ap(), pg.h.ap(), AF.Silu), reads=[pg], writes=[s_])
                kb.op("dve", lambda e, s_=s_, pu=pu, f=f: e.tensor_tensor(act[f].h.ap(), s_.h.ap(), pu.h.ap(), ALU.mult),
                      reads=[s_, pu], writes=[act[f]])

        prenorm(0)
        for t in range(NT):
            gateup(t)
            if t + 1 < NT:
                prenorm(t + 1)
            down_post(t, cur, dst, rd, FC, act, None)
        cur = dst
        stage += 1
    kb.finish([out])


def attn_builder(kb, S, NH=4, T=512, DM=2048, SB=2048):
    KC = DM // 128
    NSB = S // SB
    hT = kb.dram("hT", [DM, S], F32, kind="ExternalInput")
    npre = kb.dram("npre", [128, KC], F32, kind="ExternalInput")
    w = kb.dram("w", [NH, 128, KC, 640], BF16, kind="ExternalInput")
    cst = kb.dram("cst", [128, 4, 128], BF16, kind="ExternalInput")
    catT = kb.dram("catT", [NH * 128, S], BF16, kind="ExternalOutput")
    catT.nowaw = True

    cs = kb.sb("cs", [128, 4, 128], BF16)
    kb.dma("sp", cs, lambda: cs.h.ap(), cst, lambda: cst.h.ap())
    wpre = kb.sb("wpre", [128, KC], F32)
    kb.dma("sp", wpre, lambda: wpre.h.ap(), npre, lambda: npre.h.ap())
    kb.op("dve", lambda e: e.tensor_scalar_mul(wpre.h.ap(), wpre.h.ap(), float(np.sqrt(DM))), reads=[wpre], writes=[wpre])
    hinB = [kb.sb(f"hin{i}", [128, KC, T], F32) for i in range(2)]
    xTB = [[kb.sb(f"xT{i}_{k}", [128, T], BF16) for k in range(KC)] for i in range(2)]
    sq = [kb.sb(f"sq{i}", [128, T], BF16) for i in range(2)]
    rstdB = [kb.sb(f"rstd{i}", [128, T], F32) for i in range(2)]
    wsb = kb.sb("wsb", [128, KC, 640], BF16)
    Q = [kb.sb(f"Q{g}", [128, SB], BF16) for g in range(3)]
    Kb = kb.sb("Kb", [128, 2 * SB], BF16)
    VT = kb.sb("VT", [128, SB], BF16)
    Vs = [kb.sb(f"Vs{g}", [128, 32, 128], BF16) for g in range(3)]
    acc = kb.sb("acc", [128, 2, SB], F32)
    pt = [kb.sb(f"pt{i}", [128, 2, 128], BF16) for i in range(2)]
    ob = kb.sb("ob", [128, SB], BF16)
    pstat = kb.ps("pstat", [128, T])
    pproj = [kb.ps(f"pproj{i}", [128, T]) for i in range(2)]
    psc = [kb.ps(f"psc{i}", [128, 2, 128]) for i in range(2)]
    pol = [kb.ps(f"pol{i}", [128, 2, 128]) for i in range(2)]
    ptr = kb.ps("ptr", [128, 4, 128], BF16)
    DIL = (1, 4, 16)
    SCALE = float(128 ** -0.5)

    def prenorm(t0, par):
        hin, xT, rstd = hinB[par], xTB[par], rstdB[par]
        kb.dma("sp", hin, lambda: hin.h.ap(), hT, lambda: hT.h[:, t0:t0 + T].rearrange("(kc p) n -> p kc n", p=128))
        for k in range(KC):
            q = sq[k % 2]
            kb.op("act", lambda e, q=q, k=k: e.activation(q.h.ap(), hin.h[:, k, :], AF.Square), reads=[hin], writes=[q])
            kb.op("pe", lambda e, q=q, k=k: e.matmul(pstat.h.ap(), cs.h[:, 3, :], q.h.ap(), start=(k == 0), stop=(k == KC - 1)),
                  reads=[cs, q], writes=[pstat])
        kb.op("act", lambda e: e.activation(rstd.h.ap(), pstat.h.ap(), AF.Sqrt, bias=float(DM * EPS)), reads=[pstat], writes=[rstd])
        kb.op("dve", lambda e: e.reciprocal(rstd.h.ap(), rstd.h.ap()), reads=[rstd], writes=[rstd])
        for k in range(KC):
            kb.op("dve", lambda e, k=k: e.scalar_tensor_tensor(xT[k].h.ap(), hin.h[:, k, :], wpre.h[:, k:k + 1], rstd.h.ap(), ALU.mult, ALU.mult),
                  reads=[hin, wpre, rstd], writes=[xT[k]])

    NTB = SB // T
    blocks = [(hh, sbi, tb) for hh in range(NH) for sbi in range(NSB) for tb in range(NTB)]
    bidx = {b: i for i, b in enumerate(blocks)}
    prenorm(0, 0)
    for hh in range(NH):
        kb.dma("sp", wsb, lambda hh=hh: wsb.h.ap(), w, lambda hh=hh: w.h[hh])
        for sbi in range(NSB):
            for tb in range(NTB):
                bi = bidx[(hh, sbi, tb)]
                xT = xTB[bi % 2]
                if bi + 1 < len(blocks):
                    _, nsb, ntb = blocks[bi + 1]
                    prenorm(nsb * SB + ntb * T, (bi + 1) % 2)
                for cb in range(5):
                    pp = pproj[cb % 2]
                    for k in range(KC):
                        kb.op("pe", lambda e, k=k, cb=cb, pp=pp: e.matmul(pp.h.ap(), wsb.h[:, k, cb * 128:(cb + 1) * 128], xT[k].h.ap(),
                                                                          start=(k == 0), stop=(k == KC - 1)), reads=[wsb, xT[k]], writes=[pp])
                    if cb < 3:
                        dstt = Q[cb]
                        kb.op("act", lambda e, pp=pp, dstt=dstt, tb=tb: e.activation(dstt.h[:, tb * T:(tb + 1) * T], pp.h.ap(), AF.Copy, scale=SCALE),
                              reads=[pp], writes=[dstt])
                    elif cb == 3:
                        kb.op("act", lambda e, pp=pp, tb=tb: e.activation(Kb.h[:, SB + tb * T:SB + (tb + 1) * T], pp.h.ap(), AF.Copy),
                              reads=[pp], writes=[Kb])
                    else:
                        kb.op("dve", lambda e, pp=pp, tb=tb: e.tensor_copy(VT.h[:, tb * T:(tb + 1) * T], pp.h.ap()), reads=[pp], writes=[VT])
            for g, d in enumerate(DIL):
                for s4 in range(4):
                    for i in range(4):
                        st_ = s4 * 4 + i
                        blk, r = st_ // d, st_ % d
                        a = blk * 128 * d + r
                        kb.op("pe", lambda e, i=i, a=a, d=d: e.transpose(ptr.h[:, i, :], VT.h[:, a:a + 127 * d + 1:d], cs.h[:, 0, :]),
                              reads=[VT, cs], writes=[ptr])
                    kb.op("dve", lambda e, g=g, s4=s4: e.tensor_copy(Vs[g].h[:, 16 + s4 * 4:16 + s4 * 4 + 4, :], ptr.h.ap()),
                          reads=[ptr], writes=[Vs[g]])
            n = 0
            for g, d in enumerate(DIL):
                for st_ in range(16):
                    blk, r = st_ // d, st_ % d
                    a = blk * 128 * d + r
                    has_prev = not (sbi == 0 and blk == 0)
                    sc, ol, p_ = psc[n % 2], pol[n % 2], pt[n % 2]
                    n += 1
                    halves = (0, 1) if has_prev else (1,)
                    for hf in halves:
                        ka = SB + a - (128 * d if hf == 0 else 0)
                        kb.op("pe", lambda e, hf=hf, ka=ka, a=a, d=d, g=g, sc=sc: e.matmul(sc.h[:, hf, :], Kb.h[:, ka:ka + 127 * d + 1:d],
                                                                                          Q[g].h[:, a:a + 127 * d + 1:d], start=True, stop=True),
                              reads=[Kb, Q[g]], writes=[sc])
                    lo = halves[0]
                    kb.op("act", lambda e, sc=sc, p_=p_, lo=lo: e.activation(p_.h[:, lo:2, :], sc.h[:, lo:2, :], AF.Exp), reads=[sc], writes=[p_])
                    kb.op("dve", lambda e, p_=p_, lo=lo: e.tensor_tensor(p_.h[:, lo:2, :], p_.h[:, lo:2, :], cs.h[:, 1 + lo:3, :], ALU.mult),
                          reads=[p_, cs], writes=[p_])
                    for j, hf in enumerate(halves):
                        gs = 16 + st_ - (d if hf == 0 else 0)
                        kb.op("pe", lambda e, hf=hf, gs=gs, g=g, ol=ol, p_=p_, j=j: e.matmul(ol.h[:, 0, :], Vs[g].h[:, gs, :], p_.h[:, hf, :],
                                                                                            start=(j == 0), stop=(j == len(halves) - 1)),
                              reads=[Vs[g], p_], writes=[ol])
                    for j, hf in enumerate(halves):
                        kb.op("pe", lambda e, hf=hf, ol=ol, p_=p_, j=j: e.matmul(ol.h[:, 1, :], cs.h[:, 3, :], p_.h[:, hf, :],
                                                                                  start=(j == 0), stop=(j == len(halves) - 1)),
                              reads=[cs, p_], writes=[ol])
                    if g == 0:
                        kb.op("act", lambda e, ol=ol, a=a, d=d: e.activation(acc.h[:, :, a:a + 127 * d + 1:d], ol.h.ap(), AF.Copy), reads=[ol], writes=[acc])
                    else:
                        kb.op("dve", lambda e, ol=ol, a=a, d=d: e.tensor_tensor(acc.h[:, :, a:a + 127 * d + 1:d], acc.h[:, :, a:a + 127 * d + 1:d], ol.h.ap(), ALU.add),
                              reads=[ol, acc], writes=[acc])
            kb.op("dve", lambda e: e.reciprocal(acc.h[:, 1, :], acc.h[:, 1, :]), reads=[acc], writes=[acc])
            kb.op("dve", lambda e: e.tensor_tensor(ob.h.ap(), acc.h[:, 0, :], acc.h[:, 1, :], ALU.mult), reads=[acc], writes=[ob])
            kb.dma("sp", catT, lambda hh=hh, sbi=sbi: catT.h[hh * 128:(hh + 1) * 128, sbi * SB:(sbi + 1) * SB], ob, lambda: ob.h.ap())
            if sbi + 1 < NSB:
                kb.op("pool", lambda e: e.tensor_copy(Kb.h[:, 0:SB], Kb.h[:, SB:2 * SB]), reads=[Kb], writes=[Kb])
                for g in range(3):
                    kb.op("pool", lambda e, g=g: e.tensor_copy(Vs[g].h[:, 0:16, :], Vs[g].h[:, 16:32, :]), reads=[Vs[g]], writes=[Vs[g]])
    kb.finish([catT])


def attn_consts():
    c = np.zeros((128, 4, 128), np.float32)
    k = np.arange(128)[:, None]
    q = np.arange(128)[None, :]
    c[:, 0, :] = np.eye(128)
    c[:, 1, :] = (k >= q)
    c[:, 2, :] = (k <= q)
    c[:, 3, :] = 1.0
    return c.astype(NPBF)


def attn_wslab(att_w_in_bf, heads):
    out = []
    for h in heads:
        cols = [att_w_in_bf[:, g * 2048 + h * 128: g * 2048 + (h + 1) * 128] for g in range(3)]
        cols.append(att_w_in_bf[:, 6144 + h * 128: 6144 + (h + 1) * 128])
        cols.append(att_w_in_bf[:, 8192 + h * 128: 8192 + (h + 1) * 128])
        wcat = np.concatenate(cols, axis=1)
        out.append(wcat.reshape(16, 128, 640).transpose(1, 0, 2))
    return np.ascontiguousarray(np.stack(out))


AB_NCOL = 2308


def ab_builder(kb, S, T=256, DM=2048, C=128, do_hgrn=True, do_ssd=True):
    KC = DM // 128
    NB = S // T
    NCH = T // C
    hT = kb.dram("hT", [DM, S], F32, kind="ExternalInput")
    npre = kb.dram("npre", [128, KC], F32, kind="ExternalInput")
    w = kb.dram("w", [128, KC, AB_NCOL], BF16, kind="ExternalInput")
    cst = kb.dram("cst", [128, 4, 128], BF16, kind="ExternalInput")
    prm = kb.dram("prm", [128, 24], F32, kind="ExternalInput")
    cw = kb.dram("cw", [128, 6, 5], F32, kind="ExternalInput")
    dtbrow = kb.dram("dtbrow", [128, 4], F32, kind="ExternalInput")
    catT = kb.dram("catT", [512, S], BF16, kind="ExternalOutput")
    catT.nowaw = True

    cs = kb.sb("cs", [128, 4, 128], BF16)
    kb.dma("sp", cs, lambda: cs.h.ap(), cst, lambda: cst.h.ap())
    P = kb.sb("P", [128, 24], F32)
    kb.dma("sp", P, lambda: P.h.ap(), prm, lambda: prm.h.ap())
    CW = kb.sb("CW", [128, 6, 5], F32)
    kb.dma("sp", CW, lambda: CW.h.ap(), cw, lambda: cw.h.ap())
    DTB = kb.sb("DTB", [128, 4], F32)
    kb.dma("sp", DTB, lambda: DTB.h.ap(), dtbrow, lambda: dtbrow.h.ap())
    wpre = kb.sb("wpre", [128, KC], F32)
    kb.dma("sp", wpre, lambda: wpre.h.ap(), npre, lambda: npre.h.ap())
    kb.op("dve", lambda e: e.tensor_scalar_mul(wpre.h.ap(), wpre.h.ap(), float(np.sqrt(DM))), reads=[wpre], writes=[wpre])
    wsb = kb.sb("wsb", [128, KC, AB_NCOL], BF16)
    kb.dma("sp", wsb, lambda: wsb.h.ap(), w, lambda: w.h.ap())
    onesf = kb.sb("onesf", [128, C], F32)
    kb.op("pool", lambda e: e.memset(onesf.h.ap(), 1.0), writes=[onesf])
    LB = kb.sb("LB", [128, 4], F32)
    NEGA = kb.sb("NEGA", [128, 4], F32)
    for hd in range(2):
        kb.op("dve", lambda e, hd=hd: e.tensor_tensor(LB.h[:, 2 * hd:2 * hd + 1], P.h[:, 2 * hd:2 * hd + 1], P.h[:, 2 * hd + 1:2 * hd + 2], ALU.subtract),
              reads=[P], writes=[LB])
        kb.op("act", lambda e, hd=hd: e.activation(LB.h[:, 2 * hd:2 * hd + 1], LB.h[:, 2 * hd:2 * hd + 1], AF.Sigmoid), reads=[LB], writes=[LB])
        kb.op("dve", lambda e, hd=hd: e.tensor_scalar(LB.h[:, 2 * hd + 1:2 * hd + 2], LB.h[:, 2 * hd:2 * hd + 1], -1.0, 1.0, ALU.mult, ALU.add),
              reads=[LB], writes=[LB])
    kb.op("act", lambda e: e.activation(NEGA.h.ap(), P.h[:, 6:10], AF.Exp), reads=[P], writes=[NEGA])
    kb.op("dve", lambda e: e.tensor_scalar_mul(NEGA.h.ap(), NEGA.h.ap(), -1.0), reads=[NEGA], writes=[NEGA])

    hinB = [kb.sb(f"hin{i}", [128, KC, T], F32) for i in range(2)]
    xTB = [[kb.sb(f"xT{i}_{k}", [128, T], BF16) for k in range(KC)] for i in range(2)]
    sq = [kb.sb(f"sq{i}", [128, T], BF16) for i in range(2)]
    rstdB = [kb.sb(f"rstd{i}", [128, T], F32) for i in range(2)]
    cur = {"xT": xTB[0]}
    pstat = kb.ps("pstat", [128, T])
    pproj = [kb.ps(f"pproj{i}", [128, T]) for i in range(2)]
    pA = kb.ps("pA", [128, 128])
    pO = kb.ps("pO", [128, 128])
    pS = kb.ps("pS", [128, 128])
    pT_ = kb.ps("pT", [128, 128], BF16)
    pN = kb.ps("pN", [128, 128])

    def prenorm(t0, par):
        hin, xT, rstd = hinB[par], xTB[par], rstdB[par]
        kb.dma("sp", hin, lambda: hin.h.ap(), hT, lambda: hT.h[:, t0:t0 + T].rearrange("(kc p) n -> p kc n", p=128))
        for k in range(KC):
            q = sq[k % 2]
            kb.op("act", lambda e, q=q, k=k: e.activation(q.h.ap(), hin.h[:, k, :], AF.Square), reads=[hin], writes=[q])
            kb.op("pe", lambda e, q=q, k=k: e.matmul(pstat.h.ap(), cs.h[:, 3, :], q.h.ap(), start=(k == 0), stop=(k == KC - 1)),
                  reads=[cs, q], writes=[pstat])
        kb.op("act", lambda e: e.activation(rstd.h.ap(), pstat.h.ap(), AF.Sqrt, bias=float(DM * EPS)), reads=[pstat], writes=[rstd])
        kb.op("dve", lambda e: e.reciprocal(rstd.h.ap(), rstd.h.ap()), reads=[rstd], writes=[rstd])
        for k in range(KC):
            kb.op("dve", lambda e, k=k: e.scalar_tensor_tensor(xT[k].h.ap(), hin.h[:, k, :], wpre.h[:, k:k + 1], rstd.h.ap(), ALU.mult, ALU.mult),
                  reads=[hin, wpre, rstd], writes=[xT[k]])

    npj = [0]

    def proj_fm(col, M, evac):
        pp = pproj[npj[0] % 2]
        npj[0] += 1
        for k in range(KC):
            xk = cur["xT"][k]
            kb.op("pe", lambda e, k=k, pp=pp, xk=xk: e.matmul(pp.h[0:M, :], wsb.h[:, k, col:col + M], xk.h.ap(), start=(k == 0), stop=(k == KC - 1)),
                  reads=[wsb, xk], writes=[pp])
        evac(pp)

    def proj_tm(col, N, c, evac):
        pp = pproj[npj[0] % 2]
        npj[0] += 1
        for k in range(KC):
            xk = cur["xT"][k]
            kb.op("pe", lambda e, k=k, pp=pp, xk=xk: e.matmul(pp.h[:, 0:N], xk.h[:, c * C:(c + 1) * C], wsb.h[:, k, col:col + N], start=(k == 0), stop=(k == KC - 1)),
                  reads=[wsb, xk], writes=[pp])
        evac(pp)

    bt = kb.sb("g_b", [128, C], F32)
    nbm = kb.sb("g_nbm", [128, 1], F32)
    eq = kb.sb("g_eq", [128, C], F32)
    ek = kb.sb("g_ek", [128, C], F32)
    ei = kb.sb("g_ei", [128, C], F32)
    es = kb.sb("g_es", [128, C], F32)
    qs = kb.sb("g_qs", [128, C], BF16)
    ks = kb.sb("g_ks", [128, C], BF16)
    qi = kb.sb("g_qi", [128, C], BF16)
    kT = kb.sb("g_kT", [128, C], BF16)
    kst = kb.sb("g_kst", [128, 128], BF16)
    pTs = kb.sb("g_pT", [128, 128], BF16)

    acol = kb.sb("g_acol", [128, 1], F32)
    dl = kb.sb("g_dl", [128, C], F32)
    scS = kb.sb("g_scS", [128, C], F32)
    qbf = kb.sb("g_qbf", [128, C], BF16)
    kbf = kb.sb("g_kbf", [128, C], BF16)

    def gla(qt, qap, kt, kap, ldt, ldap, V, dv, Sf, Sb, scalar_decay=False):
        kb.op("dve", lambda e: e.tensor_tensor_scan(bt.h.ap(), onesf.h.ap(), ldap(), 0.0, ALU.mult, ALU.add), reads=[onesf, ldt], writes=[bt])
        kb.op("dve", lambda e: e.tensor_scalar_mul(nbm.h.ap(), bt.h[:, C // 2:C // 2 + 1], -1.0), reads=[bt], writes=[nbm])
        kb.op("act", lambda e: e.activation(ei.h.ap(), bt.h.ap(), AF.Exp), reads=[bt], writes=[ei])
        kb.op("act", lambda e: e.activation(es.h.ap(), bt.h.ap(), AF.Exp, bias=bt.h[:, C - 1:C], scale=-1.0), reads=[bt], writes=[es])
        kb.op("dve", lambda e: e.tensor_tensor(qi.h.ap(), qap(), ei.h.ap(), ALU.mult), reads=[qt, ei], writes=[qi])
        kb.op("pool", lambda e: e.tensor_tensor(kT.h.ap(), kap(), es.h.ap(), ALU.mult), reads=[kt, es], writes=[kT])
        kb.op("pe", lambda e: e.transpose(pT_.h.ap(), kT.h.ap(), cs.h[:, 0, :]), reads=[kT, cs], writes=[pT_])
        kb.op("act", lambda e: e.activation(kst.h.ap(), pT_.h.ap(), AF.Copy), reads=[pT_], writes=[kst])
        if not scalar_decay:
            kb.op("act", lambda e: e.activation(eq.h.ap(), bt.h.ap(), AF.Exp, bias=nbm.h.ap()), reads=[bt, nbm], writes=[eq])
            kb.op("act", lambda e: e.activation(ek.h.ap(), bt.h.ap(), AF.Exp, bias=bt.h[:, C // 2:C // 2 + 1], scale=-1.0), reads=[bt], writes=[ek])
            kb.op("dve", lambda e: e.tensor_tensor(qs.h.ap(), qap(), eq.h.ap(), ALU.mult), reads=[qt, eq], writes=[qs])
            kb.op("pool", lambda e: e.tensor_tensor(ks.h.ap(), kap(), ek.h.ap(), ALU.mult), reads=[kt, ek], writes=[ks])
            kb.op("pe", lambda e: e.matmul(pA.h.ap(), ks.h.ap(), qs.h.ap(), start=True, stop=True), reads=[ks, qs], writes=[pA])
            kb.op("dve", lambda e: e.tensor_tensor(pTs.h.ap(), pA.h.ap(), cs.h[:, 2, :], ALU.mult), reads=[pA, cs], writes=[pTs])
        else:
            kb.op("dve", lambda e: e.tensor_tensor(dl.h.ap(), bt.h.ap(), cs.h[:, 0, :], ALU.mult), reads=[bt, cs], writes=[dl])
            kb.op("dve", lambda e: e.reduce_sum(acol.h.ap(), dl.h.ap(), mybir.AxisListType.X), reads=[dl], writes=[acol])
            kb.op("dve", lambda e: e.tensor_scalar(dl.h.ap(), bt.h.ap(), acol.h.ap(), 0.0, ALU.subtract, ALU.min), reads=[bt, acol], writes=[dl])
            kb.op("act", lambda e: e.activation(dl.h.ap(), dl.h.ap(), AF.Exp), reads=[dl], writes=[dl])
            kb.op("pool", lambda e: e.tensor_tensor(dl.h.ap(), dl.h.ap(), cs.h[:, 2, :], ALU.mult), reads=[dl, cs], writes=[dl])
            kb.op("dve", lambda e: e.tensor_tensor(pTs.h.ap(), scS.h.ap(), dl.h.ap(), ALU.mult), reads=[scS, dl], writes=[pTs])
        kb.op("pe", lambda e: e.matmul(pO.h[0:dv, :], V.h[:, 0:dv], pTs.h.ap(), start=True, stop=False), reads=[V, pTs], writes=[pO])
        kb.op("pe", lambda e: e.matmul(pO.h[0:dv, :], Sb.h[:, 0:dv], qi.h.ap(), start=False, stop=True), reads=[Sb, qi], writes=[pO])
        kb.op("pe", lambda e: e.matmul(pS.h[:, 0:dv], kst.h.ap(), V.h[:, 0:dv], start=True, stop=True), reads=[kst, V], writes=[pS])
        kb.op("dve", lambda e: e.scalar_tensor_tensor(Sf.h[:, 0:dv], Sf.h[:, 0:dv], ei.h[:, C - 1:C], pS.h[:, 0:dv], ALU.mult, ALU.add),
              reads=[Sf, ei, pS], writes=[Sf])
        kb.op("act", lambda e: e.activation(Sb.h[:, 0:dv], Sf.h[:, 0:dv], AF.Copy), reads=[Sf], writes=[Sb])

    if do_hgrn:
        hq = kb.sb("hq", [128, T], F32)
        hk = kb.sb("hk", [128, T], F32)
        hl = kb.sb("hl", [128, T], F32)
        hg = kb.sb("hg", [128, T], F32)
        hV = kb.sb("hV", [128, 128], BF16)
        hsq = kb.sb("hsq", [128, C], BF16)
        hrs = kb.sb("hrs", [128, C], F32)
        ho = kb.sb("ho", [128, C], F32)
        hob = [kb.sb(f"hob{i}", [128, T], BF16) for i in range(2)]
        HSf = [kb.sb(f"HSf{i}", [128, 128], F32) for i in range(2)]
        HSb = [kb.sb(f"HSb{i}", [128, 128], BF16) for i in range(2)]
        for i in range(2):
            kb.op("pool", lambda e, i=i: e.memset(HSf[i].h.ap(), 0.0), writes=[HSf[i]])
            kb.op("pool", lambda e, i=i: e.memset(HSb[i].h.ap(), 0.0), writes=[HSb[i]])

    if do_ssd:
        raw = [kb.sb(f"raw{i}", [128, 3 + T], F32) for i in range(6)]
        cv = [kb.sb(f"cv{i}", [128, T], F32) for i in range(6)]
        zs = [kb.sb(f"zs{i}", [64, T], F32) for i in range(4)]
        ld = [kb.sb(f"ld{i}", [128, T], F32) for i in range(4)]
        dtt = kb.sb("dtt", [128, 4], F32)
        sV = kb.sb("sV", [128, 64], BF16)
        syz = [kb.sb(f"syz{i}", [64, C], F32) for i in range(4)]
        ssq = kb.sb("ssq", [64, C], BF16)
        srs = kb.sb("srs", [64, C], F32)
        sob = [kb.sb(f"sob{i}", [64, C], BF16) for i in range(4)]
        SSf = [kb.sb(f"SSf{i}", [128, 64], F32) for i in range(4)]
        SSb = [kb.sb(f"SSb{i}", [128, 64], BF16) for i in range(4)]
        for i in range(4):
            kb.op("pool", lambda e, i=i: e.memset(SSf[i].h.ap(), 0.0), writes=[SSf[i]])
            kb.op("pool", lambda e, i=i: e.memset(SSb[i].h.ap(), 0.0), writes=[SSb[i]])
        for i in range(6):
            kb.op("pool", lambda e, i=i: e.memset(raw[i].h[:, 0:3], 0.0), writes=[raw[i]])

    prenorm(0, 0)
    for blk in range(NB):
        t0 = blk * T
        cur["xT"] = xTB[blk % 2]
        if do_hgrn:
            for hd in range(2):
                base = hd * 512
                proj_fm(base, 128, lambda pp: kb.op("act", lambda e: e.activation(hq.h.ap(), pp.h.ap(), AF.Copy), reads=[pp], writes=[hq]))
                proj_fm(base + 128, 128, lambda pp: kb.op("act", lambda e: e.activation(hk.h.ap(), pp.h.ap(), AF.Sigmoid), reads=[pp], writes=[hk]))
                kb.op("dve", lambda e, hd=hd: e.tensor_scalar(hk.h.ap(), hk.h.ap(), LB.h[:, 2 * hd + 1:2 * hd + 2], LB.h[:, 2 * hd:2 * hd + 1], ALU.mult, ALU.add),
                      reads=[hk, LB], writes=[hk])
                kb.op("act", lambda e: e.activation(hl.h.ap(), hk.h.ap(), AF.Ln), reads=[hk], writes=[hl])
                kb.op("dve", lambda e: e.tensor_scalar(hk.h.ap(), hk.h.ap(), -1.0, 1.0, ALU.mult, ALU.add), reads=[hk], writes=[hk])
                proj_fm(base + 256, 128, lambda pp: kb.op("act", lambda e: e.activation(hg.h.ap(), pp.h.ap(), AF.Silu), reads=[pp], writes=[hg]))
                ob_ = hob[hd]
                for c in range(NCH):
                    cl = slice(c * C, (c + 1) * C)
                    proj_tm(base + 384, 128, c, lambda pp: kb.op("dve", lambda e: e.tensor_copy(hV.h.ap(), pp.h[:, 0:128]), reads=[pp], writes=[hV]))
                    gla(hq, lambda cl=cl: hq.h[:, cl], hk, lambda cl=cl: hk.h[:, cl], hl, lambda cl=cl: hl.h[:, cl], hV, 128, HSf[hd], HSb[hd])
                    kb.op("act", lambda e: e.activation(hsq.h.ap(), pO.h.ap(), AF.Square), reads=[pO], writes=[hsq])
                    kb.op("pe", lambda e: e.matmul(pN.h.ap(), cs.h[:, 3, :], hsq.h.ap(), start=True, stop=True), reads=[cs, hsq], writes=[pN])
                    kb.op("act", lambda e: e.activation(hrs.h.ap(), pN.h.ap(), AF.Sqrt, bias=float(EPS), scale=1.0 / 128), reads=[pN], writes=[hrs])
                    kb.op("dve", lambda e: e.reciprocal(hrs.h.ap(), hrs.h.ap()), reads=[hrs], writes=[hrs])
                    kb.op("dve", lambda e: e.tensor_tensor(ho.h.ap(), pO.h.ap(), hrs.h.ap(), ALU.mult), reads=[pO, hrs], writes=[ho])
                    kb.op("dve", lambda e, hd=hd, cl=cl, ob_=ob_: e.scalar_tensor_tensor(ob_.h[:, cl], ho.h.ap(), P.h[:, 4 + hd:5 + hd], hg.h[:, cl], ALU.mult, ALU.mult),
                          reads=[ho, P, hg], writes=[ob_])
                kb.dma("sp", catT, lambda hd=hd, t0=t0: catT.h[hd * 128:(hd + 1) * 128, t0:t0 + T], ob_, lambda ob_=ob_: ob_.h.ap())
        if do_ssd:
            for r in range(4):
                proj_fm(1024 + r * 64, 64, lambda pp, r=r: kb.op("act", lambda e: e.activation(raw[r].h[0:64, 3:3 + T], pp.h[0:64, :], AF.Copy), reads=[pp], writes=[raw[r]]))
                proj_fm(1280 + r * 64, 64, lambda pp, r=r: kb.op("act", lambda e: e.activation(zs[r].h.ap(), pp.h[0:64, :], AF.Silu), reads=[pp], writes=[zs[r]]))
            proj_fm(1536, 128, lambda pp: kb.op("act", lambda e: e.activation(raw[4].h[:, 3:3 + T], pp.h.ap(), AF.Copy), reads=[pp], writes=[raw[4]]))
            proj_fm(1664, 128, lambda pp: kb.op("act", lambda e: e.activation(raw[5].h[:, 3:3 + T], pp.h.ap(), AF.Copy), reads=[pp], writes=[raw[5]]))
            if blk + 1 < NB:
                prenorm(t0 + T, (blk + 1) % 2)
            for i in range(6):
                np_ = 64 if i < 4 else 128
                kb.op("dve", lambda e, i=i, np_=np_: e.tensor_scalar(cv[i].h[0:np_, :], raw[i].h[0:np_, 0:T], CW.h[0:np_, i, 0:1], CW.h[0:np_, i, 4:5], ALU.mult, ALU.add),
                      reads=[raw[i], CW], writes=[cv[i]])
                for j in range(1, 4):
                    kb.op("dve", lambda e, i=i, j=j, np_=np_: e.scalar_tensor_tensor(cv[i].h[0:np_, :], raw[i].h[0:np_, j:j + T], CW.h[0:np_, i, j:j + 1], cv[i].h[0:np_, :], ALU.mult, ALU.add),
                          reads=[raw[i], CW, cv[i]], writes=[cv[i]])
                kb.op("act", lambda e, i=i, np_=np_: e.activation(cv[i].h[0:np_, :], cv[i].h[0:np_, :], AF.Silu), reads=[cv[i]], writes=[cv[i]])
                kb.op("pool", lambda e, i=i, np_=np_: e.tensor_copy(raw[i].h[0:np_, 0:3], raw[i].h[0:np_, T:T + 3]), reads=[raw[i]], writes=[raw[i]])
            for r in range(4):
                proj_fm(1792 + r * 128, 128, lambda pp, r=r: kb.op("act", lambda e: e.activation(ld[r].h.ap(), pp.h.ap(), AF.Exp, bias=P.h[:, 10 + r:11 + r]), reads=[pp, P], writes=[ld[r]]))
                kb.op("act", lambda e, r=r: e.activation(ld[r].h.ap(), ld[r].h.ap(), AF.Ln, bias=1.0), reads=[ld[r]], writes=[ld[r]])
                kb.op("dve", lambda e, r=r: e.tensor_scalar_mul(ld[r].h.ap(), ld[r].h.ap(), NEGA.h[:, r:r + 1]), reads=[ld[r], NEGA], writes=[ld[r]])
            for c in range(NCH):
                cl = slice(c * C, (c + 1) * C)
                proj_tm(2304, 4, c, lambda pp: kb.op("dve", lambda e: e.tensor_tensor(dtt.h.ap(), pp.h[:, 0:4], DTB.h.ap(), ALU.add), reads=[pp, DTB], writes=[dtt]))
                kb.op("act", lambda e: e.activation(dtt.h.ap(), dtt.h.ap(), AF.Exp), reads=[dtt], writes=[dtt])
                kb.op("act", lambda e: e.activation(dtt.h.ap(), dtt.h.ap(), AF.Ln, bias=1.0), reads=[dtt], writes=[dtt])
                kb.op("dve", lambda e, cl=cl: e.tensor_copy(kbf.h.ap(), cv[4].h[:, cl]), reads=[cv[4]], writes=[kbf])
                kb.op("pool", lambda e, cl=cl: e.tensor_copy(qbf.h.ap(), cv[5].h[:, cl]), reads=[cv[5]], writes=[qbf])
                kb.op("pe", lambda e: e.matmul(pA.h.ap(), kbf.h.ap(), qbf.h.ap(), start=True, stop=True), reads=[kbf, qbf], writes=[pA])
                kb.op("act", lambda e: e.activation(scS.h.ap(), pA.h.ap(), AF.Copy), reads=[pA], writes=[scS])
                for r in range(4):
                    kb.op("dve", lambda e, r=r, cl=cl: e.tensor_copy(kT.h[0:64, :], cv[r].h[0:64, cl]), reads=[cv[r]], writes=[kT])
                    kb.op("pe", lambda e: e.transpose(pT_.h[:, 0:64], kT.h[0:64, :], cs.h[0:64, 0, 0:64]), reads=[kT, cs], writes=[pT_])
                    kb.op("dve", lambda e, r=r: e.tensor_scalar_mul(sV.h.ap(), pT_.h[:, 0:64], dtt.h[:, r:r + 1]), reads=[pT_, dtt], writes=[sV])
                    gla(cv[5], lambda cl=cl: cv[5].h[:, cl], cv[4], lambda cl=cl: cv[4].h[:, cl], ld[r], lambda r=r, cl=cl: ld[r].h[:, cl], sV, 64, SSf[r], SSb[r], scalar_decay=True)
                    kb.op("dve", lambda e, r=r, cl=cl: e.scalar_tensor_tensor(syz[r].h.ap(), cv[r].h[0:64, cl], P.h[0:64, 14 + r:15 + r], pO.h[0:64, :], ALU.mult, ALU.add),
                          reads=[cv[r], P, pO], writes=[syz[r]])
                    kb.op("dve", lambda e, r=r, cl=cl: e.tensor_tensor(syz[r].h.ap(), syz[r].h.ap(), zs[r].h[:, cl], ALU.mult), reads=[syz[r], zs[r]], writes=[syz[r]])
                    kb.op("act", lambda e, r=r: e.activation(ssq.h.ap(), syz[r].h.ap(), AF.Square), reads=[syz[r]], writes=[ssq])
                    kb.op("pe", lambda e, r=r: e.matmul(pN.h[0:64, :], cs.h[0:64, 3, 0:64], ssq.h.ap(), start=(r == 0), stop=(r == 3)), reads=[cs, ssq], writes=[pN])
                kb.op("act", lambda e: e.activation(srs.h.ap(), pN.h[0:64, :], AF.Sqrt, bias=float(EPS), scale=1.0 / 256), reads=[pN], writes=[srs])
                kb.op("dve", lambda e: e.reciprocal(srs.h.ap(), srs.h.ap()), reads=[srs], writes=[srs])
                for r in range(4):
                    kb.op("dve", lambda e, r=r: e.scalar_tensor_tensor(sob[r].h.ap(), syz[r].h.ap(), P.h[0:64, 18 + r:19 + r], srs.h.ap(), ALU.mult, ALU.mult),
                          reads=[syz[r], P, srs], writes=[sob[r]])
                    kb.dma("sp", catT, lambda r=r, c=c, t0=t0: catT.h[256 + r * 64:256 + (r + 1) * 64, t0 + c * C:t0 + (c + 1) * C], sob[r], lambda r=r: sob[r].h.ap())
    kb.finish([catT])


def ab_inputs(inp_bf_w_in, z, j):
    W = inp_bf_w_in
    cols = []
    for hd in (2 * j, 2 * j + 1):
        for off in (0, 1024, 3072, 2048):
            cols.append(W[:, off + hd * 128: off + (hd + 1) * 128])
    for r in range(4):
        cols.append(W[:, 5120 + j * 256 + r * 64: 5120 + j * 256 + (r + 1) * 64])
    for r in range(4):
        cols.append(W[:, 4096 + j * 256 + r * 64: 4096 + j * 256 + (r + 1) * 64])
    cols.append(W[:, 6144 + j * 128: 6144 + (j + 1) * 128])
    cols.append(W[:, 6656 + j * 128: 6656 + (j + 1) * 128])
    for r in range(4):
        cols.append(np.repeat(W[:, 7168 + 4 * j + r: 7168 + 4 * j + r + 1], 128, axis=1))
    cols.append(W[:, 7168 + 4 * j: 7168 + 4 * j + 4])
    wc = np.concatenate(cols, axis=1)
    assert wc.shape[1] == AB_NCOL
    wt = np.ascontiguousarray(wc.reshape(16, 128, AB_NCOL).transpose(1, 0, 2))
    prm = np.zeros((128, 24), np.float32)
    for i, hd in enumerate((2 * j, 2 * j + 1)):
        prm[:, 2 * i] = z["hgrn_lb"][0, hd * 128:(hd + 1) * 128]
        prm[:, 2 * i + 1] = z["hgrn_lb"][1, hd * 128:(hd + 1) * 128]
        prm[:, 4 + i] = z["hgrn_norm_w"][0, hd * 128:(hd + 1) * 128]
    for r in range(4):
        hh = 4 * j + r
        prm[:, 6 + r] = z["ssm_A_log"][0, hh]
        prm[:, 10 + r] = z["ssm_dt_bias"][0, hh]
        prm[:, 14 + r] = z["ssm_D"][0, hh]
        prm[0:64, 18 + r] = z["ssm_norm_w"][0, j * 256 + r * 64: j * 256 + (r + 1) * 64]
    cw = np.zeros((128, 6, 5), np.float32)
    cwt, cb = z["ssm_conv_w"][0], z["ssm_conv_b"][0]
    for r in range(4):
        ch = slice(j * 256 + r * 64, j * 256 + (r + 1) * 64)
        cw[0:64, r, 0:4] = cwt[:, ch].T
        cw[0:64, r, 4] = cb[ch]
    for i, off in ((4, 1024), (5, 1536)):
        ch = slice(off + j * 128, off + (j + 1) * 128)
        cw[:, i, 0:4] = cwt[:, ch].T
        cw[:, i, 4] = cb[ch]
    dtb = np.tile(z["ssm_dt_bias"][0, 4 * j:4 * j + 4][None, :], (128, 1)).astype(np.float32)
    return {"w": wt, "prm": prm, "cw": cw, "dtbrow": dtb}


def cast_builder(kb, M, CH=4096):
    x = kb.dram("x", [128, M], F32, kind="ExternalInput")
    y = kb.dram("y", [128, M], BF16, kind="ExternalOutput")
    y.nowaw = True
    tin = [kb.sb(f"ci{i}", [128, CH], F32) for i in range(3)]
    tout = [kb.sb(f"co{i}", [128, CH], BF16) for i in range(3)]
    n = M // CH
    for i in range(n):
        a, b = tin[i % 3], tout[i % 3]
        kb.dma("sp", a, lambda a=a: a.h.ap(), x, lambda i=i: x.h[:, i * CH:(i + 1) * CH])
        if i % 2 == 0:
            kb.op("act", lambda e, a=a, b=b: e.activation(b.h.ap(), a.h.ap(), AF.Copy), reads=[a], writes=[b])
        else:
            kb.op("dve", lambda e, a=a, b=b: e.tensor_copy(b.h.ap(), a.h.ap()), reads=[a], writes=[b])
        kb.dma("sp", y, lambda i=i: y.h[:, i * CH:(i + 1) * CH], b, lambda b=b: b.h.ap())
    kb.finish([y])


def _tile_w(w):
    K, N = w.shape
    return np.ascontiguousarray(w.reshape(K // 128, 128, N // 128, 128).transpose(2, 1, 0, 3))


def _pp(v):
    return np.ascontiguousarray(np.asarray(v, np.float32).reshape(16, 128).T)


NCORES = 8
_CORES = list(range(NCORES))


def _run(nc, maps):
    return run_bass_kernel_spmd(nc, maps, core_ids=_CORES).results


def kernel(x, norm_pre, norm_post, ffn_w_gate, ffn_w_up, ffn_w_down, ab_w_in, ab_w_out,
           hgrn_lb, hgrn_norm_w, ssm_conv_w, ssm_conv_b, ssm_dt_bias, ssm_A_log, ssm_D,
           ssm_norm_w, att_w_in, att_w_out):
    f32 = lambda a: np.asarray(a, np.float32)
    x = f32(x)
    norm_pre, norm_post = f32(norm_pre), f32(norm_post)
    B, S, D = x.shape
    NTOK = B * S // NCORES
    QC = NCORES // B
    srcs = []
    for l in range(2):
        for j in range(2):
            for w in (ffn_w_gate[l, j], ffn_w_up[l, j], ffn_w_down[l, j]):
                srcs.append(_tile_w(f32(w)))
    srcs += [f32(ab_w_in[0]), _tile_w(f32(ab_w_out[0])), f32(att_w_in[0]), _tile_w(f32(att_w_out[0]))]
    shapes = [t.shape for t in srcs]
    flat = np.concatenate([t.reshape(-1) for t in srcs])
    CH = 4096
    per = NCORES * 128 * CH
    tot = -(-flat.size // per) * per
    flat = np.concatenate([flat, np.zeros(tot - flat.size, np.float32)])
    M = tot // (NCORES * 128)
    nc0, _ = build(cast_builder, M, CH)
    parts = flat.reshape(NCORES, 128, M)
    r0 = _run(nc0, [{"x": parts[c]} for c in range(NCORES)])
    fb = np.concatenate([np.asarray(r0[c]["y"]).reshape(-1) for c in range(NCORES)])
    del flat, parts, srcs
    wts, off = [], 0
    for shp in shapes:
        n = int(np.prod(shp))
        wts.append(fb[off:off + n].reshape(shp))
        off += n
    ffw = wts[:12]
    ab_in_bf, ab_out_t, att_in_bf, att_out_t = wts[12:16]
    small = {"hgrn_lb": f32(hgrn_lb), "hgrn_norm_w": f32(hgrn_norm_w), "ssm_conv_w": f32(ssm_conv_w),
             "ssm_conv_b": f32(ssm_conv_b), "ssm_dt_bias": f32(ssm_dt_bias), "ssm_A_log": f32(ssm_A_log),
             "ssm_D": f32(ssm_D), "ssm_norm_w": f32(ssm_norm_w)}

    def ffn_map(k, l, slot, idx):
        return {f"wg{idx}": ffw[3 * k], f"wu{idx}": ffw[3 * k + 1], f"wd{idx}": ffw[3 * k + 2],
                f"npre{idx}": _pp(norm_pre[l, slot]), f"npost{idx}": _pp(norm_post[l, slot])}

    def full_seq(hs):
        return [np.ascontiguousarray(np.concatenate(hs[b * QC:(b + 1) * QC], axis=1)) for b in range(B)]

    def tok_shards(cat_b):
        return [np.ascontiguousarray(cat_b[c // QC][:, (c % QC) * NTOK:((c % QC) + 1) * NTOK]) for c in range(NCORES)]

    h = x.reshape(B * S, D)
    hT = [np.ascontiguousarray(h[c * NTOK:(c + 1) * NTOK].T) for c in range(NCORES)]
    consts = attn_consts()
    nc1, _ = build(tok_builder, NTOK, nffn=1, prologue=False)
    com = ffn_map(0, 0, 0, 0)
    r = _run(nc1, [dict(com, hT=hT[c]) for c in range(NCORES)])
    hT = [np.asarray(r[c]["out"]) for c in range(NCORES)]
    nc2, _ = build(ab_builder, S)
    hb = full_seq(hT)
    maps = []
    for c in range(NCORES):
        b, j = c // QC, c % QC
        m = ab_inputs(ab_in_bf, small, j)
        m.update({"hT": hb[b], "npre": _pp(norm_pre[0, 1]), "cst": consts})
        maps.append(m)
    r = _run(nc2, maps)
    del hb, maps
    cat_b = []
    for b in range(B):
        cat = np.empty((D, S), NPBF)
        for j in range(QC):
            o = np.asarray(r[b * QC + j]["catT"])
            cat[j * 256:(j + 1) * 256] = o[0:256]
            cat[1024 + j * 256:1024 + (j + 1) * 256] = o[256:512]
        cat_b.append(cat)
    nc3, _ = build(tok_builder, NTOK, nffn=2, prologue=True)
    com = {"wo": ab_out_t, "npo": _pp(norm_post[0, 1])}
    com.update(ffn_map(1, 0, 2, 0))
    com.update(ffn_map(2, 1, 0, 1))
    cs_ = tok_shards(cat_b)
    r = _run(nc3, [dict(com, hT=hT[c], catT=cs_[c]) for c in range(NCORES)])
    hT = [np.asarray(r[c]["out"]) for c in range(NCORES)]
    nc4, _ = build(attn_builder, S, NH=4)
    hb = full_seq(hT)
    maps = []
    for c in range(NCORES):
        b, j = c // QC, c % QC
        maps.append({"hT": hb[b], "npre": _pp(norm_pre[1, 1]), "cst": consts,
                     "w": attn_wslab(att_in_bf, list(range(4 * j, 4 * j + 4)))})
    r = _run(nc4, maps)
    del hb, maps
    cat_b = [np.ascontiguousarray(np.concatenate([np.asarray(r[b * QC + j]["catT"]) for j in range(QC)], axis=0)) for b in range(B)]
    nc5, _ = build(tok_builder, NTOK, nffn=1, prologue=True)
    com = {"wo": att_out_t, "npo": _pp(norm_post[1, 1])}
    com.update(ffn_map(3, 1, 2, 0))
    cs_ = tok_shards(cat_b)
    r = _run(nc5, [dict(com, hT=hT[c], catT=cs_[c]) for c in range(NCORES)])
    out = np.concatenate([np.asarray(r[c]["out"]).T for c in range(NCORES)], axis=0).reshape(B, S, D)
    return np.ascontiguousarray(out.astype(np.float32))
```
